# Optimizing a Trainium2 kernel written in Bass

```python
import math
import jax, jax.numpy as jnp
from jax import lax
import numpy as np

D_MODEL = 2048
BATCH = 2
SEQ = 4096
DEPTH = 4

D_MIX = D_MODEL
M_WIDTH = D_MIX // 2
M_HEADS = 4
M_HEAD_V = M_WIDTH // M_HEADS
M_HEAD_QK = M_HEAD_V // 2
M_CONV = 4
M_CHUNK = 64
S5_CH = D_MIX - M_WIDTH
S5_GROUP_CH = 16
S5_GROUPS = S5_CH // S5_GROUP_CH
S5_STATE = 64
D_FF = 5504
F_CONV = 3
EPS = 1e-6

C_XM = M_WIDTH
C_V = M_WIDTH
C_O = M_WIDTH
C_I = M_HEADS
C_F = M_HEADS
C_U = S5_CH
IN_COLS = C_XM + C_V + C_O + C_I + C_F + C_U
SPLITS = (C_XM, C_XM + C_V, C_XM + C_V + C_O, C_XM + C_V + C_O + C_I, C_XM + C_V + C_O + C_I + C_F)

kernel_name = "hybrid_mlstm_s5_convffn"

f32 = jnp.float32


def rmsnorm(x, g):
    xf = x.astype(f32)
    y = xf * lax.rsqrt(jnp.mean(xf * xf, axis=-1, keepdims=True) + EPS) * g.astype(f32)
    return y.astype(x.dtype)


def causal_dwconv(x, w, b):
    K, C = w.shape
    y = lax.conv_general_dilated(x, w[:, None, :].astype(x.dtype), window_strides=(1,), padding=[(K - 1, 0)], dimension_numbers=('NWC', 'WIO', 'NWC'), feature_group_count=C)
    return y + b.astype(x.dtype)


def mlstm_chunkwise(q, k, v, i_pre, f_pre):
    Bsz, H, T, DK = q.shape
    DV = v.shape[-1]
    L = M_CHUNK
    NC = T // L
    q = q * (DK ** -0.5)
    lf = jax.nn.log_sigmoid(f_pre)

    def to_chunks(a):
        return jnp.moveaxis(a.reshape(Bsz, H, NC, L, *a.shape[3:]), 2, 0)

    qc, kc, vc, ic, fc = (to_chunks(a) for a in (q, k, v, i_pre, lf))
    causal = jnp.tril(jnp.ones((L, L), dtype=bool))

    def step(carry, inp):
        C, n, m = carry
        qb, kb, vb, ib, fb = inp
        b = jnp.cumsum(fb, axis=-1)
        g = b[..., -1]
        Dm = jnp.where(causal, b[..., :, None] - b[..., None, :] + ib[..., None, :], -jnp.inf)
        inter = b + m[..., None]
        m_row = jnp.maximum(inter, jnp.max(Dm, axis=-1))
        w_intra = jnp.exp(Dm - m_row[..., None])
        w_inter = jnp.exp(inter - m_row)
        s = jnp.einsum('bhid,bhjd->bhij', qb, kb) * w_intra
        num = jnp.einsum('bhij,bhjv->bhiv', s, vb) + w_inter[..., None] * jnp.einsum('bhid,bhdv->bhiv', qb, C)
        den = jnp.sum(s, axis=-1) + w_inter * jnp.einsum('bhid,bhd->bhi', qb, n)
        h = num / jnp.maximum(jnp.abs(den), jnp.exp(-m_row))[..., None]
        dec = g[..., None] - b + ib
        m_new = jnp.maximum(g + m, jnp.max(dec, axis=-1))
        w_k = jnp.exp(dec - m_new[..., None])
        w_c = jnp.exp(g + m - m_new)
        C = w_c[..., None, None] * C + jnp.einsum('bhj,bhjd,bhjv->bhdv', w_k, kb, vb)
        n = w_c[..., None] * n + jnp.einsum('bhj,bhjd->bhd', w_k, kb)
        return (C, n, m_new), h

    init = (jnp.zeros((Bsz, H, DK, DV), f32), jnp.zeros((Bsz, H, DK), f32), jnp.zeros((Bsz, H), f32))
    _, hc = lax.scan(step, init, (qc, kc, vc, ic, fc))
    return jnp.moveaxis(hc, 0, 2).reshape(Bsz, H, T, DV)


def mlstm_mixer(xm, v_raw, o_raw, i_raw, f_raw, conv_w, conv_b, w_q, w_k, i_bias, f_bias, head_g, skip):
    Bsz, T, _ = xm.shape
    c = jax.nn.silu(causal_dwconv(xm, conv_w, conv_b)).astype(f32)
    ch = c.reshape(Bsz, T, M_HEADS, M_HEAD_V)
    q = jnp.einsum('bthd,hde->bhte', ch, w_q.astype(f32))
    k = jnp.einsum('bthd,hde->bhte', ch, w_k.astype(f32))
    v = v_raw.astype(f32).reshape(Bsz, T, M_HEADS, M_HEAD_V).transpose(0, 2, 1, 3)
    i_pre = (i_raw.astype(f32) + i_bias.astype(f32)).transpose(0, 2, 1)
    f_pre = (f_raw.astype(f32) + f_bias.astype(f32)).transpose(0, 2, 1)
    h = mlstm_chunkwise(q, k, v, i_pre, f_pre).transpose(0, 2, 1, 3)
    hn = h * lax.rsqrt(jnp.mean(h * h, axis=-1, keepdims=True) + EPS)
    hn = (hn * head_g.astype(f32).reshape(M_HEADS, M_HEAD_V)).reshape(Bsz, T, M_WIDTH)
    out = jax.nn.sigmoid(o_raw.astype(f32)) * (hn + skip.astype(f32) * c)
    return out.astype(xm.dtype)


def s5_mixer(u, a_re, a_im, log_dt, b_re, b_im, c_re, c_im, d, w_glu, b_glu):
    Bsz, T, _ = u.shape
    uf = u.astype(f32).reshape(Bsz, T, S5_GROUPS, S5_GROUP_CH)
    A = lax.complex(a_re.astype(f32), a_im.astype(f32))
    dt = jnp.exp(log_dt.astype(f32))[:, None]
    A_bar = jnp.exp(A * dt)
    Bm = lax.complex(b_re.astype(f32), b_im.astype(f32))
    B_bar = ((A_bar - 1.0) / A)[..., None] * Bm
    Cm = lax.complex(c_re.astype(f32), c_im.astype(f32))
    Bu = jnp.einsum('gpc,btgc->btgp', B_bar, uf.astype(B_bar.dtype))
    a = jnp.broadcast_to(A_bar, Bu.shape)

    def combine(e1, e2):
        a1, b1 = e1
        a2, b2 = e2
        return a2 * a1, a2 * b1 + b2

    _, states = lax.associative_scan(combine, (a, Bu), axis=1)
    y = jnp.real(jnp.einsum('gcp,btgp->btgc', Cm, states)) + d.astype(f32).reshape(S5_GROUPS, S5_GROUP_CH) * uf
    y = jax.nn.gelu(y.reshape(Bsz, T, S5_CH))
    out = y * jax.nn.sigmoid(y @ w_glu.astype(f32) + b_glu.astype(f32))
    return out.astype(u.dtype)


def setup_inputs(seed: int = 0) -> dict:
    key = jax.random.key(seed)
    ks = jax.random.split(key, 32)
    nrm = jax.random.normal
    L = DEPTH
    x = nrm(ks[0], (BATCH, SEQ, D_MODEL), f32)
    norm_mix_g = 1.0 + 0.02 * nrm(ks[1], (L, D_MODEL), f32)
    w_in = nrm(ks[2], (L, D_MODEL, IN_COLS), f32) * D_MODEL ** -0.5
    m_conv_w = nrm(ks[3], (L, M_CONV, M_WIDTH), f32) * M_CONV ** -0.5
    m_conv_b = 0.02 * nrm(ks[4], (L, M_WIDTH), f32)
    w_q = nrm(ks[5], (L, M_HEADS, M_HEAD_V, M_HEAD_QK), f32) * M_HEAD_V ** -0.5
    w_k = nrm(ks[6], (L, M_HEADS, M_HEAD_V, M_HEAD_QK), f32) * M_HEAD_V ** -0.5
    m_i_bias = 0.1 * nrm(ks[7], (L, M_HEADS), f32)
    m_f_bias = jnp.linspace(3.0, 6.0, M_HEADS, dtype=f32)[None, :] + 0.1 * nrm(ks[8], (L, M_HEADS), f32)
    m_head_g = 1.0 + 0.02 * nrm(ks[9], (L, M_WIDTH), f32)
    m_skip = 1.0 + 0.02 * nrm(ks[10], (L, M_WIDTH), f32)
    s5_a_re = -0.5 + 0.01 * nrm(ks[11], (L, S5_GROUPS, S5_STATE), f32)
    s5_a_im = math.pi * jnp.arange(S5_STATE, dtype=f32)[None, None, :] + 0.01 * nrm(ks[12], (L, S5_GROUPS, S5_STATE), f32)
    s5_log_dt = jax.random.uniform(ks[13], (L, S5_GROUPS), f32, math.log(1e-3), math.log(1e-1))
    s5_b_re = nrm(ks[14], (L, S5_GROUPS, S5_STATE, S5_GROUP_CH), f32) * (2 * S5_GROUP_CH) ** -0.5
    s5_b_im = nrm(ks[15], (L, S5_GROUPS, S5_STATE, S5_GROUP_CH), f32) * (2 * S5_GROUP_CH) ** -0.5
    s5_c_re = nrm(ks[16], (L, S5_GROUPS, S5_GROUP_CH, S5_STATE), f32) * (2 * S5_STATE) ** -0.5
    s5_c_im = nrm(ks[17], (L, S5_GROUPS, S5_GROUP_CH, S5_STATE), f32) * (2 * S5_STATE) ** -0.5
    s5_d = nrm(ks[18], (L, S5_CH), f32)
    s5_w_glu = nrm(ks[19], (L, S5_CH, S5_CH), f32) * S5_CH ** -0.5
    s5_b_glu = 0.02 * nrm(ks[20], (L, S5_CH), f32)
    w_out = nrm(ks[21], (L, D_MIX, D_MODEL), f32) * D_MIX ** -0.5
    norm_ffn_g = 1.0 + 0.02 * nrm(ks[22], (L, D_MODEL), f32)
    w_gate = nrm(ks[23], (L, D_MODEL, D_FF), f32) * D_MODEL ** -0.5
    w_val = nrm(ks[24], (L, D_MODEL, D_FF), f32) * D_MODEL ** -0.5
    f_conv_w = nrm(ks[25], (L, F_CONV, D_FF), f32) * F_CONV ** -0.5
    f_conv_b = 0.02 * nrm(ks[26], (L, D_FF), f32)
    w_down = nrm(ks[27], (L, D_FF, D_MODEL), f32) * D_FF ** -0.5
    norm_final_g = 1.0 + 0.02 * nrm(ks[28], (D_MODEL,), f32)
    return {"x": x, "norm_mix_g": norm_mix_g, "w_in": w_in, "m_conv_w": m_conv_w, "m_conv_b": m_conv_b, "w_q": w_q, "w_k": w_k, "m_i_bias": m_i_bias, "m_f_bias": m_f_bias, "m_head_g": m_head_g, "m_skip": m_skip, "s5_a_re": s5_a_re, "s5_a_im": s5_a_im, "s5_log_dt": s5_log_dt, "s5_b_re": s5_b_re, "s5_b_im": s5_b_im, "s5_c_re": s5_c_re, "s5_c_im": s5_c_im, "s5_d": s5_d, "s5_w_glu": s5_w_glu, "s5_b_glu": s5_b_glu, "w_out": w_out, "norm_ffn_g": norm_ffn_g, "w_gate": w_gate, "w_val": w_val, "f_conv_w": f_conv_w, "f_conv_b": f_conv_b, "w_down": w_down, "norm_final_g": norm_final_g}


def reference(x, norm_mix_g, w_in, m_conv_w, m_conv_b, w_q, w_k, m_i_bias, m_f_bias, m_head_g, m_skip, s5_a_re, s5_a_im, s5_log_dt, s5_b_re, s5_b_im, s5_c_re, s5_c_im, s5_d, s5_w_glu, s5_b_glu, w_out, norm_ffn_g, w_gate, w_val, f_conv_w, f_conv_b, w_down, norm_final_g):
    for l in range(DEPTH):
        h = rmsnorm(x, norm_mix_g[l])
        proj = h @ w_in[l].astype(h.dtype)
        xm, v_raw, o_raw, i_raw, f_raw, u = jnp.split(proj, SPLITS, axis=-1)
        m_out = mlstm_mixer(xm, v_raw, o_raw, i_raw, f_raw, m_conv_w[l], m_conv_b[l], w_q[l], w_k[l], m_i_bias[l], m_f_bias[l], m_head_g[l], m_skip[l])
        s_out = s5_mixer(u, s5_a_re[l], s5_a_im[l], s5_log_dt[l], s5_b_re[l], s5_b_im[l], s5_c_re[l], s5_c_im[l], s5_d[l], s5_w_glu[l], s5_b_glu[l])
        mix = jnp.concatenate([m_out, s_out], axis=-1)
        x = x + mix @ w_out[l].astype(mix.dtype)
        h = rmsnorm(x, norm_ffn_g[l])
        gate = causal_dwconv(h @ w_gate[l].astype(h.dtype), f_conv_w[l], f_conv_b[l])
        val = h @ w_val[l].astype(h.dtype)
        x = x + (jax.nn.gelu(gate) * val) @ w_down[l].astype(h.dtype)
    return rmsnorm(x, norm_final_g)
```

```python
import contextlib
import numpy as np
import ml_dtypes
import concourse.bass as bass
import concourse.mybir as mybir
from concourse.bass_utils import run_bass_kernel_spmd

F32 = mybir.dt.float32
BF16 = mybir.dt.bfloat16
AF = mybir.ActivationFunctionType
ALU = mybir.AluOpType
AX = mybir.AxisListType

D_MODEL = 2048
BATCH = 2
SEQ = 4096
DEPTH = 4
M_WIDTH = 1024
M_HEADS = 4
S5_CH = 1024
D_FF = 5504
IN_COLS = 4104
EPS = 1e-6
NCORES = 8
NTOK = 1024
KC = D_MODEL // 128


class Buf:
    __slots__ = ("w", "r", "name")

    def __init__(self, name=""):
        self.w = None
        self.r = []
        self.name = name


def bufs(n, name=""):
    return [Buf(f"{name}{i}") for i in range(n)]


class Sched:
    ENG = ("pe", "act", "dve", "pool", "sp")
    NDMA_SEM = 8

    def __init__(self, nc, same_engine_sync=True):
        self.nc = nc
        self.ops = []
        self.same_engine_sync = same_engine_sync

    def op(self, eng, fn, reads=(), writes=(), dma=False):
        idx = len(self.ops)
        deps = set()
        for b in reads:
            if b.w is not None:
                deps.add(b.w)
        for b in writes:
            if b.w is not None:
                deps.add(b.w)
            deps.update(b.r)
        deps.discard(idx)
        for b in reads:
            b.r.append(idx)
        for b in writes:
            b.w = idx
            b.r = []
        self.ops.append((eng, fn, deps, dma))
        return idx

    def dma(self, eng, out, in_, reads=(), writes=(), **kw):
        return self.op(eng, lambda e: e.dma_start(out=out, in_=in_, **kw), reads, writes, dma=True)

    def emit(self):
        nc = self.nc
        engs = {"pe": nc.tensor, "act": nc.scalar, "dve": nc.vector, "pool": nc.gpsimd, "sp": nc.sync}
        ops = self.ops
        n = len(ops)
        ses = self.same_engine_sync
        needed = [False] * n
        for i, (e, fn, deps, isd) in enumerate(ops):
            for d in deps:
                pe_, _, _, pd = ops[d]
                if pd or pe_ != e or isd:
                    needed[d] = True
                elif ses and e != "pe":
                    needed[d] = True
        with contextlib.ExitStack() as st:
            esem = {e: st.enter_context(nc.semaphore(f"s_{e}")) for e in self.ENG}
            dsem = {e: [st.enter_context(nc.semaphore(f"d_{e}{k}")) for k in range(self.NDMA_SEM)]
                    for e in ("sp", "act", "pool")}
            ecount = {e: 0 for e in self.ENG}
            dcount = {e: 0 for e in dsem}
            tok = [None] * n
            seen = {e: {} for e in self.ENG}
            for i, (e, fn, deps, isd) in enumerate(ops):
                E = engs[e]
                waits = {}
                for d in deps:
                    t = tok[d]
                    if t is None:
                        continue
                    key = t[:-1]
                    if t[-1] > waits.get(key, 0):
                        waits[key] = t[-1]
                if isd:
                    k = dcount[e] % self.NDMA_SEM
                    rnd = dcount[e] // self.NDMA_SEM
                    if rnd > 0:
                        key = ("d", e, k)
                        waits[key] = max(waits.get(key, 0), 16 * rnd)
                for key, val in waits.items():
                    if seen[e].get(key, 0) >= val:
                        continue
                    seen[e][key] = val
                    sem = esem[key[1]] if key[0] == "e" else dsem[key[1]][key[2]]
                    E.wait_ge(sem, val)
                ins = fn(E)
                if isd:
                    k = dcount[e] % self.NDMA_SEM
                    rnd = dcount[e] // self.NDMA_SEM
                    ins.then_inc(dsem[e][k], 16)
                    tok[i] = ("d", e, k, 16 * (rnd + 1))
                    dcount[e] += 1
                elif needed[i]:
                    ecount[e] += 1
                    ins.then_inc(esem[e], 1)
                    tok[i] = ("e", e, ecount[e])
            E = engs["sp"]
            for e in dsem:
                for k in range(self.NDMA_SEM):
                    cnt = (dcount[e] - k + self.NDMA_SEM - 1) // self.NDMA_SEM
                    if cnt > 0 and seen["sp"].get(("d", e, k), 0) < 16 * cnt:
                        E.wait_ge(dsem[e][k], 16 * cnt)
            self.stats = dict(n_ops=n, ecount=dict(ecount), dcount=dict(dcount))


class Ctx:
    def __init__(self):
        self.nc = bass.Bass("TRN2", target_bir_lowering=False)
        self.s = Sched(self.nc)
        self.st = contextlib.ExitStack()
        self._n = 0

    def sb(self, shape, dt, name=None):
        self._n += 1
        return self.st.enter_context(self.nc.sbuf_tensor(name or f"sb{self._n}", list(shape), dt))

    def ps(self, shape, dt=F32, name=None):
        self._n += 1
        return self.st.enter_context(self.nc.psum_tensor(name or f"ps{self._n}", list(shape), dt))

    def dram(self, name, shape, dt, kind):
        return self.nc.dram_tensor(name, list(shape), dt, kind=kind).ap()

    def finish(self):
        self.s.emit()
        self.st.close()
        return self.nc


class Rot:
    def __init__(self, items):
        self.items = items
        self.i = 0

    def next(self):
        it = self.items[self.i % len(self.items)]
        self.i += 1
        return it


def mm(s, out, lhsT, rhs, start, stop, reads, writes):
    s.op("pe", lambda e, o=out, l=lhsT, r=rhs, a=start, b=stop: e.matmul(o, l, r, start=a, stop=b),
         reads=reads, writes=writes)


def load_w(s, wrot, w_dram, kcs, f0, fw, k0=0):
    wt, wb = wrot.next()
    src = w_dram[k0 * 128:(k0 + kcs) * 128, f0:f0 + fw].rearrange("(kc p) f -> p kc f", p=128)
    s.dma("pool", wt[:, 0:kcs, 0:fw], src, writes=[wb])
    return wt, wb


def rmsnorm_T(c, xT, xb, gcol, gb, hT, hb, ntok, tiles, sq, sqb, ones, onesb, pstile, rstd, rstdb, epsc, epsb):
    s = c.s
    for kc in range(KC):
        s.op("act", lambda e, kc=kc: e.activation(out=sq[:, kc, 0:ntok], in_=xT[:, kc, 0:ntok], func=AF.Square),
             reads=[xb[kc]], writes=[sqb[kc]])
    for (t0, tn) in tiles:
        pt, pb = pstile(tn)
        for kc in range(KC):
            mm(s, pt[:, 0:tn], ones[:, :], sq[:, kc, t0:t0 + tn], kc == 0, kc == KC - 1,
               reads=[sqb[kc], onesb], writes=[pb])
        s.op("act", lambda e, pt=pt, t0=t0, tn=tn: e.activation(
            out=rstd[:, t0:t0 + tn], in_=pt[:, 0:tn], func=AF.Sqrt, bias=epsc[:, 0:1]), reads=[pb, epsb], writes=[rstdb])
        s.op("dve", lambda e, t0=t0, tn=tn: e.reciprocal(out=rstd[:, t0:t0 + tn], in_=rstd[:, t0:t0 + tn]),
             reads=[rstdb], writes=[rstdb])
    for kc in range(KC):
        s.op("dve", lambda e, kc=kc: e.scalar_tensor_tensor(
            out=hT[:, kc, 0:ntok], in0=xT[:, kc, 0:ntok], scalar=gcol[:, kc:kc + 1], in1=rstd[:, 0:ntok],
            op0=ALU.mult, op1=ALU.mult), reads=[xb[kc], rstdb, gb], writes=[hb[kc]])


def build_A():
    c = Ctx()
    s = c.s
    xT_d = c.dram("xT", [D_MODEL, NTOK], F32, "ExternalInput")
    g_d = c.dram("g", [128, KC], F32, "ExternalInput")
    w_d = c.dram("w", [D_MODEL, IN_COLS], F32, "ExternalInput")
    o_d = c.dram("projT", [IN_COLS, NTOK], F32, "ExternalOutput")

    xT = c.sb([128, KC, NTOK], F32)
    xb = bufs(KC, "x")
    hT = c.sb([128, KC, NTOK], BF16)
    hb = bufs(KC, "h")
    sq = c.sb([128, KC, NTOK], BF16)
    sqb = bufs(KC, "sq")
    gcol = c.sb([128, KC], F32)
    gb = Buf("g")
    ones = c.sb([128, 128], BF16)
    onesb = Buf("ones")
    rstd = c.sb([128, NTOK], F32)
    rstdb = Buf("rstd")
    psrot = Rot([(c.ps([128, 512]), Buf(f"ps{i}")) for i in range(8)])
    wrot = Rot([(c.sb([128, KC, 128], BF16), Buf(f"w{i}")) for i in range(4)])
    orot = Rot([(c.sb([128, NTOK], F32), bufs(2, f"o{i}_")) for i in range(3)])

    s.dma("sp", gcol[:, :], g_d[:, :], writes=[gb])
    s.op("dve", lambda e: e.memset(ones[:, :], 1.0 / D_MODEL), writes=[onesb])
    epsc = c.sb([128, 1], F32)
    epsb = Buf("eps")
    s.op("dve", lambda e: e.memset(epsc[:, :], EPS), writes=[epsb])
    xv = xT_d.rearrange("(kc p) t -> p kc t", p=128)
    for kc in range(KC):
        s.dma("sp", xT[:, kc, :], xv[:, kc, :], writes=[xb[kc]])
    tiles = [(0, 512), (512, 512)]
    rmsnorm_T(c, xT, xb, gcol, gb, hT, hb, NTOK, tiles, sq, sqb, ones, onesb, lambda tn: psrot.next(), rstd, rstdb, epsc, epsb)
    nf = (IN_COLS + 127) // 128
    for fc in range(nf):
        f0 = fc * 128
        fw = min(128, IN_COLS - f0)
        wt, wb = load_w(s, wrot, w_d, KC, f0, fw)
        ot, ob = orot.next()
        for (t0, tn) in tiles:
            pt, pb = psrot.next()
            for kc in range(KC):
                mm(s, pt[0:fw, 0:tn], wt[:, kc, 0:fw], hT[:, kc, t0:t0 + tn], kc == 0, kc == KC - 1,
                   reads=[wb, hb[kc]], writes=[pb])
            eng = "act" if (t0 // 512) % 2 == 0 else "dve"
            if eng == "act":
                s.op("act", lambda e, pt=pt, ot=ot, t0=t0, tn=tn, fw=fw: e.activation(
                    out=ot[0:fw, t0:t0 + tn], in_=pt[0:fw, 0:tn], func=AF.Copy), reads=[pb], writes=[ob[t0 // 512]])
            else:
                s.op("dve", lambda e, pt=pt, ot=ot, t0=t0, tn=tn, fw=fw: e.tensor_copy(
                    out=ot[0:fw, t0:t0 + tn], in_=pt[0:fw, 0:tn]), reads=[pb], writes=[ob[t0 // 512]])
        s.dma("sp", o_d[f0:f0 + fw, :], ot[0:fw, :], reads=ob)
    return c.finish()


def barrier(s, eng_name, tiny, ins, outs):
    s.op(eng_name, lambda e: e.memset(tiny[:, :], 0.0), reads=[], writes=list(ins) + list(outs))


NT2 = NTOK + 2
FC_TOT = D_FF // 128
NPASS = 4


def build_C(final):
    c = Ctx()
    s = c.s
    xT_d = c.dram("xT", [D_MODEL, NT2], F32, "ExternalInput")
    mT_d = c.dram("mixT", [D_MODEL, NT2], BF16, "ExternalInput")
    wglu_d = c.dram("w_glu", [S5_CH, S5_CH], F32, "ExternalInput")
    bglu_d = c.dram("b_glu", [128, 8], F32, "ExternalInput")
    wout_d = c.dram("w_out", [D_MODEL, D_MODEL], F32, "ExternalInput")
    g_d = c.dram("g", [128, KC], F32, "ExternalInput")
    wg_d = c.dram("w_gate", [D_MODEL, D_FF], F32, "ExternalInput")
    wv_d = c.dram("w_val", [D_MODEL, D_FF], F32, "ExternalInput")
    cw_d = c.dram("cw", [128, FC_TOT, 3], F32, "ExternalInput")
    cb_d = c.dram("cb", [128, FC_TOT], F32, "ExternalInput")
    wd_d = c.dram("w_down", [D_FF, D_MODEL], F32, "ExternalInput")
    gf_d = c.dram("gf", [128, KC], F32, "ExternalInput")
    o_d = c.dram("xoT", [D_MODEL, NTOK], F32, "ExternalOutput")

    xT = c.sb([128, KC, NT2], F32)
    xb = bufs(KC, "x")
    R1 = c.sb([128, KC, NT2], BF16)
    r1b = bufs(KC, "r1")
    R2 = c.sb([128, KC, NT2], BF16)
    r2b = bufs(KC, "r2")
    mT, mb = R1, r1b
    sT, sTb = R2, r2b
    hT, hb = R2, r2b
    sq, sqb = R1, r1b
    NFP = (FC_TOT + NPASS - 1) // NPASS
    act = R1[:, :, :].rearrange("p a b -> p (a b)")[:, 0:NFP * NTOK].rearrange("p (j t) -> p j t", t=NTOK)
    actb = bufs(NFP, "act")
    gcol = c.sb([128, KC], F32)
    gb = Buf("g")
    gfcol = c.sb([128, KC], F32)
    gfb = Buf("gf")
    bglu = c.sb([128, 8], F32)
    bglub = Buf("bglu")
    cw = c.sb([128, FC_TOT, 3], F32)
    cwb = Buf("cw")
    cb = c.sb([128, FC_TOT], F32)
    cbb = Buf("cb")
    ones = c.sb([128, 128], BF16)
    onesb = Buf("ones")
    epsc = c.sb([128, 1], F32)
    epsb = Buf("eps")
    tiny = c.sb([128, 1], F32)
    rstd = c.sb([128, NT2], F32)
    rstdb = Buf("rstd")
    psrot = Rot([(c.ps([128, 512]), Buf(f"ps{i}")) for i in range(7)])
    psh = c.ps([128, 512])
    hrot = Rot([(psh[:, 64 * i:64 * i + 64], Buf(f"psh{i}")) for i in range(8)])
    wrot = Rot([(c.sb([128, KC, 128], BF16), Buf(f"w{i}")) for i in range(4)])
    wdrot = Rot([(c.sb([128, NFP, 128], BF16), Buf(f"wd{i}")) for i in range(3)])
    gprot = Rot([(c.sb([128, NT2], F32), Buf(f"gp{i}")) for i in range(2)])
    t1rot = Rot([(c.sb([128, NTOK], F32), Buf(f"t1{i}")) for i in range(2)])
    t2rot = Rot([(c.sb([128, NTOK], F32), Buf(f"t2{i}")) for i in range(2)])

    def pstile(tn):
        return hrot.next() if tn <= 64 else psrot.next()

    for (t, d, b) in ((gcol, g_d, gb), (gfcol, gf_d, gfb), (bglu, bglu_d, bglub), (cb, cb_d, cbb)):
        s.dma("sp", t[:, :], d[:, :], writes=[b])
    s.dma("sp", cw[:, :, :], cw_d[:, :, :], writes=[cwb])
    s.op("dve", lambda e: e.memset(ones[:, :], 1.0 / D_MODEL), writes=[onesb])
    s.op("dve", lambda e: e.memset(epsc[:, :], EPS), writes=[epsb])
    xv = xT_d.rearrange("(kc p) t -> p kc t", p=128)
    mv = mT_d.rearrange("(kc p) t -> p kc t", p=128)
    for kc in range(KC):
        s.dma("sp", mT[:, kc, :], mv[:, kc, :], writes=[mb[kc]])
    for kc in range(KC):
        s.dma("sp", xT[:, kc, :], xv[:, kc, :], writes=[xb[kc]])
    T3 = [(0, 2), (2, 512), (514, 512)]

    for fc in range(8):
        wt, wb = load_w(s, wrot, wglu_d, 8, fc * 128, 128)
        for (t0, tn) in T3:
            pt, pb = pstile(tn)
            for kc in range(8):
                mm(s, pt[:, 0:tn], wt[:, kc, :], mT[:, 8 + kc, t0:t0 + tn], kc == 0, kc == 7,
                   reads=[wb, mb[8 + kc]], writes=[pb])
            t1, t1b = t1rot.next()
            s.op("act", lambda e, pt=pt, t1=t1, tn=tn, fc=fc: e.activation(
                out=t1[:, 0:tn], in_=pt[:, 0:tn], func=AF.Sigmoid, bias=bglu[:, fc:fc + 1]),
                reads=[pb, bglub], writes=[t1b])
            s.op("dve", lambda e, t1=t1, t0=t0, tn=tn, fc=fc: e.tensor_tensor(
                out=sT[:, fc, t0:t0 + tn], in0=t1[:, 0:tn], in1=mT[:, 8 + fc, t0:t0 + tn], op=ALU.mult),
                reads=[t1b, mb[8 + fc]], writes=[sTb[fc]])

    for dc in range(KC):
        wt, wb = load_w(s, wrot, wout_d, KC, dc * 128, 128)
        for (t0, tn) in T3:
            pt, pb = pstile(tn)
            for kc in range(KC):
                rhs = mT[:, kc, t0:t0 + tn] if kc < 8 else sT[:, kc - 8, t0:t0 + tn]
                rb = mb[kc] if kc < 8 else sTb[kc - 8]
                mm(s, pt[:, 0:tn], wt[:, kc, :], rhs, kc == 0, kc == KC - 1, reads=[wb, rb], writes=[pb])
            s.op("dve", lambda e, pt=pt, dc=dc, t0=t0, tn=tn: e.tensor_tensor(
                out=xT[:, dc, t0:t0 + tn], in0=xT[:, dc, t0:t0 + tn], in1=pt[:, 0:tn], op=ALU.add),
                reads=[pb, xb[dc]], writes=[xb[dc]])

    rmsnorm_T(c, xT, xb, gcol, gb, hT, hb, NT2, T3, sq, sqb, ones, onesb, pstile, rstd, rstdb, epsc, epsb)
    barrier(s, "dve", tiny, r1b, actb)

    fc_lists = [list(range(p * NFP, min(FC_TOT, (p + 1) * NFP))) for p in range(NPASS)]
    for fcs in fc_lists:
        for j, fc in enumerate(fcs):
            wg, wgb = load_w(s, wrot, wg_d, KC, fc * 128, 128)
            wv, wvb = load_w(s, wrot, wv_d, KC, fc * 128, 128)
            gp, gpb = gprot.next()
            for (t0, tn) in T3:
                pt, pb = pstile(tn)
                for kc in range(KC):
                    mm(s, pt[:, 0:tn], wg[:, kc, :], hT[:, kc, t0:t0 + tn], kc == 0, kc == KC - 1,
                       reads=[wgb, hb[kc]], writes=[pb])
                s.op("act", lambda e, pt=pt, gp=gp, t0=t0, tn=tn: e.activation(
                    out=gp[:, t0:t0 + tn], in_=pt[:, 0:tn], func=AF.Copy), reads=[pb], writes=[gpb])
            t1, t1b = t1rot.next()
            s.op("dve", lambda e, gp=gp, t1=t1, fc=fc: e.tensor_scalar(
                out=t1[:, :], in0=gp[:, 2:NT2], scalar1=cw[:, fc, 2:3], scalar2=cb[:, fc:fc + 1],
                op0=ALU.mult, op1=ALU.add), reads=[gpb, cwb, cbb], writes=[t1b])
            s.op("dve", lambda e, gp=gp, t1=t1, fc=fc: e.scalar_tensor_tensor(
                out=t1[:, :], in0=gp[:, 1:NT2 - 1], scalar=cw[:, fc, 1:2], in1=t1[:, :],
                op0=ALU.mult, op1=ALU.add), reads=[gpb, cwb, t1b], writes=[t1b])
            s.op("dve", lambda e, gp=gp, t1=t1, fc=fc: e.scalar_tensor_tensor(
                out=t1[:, :], in0=gp[:, 0:NT2 - 2], scalar=cw[:, fc, 0:1], in1=t1[:, :],
                op0=ALU.mult, op1=ALU.add), reads=[gpb, cwb, t1b], writes=[t1b])
            t2, t2b = t2rot.next()
            s.op("act", lambda e, t1=t1, t2=t2: e.activation(out=t2[:, :], in_=t1[:, :], func=AF.Gelu),
                 reads=[t1b], writes=[t2b])
            for h in range(2):
                t0 = 2 + 512 * h
                pt, pb = psrot.next()
                for kc in range(KC):
                    mm(s, pt[:, 0:512], wv[:, kc, :], hT[:, kc, t0:t0 + 512], kc == 0, kc == KC - 1,
                       reads=[wvb, hb[kc]], writes=[pb])
                s.op("dve", lambda e, pt=pt, t2=t2, j=j, h=h: e.tensor_tensor(
                    out=act[:, j, 512 * h:512 * h + 512], in0=t2[:, 512 * h:512 * h + 512], in1=pt[:, 0:512],
                    op=ALU.mult), reads=[pb, t2b], writes=[actb[j]])
        nfc = len(fcs)
        for dc in range(KC):
            wdt, wdb = wdrot.next()
            src = wd_d[fcs[0] * 128:(fcs[0] + nfc) * 128, dc * 128:(dc + 1) * 128].rearrange("(j p) f -> p j f", p=128)
            s.dma("pool", wdt[:, 0:nfc, :], src, writes=[wdb])
            for h in range(2):
                t0 = 2 + 512 * h
                pt, pb = psrot.next()
                for j in range(nfc):
                    mm(s, pt[:, 0:512], wdt[:, j, :], act[:, j, 512 * h:512 * h + 512], j == 0, j == nfc - 1,
                       reads=[wdb, actb[j]], writes=[pb])
                s.op("dve", lambda e, pt=pt, dc=dc, t0=t0: e.tensor_tensor(
                    out=xT[:, dc, t0:t0 + 512], in0=xT[:, dc, t0:t0 + 512], in1=pt[:, 0:512], op=ALU.add),
                    reads=[pb, xb[dc]], writes=[xb[dc]])

    ov = o_d.rearrange("(kc p) t -> p kc t", p=128)
    if final:
        barrier(s, "dve", tiny, actb, r1b)
        T2 = [(2, 512), (514, 512)]
        for kc in range(KC):
            s.op("act", lambda e, kc=kc: e.activation(out=sq[:, kc, 2:NT2], in_=xT[:, kc, 2:NT2], func=AF.Square),
                 reads=[xb[kc]], writes=[sqb[kc]])
        for (t0, tn) in T2:
            pt, pb = psrot.next()
            for kc in range(KC):
                mm(s, pt[:, 0:tn], ones[:, :], sq[:, kc, t0:t0 + tn], kc == 0, kc == KC - 1,
                   reads=[sqb[kc], onesb], writes=[pb])
            s.op("act", lambda e, pt=pt, t0=t0, tn=tn: e.activation(
                out=rstd[:, t0:t0 + tn], in_=pt[:, 0:tn], func=AF.Sqrt, bias=epsc[:, 0:1]),
                reads=[pb, epsb], writes=[rstdb])
            s.op("dve", lambda e, t0=t0, tn=tn: e.reciprocal(out=rstd[:, t0:t0 + tn], in_=rstd[:, t0:t0 + tn]),
                 reads=[rstdb], writes=[rstdb])
        for kc in range(KC):
            s.op("dve", lambda e, kc=kc: e.scalar_tensor_tensor(
                out=xT[:, kc, 2:NT2], in0=xT[:, kc, 2:NT2], scalar=gfcol[:, kc:kc + 1], in1=rstd[:, 2:NT2],
                op0=ALU.mult, op1=ALU.mult), reads=[xb[kc], rstdb, gfb], writes=[xb[kc]])
    for kc in range(KC):
        s.dma("sp", ov[:, kc, :], xT[:, kc, 2:NT2], reads=[xb[kc]])
    return c.finish()


LCH = 128
NCH = SEQ // LCH
DV = 256
DK = 128


def build_B1():
    c = Ctx()
    s = c.s
    xm_d = c.dram("xmT", [DV, SEQ + 3], F32, "ExternalInput")
    v_d = c.dram("v", [SEQ, DV], F32, "ExternalInput")
    o_d = c.dram("oT", [DV, SEQ], F32, "ExternalInput")
    i_d = c.dram("icol", [128, NCH], F32, "ExternalInput")
    f_d = c.dram("fcol", [128, NCH], F32, "ExternalInput")
    cw_d = c.dram("cw", [128, 2, 4], F32, "ExternalInput")
    cb_d = c.dram("cb", [128, 2], F32, "ExternalInput")
    wq_d = c.dram("wq", [DV, DK], F32, "ExternalInput")
    wk_d = c.dram("wk", [DV, DK], F32, "ExternalInput")
    ib_d = c.dram("ib", [128, 1], F32, "ExternalInput")
    fb_d = c.dram("fb", [128, 1], F32, "ExternalInput")
    hg_d = c.dram("hg", [128, 2], F32, "ExternalInput")
    sk_d = c.dram("sk", [128, 2], F32, "ExternalInput")
    out_d = c.dram("moT", [DV, SEQ], BF16, "ExternalOutput")

    xm = c.sb([128, 2, SEQ + 3], F32); xmb = bufs(2, "xm")
    cf = c.sb([128, 2, SEQ], F32); cfb = bufs(2, "cf")
    cbf = c.sb([128, 2, SEQ], BF16); cbfb = bufs(2, "cbf")
    vaug = c.sb([128, NCH, DV + 1], BF16); vb = Buf("v")
    qT = c.sb([128, SEQ], BF16); qb = Buf("q")
    kT = c.sb([128, SEQ], BF16); kb = Buf("k")
    kt = c.sb([128, NCH, DK], BF16); ktb = bufs(NCH, "kt")
    mo = c.sb([128, 2, SEQ], BF16); mob = bufs(2, "mo")
    cw = c.sb([128, 2, 4], F32); cwb = Buf()
    cb = c.sb([128, 2], F32); cbb = Buf()
    wq = c.sb([128, 2, DK], BF16); wqb = Buf()
    wk = c.sb([128, 2, DK], BF16); wkb = Buf()
    ibc = c.sb([128, 1], F32); ibb = Buf()
    fbc = c.sb([128, 1], F32); fbb = Buf()
    hg = c.sb([128, 2], F32); hgb = Buf()
    sk = c.sb([128, 2], F32); skb = Buf()
    icol = c.sb([128, NCH], F32); icb = Buf()
    fcol = c.sb([128, NCH], F32); fcb = Buf()
    lf = c.sb([128, NCH], F32); lfb_ = Buf()
    bcol = c.sb([128, NCH], F32); bcb = Buf()
    gbc = c.sb([128, NCH], F32); gbcb = Buf()
    acol = c.sb([128, NCH], F32); acb = Buf()
    wkc = c.sb([128, NCH], F32); wkcb = Buf()
    egc = c.sb([128, NCH], F32); egb = Buf()
    dmat = c.sb([128, 128], F32); dmb = Buf()
    tri = c.sb([128, 128], F32); trib = Buf()
    negm = c.sb([128, 128], F32); negb = Buf()
    ident = c.sb([128, 128], F32); idb = Buf()
    onesf = c.sb([128, 128], F32); onb = Buf()
    one1 = c.sb([128, 1], F32); o1b = Buf()
    epsc = c.sb([128, 1], F32); epsb = Buf()
    Cf = c.sb([128, DV + 1], F32); Cfb = Buf("Cf")
    Cb = c.sb([128, DV + 1], BF16); Cbb = Buf("Cb")
    banks = [c.ps([128, 512]) for _ in range(8)]
    bankb = bufs(8, "bank")

    for (t, d, b) in ((cb, cb_d, cbb), (ibc, ib_d, ibb), (fbc, fb_d, fbb), (hg, hg_d, hgb), (sk, sk_d, skb),
                      (icol, i_d, icb), (fcol, f_d, fcb)):
        s.dma("sp", t[:, :], d[:, :], writes=[b])
    s.dma("sp", cw[:, :, :], cw_d[:, :, :], writes=[cwb])
    s.dma("pool", wq[:, :, :], wq_d.rearrange("(a p) d -> p a d", p=128), writes=[wqb])
    s.dma("pool", wk[:, :, :], wk_d.rearrange("(a p) d -> p a d", p=128), writes=[wkb])
    xv = xm_d.rearrange("(a p) t -> p a t", p=128)
    for ft in range(2):
        s.dma("sp", xm[:, ft, :], xv[:, ft, :], writes=[xmb[ft]])
    s.dma("pool", vaug[:, :, 0:DV], v_d.rearrange("(c p) f -> p c f", p=128), writes=[vb])
    s.op("dve", lambda e: e.memset(vaug[:, :, DV:DV + 1], 1.0), writes=[vb])

    s.op("pool", lambda e: e.iota(dmat[:, :], [[1, 128]], base=0, channel_multiplier=-1,
                                  allow_small_or_imprecise_dtypes=True), writes=[dmb])
    s.op("dve", lambda e: e.tensor_single_scalar(out=tri[:, :], in_=dmat[:, :], scalar=0.0, op=ALU.is_ge),
         reads=[dmb], writes=[trib])
    s.op("dve", lambda e: e.tensor_single_scalar(out=ident[:, :], in_=dmat[:, :], scalar=0.0, op=ALU.is_equal),
         reads=[dmb], writes=[idb])
    s.op("dve", lambda e: e.tensor_scalar(out=negm[:, :], in0=tri[:, :], scalar1=-1.0, scalar2=30000.0,
                                          op0=ALU.add, op1=ALU.mult), reads=[trib], writes=[negb])
    s.op("dve", lambda e: e.memset(onesf[:, :], 1.0), writes=[onb])
    s.op("dve", lambda e: e.memset(one1[:, :], 1.0), writes=[o1b])
    s.op("dve", lambda e: e.memset(epsc[:, :], EPS), writes=[epsb])
    s.op("dve", lambda e: e.memset(Cf[:, :], 0.0), writes=[Cfb])
    s.op("dve", lambda e: e.memset(Cb[:, :], 0.0), writes=[Cbb])

    s.op("dve", lambda e: e.tensor_scalar(out=fbc[:, :], in0=fbc[:, :], scalar1=-1.0, scalar2=None, op0=ALU.mult),
         reads=[fbb], writes=[fbb])
    s.op("act", lambda e: e.activation(out=lf[:, :], in_=fcol[:, :], func=AF.Exp, scale=-1.0, bias=fbc[:, 0:1]),
         reads=[fcb, fbb], writes=[lfb_])
    s.op("act", lambda e: e.activation(out=lf[:, :], in_=lf[:, :], func=AF.Ln, bias=one1[:, 0:1]),
         reads=[lfb_, o1b], writes=[lfb_])
    s.op("dve", lambda e: e.tensor_scalar(out=lf[:, :], in0=lf[:, :], scalar1=-1.0, scalar2=None, op0=ALU.mult),
         reads=[lfb_], writes=[lfb_])
    g0 = banks[1][:, 256:256 + NCH]
    g1 = banks[1][:, 320:320 + NCH]
    g0b, g1b = Buf(), Buf()
    mm(s, g0, tri[:, :], lf[:, :], True, True, reads=[trib, lfb_], writes=[g0b])
    mm(s, g1, onesf[:, :], lf[:, :], True, True, reads=[onb, lfb_], writes=[g1b])
    s.op("dve", lambda e: e.tensor_copy(out=bcol[:, :], in_=g0), reads=[g0b], writes=[bcb])
    s.op("dve", lambda e: e.tensor_copy(out=gbc[:, :], in_=g1), reads=[g1b], writes=[gbcb])
    s.op("dve", lambda e: e.scalar_tensor_tensor(out=acol[:, :], in0=icol[:, :], scalar=ibc[:, 0:1], in1=bcol[:, :],
                                                 op0=ALU.add, op1=ALU.subtract),
         reads=[icb, ibb, bcb], writes=[acb])
    s.op("dve", lambda e: e.tensor_tensor(out=wkc[:, :], in0=gbc[:, :], in1=acol[:, :], op=ALU.add),
         reads=[gbcb, acb], writes=[wkcb])
    s.op("act", lambda e: e.activation(out=wkc[:, :], in_=wkc[:, :], func=AF.Exp), reads=[wkcb], writes=[wkcb])
    s.op("act", lambda e: e.activation(out=egc[:, :], in_=gbc[:, :], func=AF.Exp), reads=[gbcb], writes=[egb])

    for ft in range(2):
        s.op("dve", lambda e, ft=ft: e.tensor_scalar(
            out=cf[:, ft, :], in0=xm[:, ft, 3:SEQ + 3], scalar1=cw[:, ft, 3:4], scalar2=cb[:, ft:ft + 1],
            op0=ALU.mult, op1=ALU.add), reads=[xmb[ft], cwb, cbb], writes=[cfb[ft]])
        for k in range(3):
            s.op("dve", lambda e, ft=ft, k=k: e.scalar_tensor_tensor(
                out=cf[:, ft, :], in0=xm[:, ft, k:SEQ + k], scalar=cw[:, ft, k:k + 1], in1=cf[:, ft, :],
                op0=ALU.mult, op1=ALU.add), reads=[xmb[ft], cwb, cfb[ft]], writes=[cfb[ft]])
        s.op("act", lambda e, ft=ft: e.activation(out=cf[:, ft, :], in_=cf[:, ft, :], func=AF.Silu),
             reads=[cfb[ft]], writes=[cfb[ft]])
        s.op("pool", lambda e, ft=ft: e.tensor_copy(out=cbf[:, ft, :], in_=cf[:, ft, :]),
             reads=[cfb[ft]], writes=[cbfb[ft]])
        s.op("pool", lambda e, ft=ft: e.tensor_scalar(out=cf[:, ft, :], in0=cf[:, ft, :], scalar1=sk[:, ft:ft + 1],
                                                     scalar2=None, op0=ALU.mult),
             reads=[cfb[ft], skb, cbfb[ft]], writes=[cfb[ft]])

    pre = Rot([(banks[i], bankb[i]) for i in (2, 3, 4, 5, 7)])
    for tt in range(SEQ // 512):
        for (wt, wtb, dst, dstb, scl) in ((wq, wqb, qT, qb, DK ** -0.5), (wk, wkb, kT, kb, 1.0)):
            pt, pb = pre.next()
            for ft in range(2):
                mm(s, pt[:, :], wt[:, ft, :], cbf[:, ft, tt * 512:(tt + 1) * 512], ft == 0, ft == 1,
                   reads=[wtb, cbfb[ft]], writes=[pb])
            s.op("act", lambda e, pt=pt, dst=dst, tt=tt, scl=scl: e.activation(
                out=dst[:, tt * 512:(tt + 1) * 512], in_=pt[:, :], func=AF.Copy, scale=scl),
                reads=[pb], writes=[dstb])
        pt, pb = pre.next()
        for cc in range(4):
            ch = tt * 4 + cc
            for ft in range(2):
                mm(s, pt[:, cc * 128:(cc + 1) * 128], cbf[:, ft, ch * 128:(ch + 1) * 128], wk[:, ft, :],
                   ft == 0, ft == 1, reads=[wkb, cbfb[ft]], writes=[pb])
        for cc in range(4):
            ch = tt * 4 + cc
            s.op("dve", lambda e, pt=pt, cc=cc, ch=ch: e.tensor_scalar(
                out=kt[:, ch, :], in0=pt[:, cc * 128:(cc + 1) * 128], scalar1=wkc[:, ch:ch + 1], scalar2=None,
                op0=ALU.mult), reads=[pb, wkcb], writes=[ktb[ch]])

    pAB = [(banks[0][:, 256 * i:256 * i + 128], banks[0][:, 256 * i + 128:256 * i + 256], Buf(), Buf()) for i in range(2)]
    pS = [(banks[1][:, 128 * i:128 * i + 128], Buf()) for i in range(2)]
    pN = [(banks[2], bankb[2]), (banks[3], bankb[3])]
    pC = [(banks[4], bankb[4]), (banks[5], bankb[5])]
    pT = [[(banks[6][:, (2 * i + ft) * 128:(2 * i + ft + 1) * 128], Buf()) for ft in range(2)] for i in range(2)]
    lfrot = Rot([(c.sb([128, 128], F32), Buf()) for _ in range(2)])
    ebrot = Rot([(c.sb([128, 128], F32), Buf()) for _ in range(2)])
    wrot_ = Rot([(c.sb([128, 128], F32), Buf()) for _ in range(2)])
    qtrot = Rot([(c.sb([128, 128], BF16), Buf()) for _ in range(2)])
    swrot = Rot([(c.sb([128, 128], BF16), Buf()) for _ in range(2)])
    hnrot = Rot([(c.sb([128, DV], F32), Buf()) for _ in range(2)])
    jkrot = Rot([(c.sb([128, DV], BF16), Buf()) for _ in range(2)])
    smrot = Rot([(c.sb([128, 8], F32), Buf()) for _ in range(2)])
    orot = Rot([(c.sb([128, 2, 128], F32), Buf()) for _ in range(3)])
    t2rot = Rot([(c.sb([128, 128], F32), Buf()) for _ in range(2)])
    ov = o_d.rearrange("(a p) t -> p a t", p=128)
    for ch in range(NCH):
        cs = slice(ch * 128, (ch + 1) * 128)
        pA, pB, pAb, pBb = pAB[ch % 2]
        lfm, lfmb = lfrot.next()
        s.op("dve", lambda e, lfm=lfm, ch=ch: e.tensor_scalar(out=lfm[:, :], in0=onesf[:, :], scalar1=lf[:, ch:ch + 1],
                                                            scalar2=None, op0=ALU.mult),
             reads=[onb, lfb_], writes=[lfmb])
        mm(s, pA, lfm[:, :], tri[:, :], True, True, reads=[lfmb, trib], writes=[pAb])
        mm(s, pB, lfm[:, :], tri[:, :], True, False, reads=[lfmb, trib], writes=[pBb])
        mm(s, pB, ident[:, :], negm[:, :], False, True, reads=[idb, negb], writes=[pBb])
        eb, ebb = ebrot.next()
        s.op("act", lambda e, eb=eb, pA=pA: e.activation(out=eb[:, :], in_=pA, func=AF.Exp), reads=[pAb], writes=[ebb])
        W, Wb = wrot_.next()
        s.op("act", lambda e, W=W, pB=pB, ch=ch: e.activation(out=W[:, :], in_=pB, func=AF.Exp, bias=acol[:, ch:ch + 1]),
             reads=[pBb, acb], writes=[Wb])
        qt, qtb = qtrot.next()
        s.op("dve", lambda e, qt=qt, eb=eb, cs=cs: e.tensor_tensor(out=qt[:, :], in0=qT[:, cs], in1=eb[:, :], op=ALU.mult),
             reads=[qb, ebb], writes=[qtb])
        ps_, psb = pS[ch % 2]
        mm(s, ps_, kT[:, cs], qT[:, cs], True, True, reads=[kb, qb], writes=[psb])
        sw, swb = swrot.next()
        s.op("dve", lambda e, sw=sw, ps_=ps_, W=W: e.tensor_tensor(out=sw[:, :], in0=ps_, in1=W[:, :], op=ALU.mult),
             reads=[psb, Wb], writes=[swb])
        pn, pnb = pN[ch % 2]
        mm(s, pn[:, 0:DV + 1], sw[:, :], vaug[:, ch, :], True, False, reads=[swb, vb], writes=[pnb])
        mm(s, pn[:, 0:DV + 1], qt[:, :], Cb[:, :], False, True, reads=[qtb, Cbb], writes=[pnb])
        pc, pcb = pC[ch % 2]
        mm(s, pc[:, 0:DV + 1], kt[:, ch, :], vaug[:, ch, :], True, True, reads=[ktb[ch], vb], writes=[pcb])
        s.op("dve", lambda e, pc=pc, ch=ch: e.scalar_tensor_tensor(
            out=Cf[:, :], in0=Cf[:, :], scalar=egc[:, ch:ch + 1], in1=pc[:, 0:DV + 1], op0=ALU.mult, op1=ALU.add),
            reads=[Cfb, egb, pcb], writes=[Cfb])
        s.op("act", lambda e: e.activation(out=Cb[:, :], in_=Cf[:, :], func=AF.Copy), reads=[Cfb], writes=[Cbb])
        sm, smb = smrot.next()
        s.op("act", lambda e, sm=sm, pn=pn: e.activation(out=sm[:, 0:1], in_=pn[:, DV:DV + 1], func=AF.Abs),
             reads=[pnb], writes=[smb])
        s.op("dve", lambda e, sm=sm: e.tensor_scalar(out=sm[:, 0:1], in0=sm[:, 0:1], scalar1=1.0, scalar2=None,
                                                    op0=ALU.max), reads=[smb], writes=[smb])
        s.op("dve", lambda e, sm=sm: e.reciprocal(out=sm[:, 0:1], in_=sm[:, 0:1]), reads=[smb], writes=[smb])
        jk, jkb = jkrot.next()
        s.op("act", lambda e, jk=jk, pn=pn, sm=sm: e.activation(out=jk[:, :], in_=pn[:, 0:DV], func=AF.Square,
                                                              accum_out=sm[:, 1:2]), reads=[pnb, smb], writes=[jkb, smb])
        s.op("dve", lambda e, sm=sm: e.tensor_tensor(out=sm[:, 2:3], in0=sm[:, 1:2], in1=sm[:, 0:1], op=ALU.mult),
             reads=[smb], writes=[smb])
        s.op("dve", lambda e, sm=sm: e.tensor_tensor(out=sm[:, 2:3], in0=sm[:, 2:3], in1=sm[:, 0:1], op=ALU.mult),
             reads=[smb], writes=[smb])
        s.op("act", lambda e, sm=sm: e.activation(out=sm[:, 3:4], in_=sm[:, 2:3], func=AF.Sqrt, scale=1.0 / DV,
                                                bias=epsc[:, 0:1]), reads=[smb, epsb], writes=[smb])
        s.op("dve", lambda e, sm=sm: e.reciprocal(out=sm[:, 3:4], in_=sm[:, 3:4]), reads=[smb], writes=[smb])
        s.op("dve", lambda e, sm=sm: e.tensor_tensor(out=sm[:, 4:5], in0=sm[:, 3:4], in1=sm[:, 0:1], op=ALU.mult),
             reads=[smb], writes=[smb])
        hn, hnb = hnrot.next()
        s.op("act", lambda e, hn=hn, pn=pn, sm=sm: e.activation(out=hn[:, :], in_=pn[:, 0:DV], func=AF.Copy,
                                                              scale=sm[:, 4:5]), reads=[pnb, smb], writes=[hnb])
        ot, otb = orot.next()
        s.dma("sp", ot[:, :, :], ov[:, :, cs], writes=[otb])
        s.op("act", lambda e, ot=ot: e.activation(out=ot[:, :, :], in_=ot[:, :, :], func=AF.Sigmoid),
             reads=[otb], writes=[otb])
        for ft in range(2):
            ptt, pttb = pT[ch % 2][ft]
            s.op("pe", lambda e, ptt=ptt, hn=hn, ft=ft: e.transpose(ptt, hn[:, ft * 128:(ft + 1) * 128], ident[:, :]),
                 reads=[hnb, idb], writes=[pttb])
            t2, t2b = t2rot.next()
            s.op("dve", lambda e, t2=t2, ptt=ptt, ft=ft, cs=cs: e.scalar_tensor_tensor(
                out=t2[:, :], in0=ptt, scalar=hg[:, ft:ft + 1], in1=cf[:, ft, cs], op0=ALU.mult, op1=ALU.add),
                reads=[pttb, hgb, cfb[ft]], writes=[t2b])
            s.op("dve", lambda e, t2=t2, ot=ot, ft=ft, cs=cs: e.tensor_tensor(
                out=mo[:, ft, cs], in0=t2[:, :], in1=ot[:, ft, :], op=ALU.mult),
                reads=[t2b, otb], writes=[mob[ft]])
    outv = out_d.rearrange("(a p) t -> p a t", p=128)
    for ft in range(2):
        s.dma("sp", outv[:, ft, :], mo[:, ft, :], reads=[mob[ft]])
    return c.finish()


NPAIR = 8
NSTEP = 12
PI = float(np.pi)


def build_B2():
    c = Ctx()
    s = c.s
    u_d = c.dram("uT", [256, SEQ], F32, "ExternalInput")
    are_d = c.dram("a_re", [128, NPAIR], F32, "ExternalInput")
    aim_d = c.dram("a_im", [128, NPAIR], F32, "ExternalInput")
    ldt_d = c.dram("log_dt", [128, NPAIR], F32, "ExternalInput")
    bre_d = c.dram("b_re", [128, NPAIR, 16], F32, "ExternalInput")
    bim_d = c.dram("b_im", [128, NPAIR, 16], F32, "ExternalInput")
    cre_d = c.dram("c_re", [128, NPAIR, 16], F32, "ExternalInput")
    cim_d = c.dram("c_im", [128, NPAIR, 16], F32, "ExternalInput")
    dd_d = c.dram("dd", [128, 2], F32, "ExternalInput")
    out_d = c.dram("yT", [256, SEQ], BF16, "ExternalOutput")

    u = c.sb([128, 2, SEQ], F32); ub = bufs(2, "u")
    ubf = c.sb([128, 2, SEQ], BF16); ubfb = bufs(2, "ubf")
    R = [c.sb([128, SEQ], F32) for _ in range(2)]; Rb = bufs(2, "R")
    I = [c.sb([128, SEQ], F32) for _ in range(2)]; Ib = bufs(2, "I")
    yacc = c.sb([128, SEQ], F32); yab = Buf("yacc")
    yo = c.sb([128, 2, SEQ], BF16); yob = bufs(2, "yo")
    P = lambda n: (c.sb([128, NPAIR], F32, name=n), Buf(n))
    are, areb = P("are"); aim, aimb = P("aim"); dt, dtb = P("dt")
    zr, zrb = P("zr"); zi, zib = P("zi"); mag, magb = P("mag")
    sn, snb = P("sn"); cs_, csb = P("cs"); t0_, t0b = P("t0"); t1_, t1b = P("t1")
    ar, arb = P("ar"); ai, aib = P("ai"); cr, crb = P("cr"); ci, cib = P("ci")
    pw = c.sb([128, NSTEP, 3, NPAIR], F32); pwb = bufs(NSTEP, "pw")
    bre = c.sb([128, NPAIR, 16], F32); breb = Buf()
    bim = c.sb([128, NPAIR, 16], F32); bimb = Buf()
    cre = c.sb([128, NPAIR, 16], F32); creb = Buf()
    cim = c.sb([128, NPAIR, 16], F32); cimb = Buf()
    bbr = c.sb([128, NPAIR, 16], F32); bbrb = Buf()
    bbi = c.sb([128, NPAIR, 16], F32); bbib = Buf()
    tmpb3 = c.sb([128, NPAIR, 16], F32); tmpb3b = Buf()
    dd = c.sb([128, 2], F32); ddb = Buf()
    pad = c.sb([128, 2, 128], F32); padb = bufs(2, "pad")
    lB = c.sb([128, NPAIR, 2, 128], BF16); lBb = bufs(NPAIR, "lB")
    lC = c.sb([128, NPAIR, 2, 128], F32); lCb = bufs(NPAIR, "lC")
    dmat = c.sb([128, 128], F32); dmb = Buf()
    ident = c.sb([128, 128], F32); idb = Buf()
    pic = c.sb([128, 1], F32); picb = Buf()
    psrot = Rot([(c.ps([128, 512]), Buf(f"ps{i}")) for i in range(8)])

    for (t, d, b) in ((are, are_d, areb), (aim, aim_d, aimb), (dt, ldt_d, dtb), (dd, dd_d, ddb)):
        s.dma("sp", t[:, :], d[:, :], writes=[b])
    for (t, d, b) in ((bre, bre_d, breb), (bim, bim_d, bimb), (cre, cre_d, creb), (cim, cim_d, cimb)):
        s.dma("sp", t[:, :, :], d[:, :, :], writes=[b])
    uv = u_d.rearrange("(a p) t -> p a t", p=128)
    for ct in range(2):
        s.dma("sp", u[:, ct, :], uv[:, ct, :], writes=[ub[ct]])
        s.op("pool", lambda e, ct=ct: e.tensor_copy(out=ubf[:, ct, :], in_=u[:, ct, :]), reads=[ub[ct]], writes=[ubfb[ct]])
    s.op("pool", lambda e: e.iota(dmat[:, :], [[1, 128]], base=0, channel_multiplier=-1,
                                  allow_small_or_imprecise_dtypes=True), writes=[dmb])
    s.op("dve", lambda e: e.tensor_single_scalar(out=ident[:, :], in_=dmat[:, :], scalar=0.0, op=ALU.is_equal),
         reads=[dmb], writes=[idb])
    s.op("dve", lambda e: e.memset(pic[:, :], -PI), writes=[picb])

    def ew(eng, fn, reads, writes):
        s.op(eng, fn, reads=reads, writes=writes)

    TT = ALU
    ew("act", lambda e: e.activation(out=dt[:, :], in_=dt[:, :], func=AF.Exp), [dtb], [dtb])
    ew("dve", lambda e: e.tensor_tensor(out=zr[:, :], in0=are[:, :], in1=dt[:, :], op=TT.mult), [areb, dtb], [zrb])
    ew("dve", lambda e: e.tensor_tensor(out=zi[:, :], in0=aim[:, :], in1=dt[:, :], op=TT.mult), [aimb, dtb], [zib])
    ew("act", lambda e: e.activation(out=mag[:, :], in_=zr[:, :], func=AF.Exp), [zrb], [magb])
    ni = c.sb([128, NPAIR], mybir.dt.int32); nib = Buf()
    nf = c.sb([128, NPAIR], F32); nfb = Buf()

    def sin_of(dst, dstb, shift):
        ew("dve", lambda e: e.tensor_scalar(out=t0_[:, :], in0=zi[:, :], scalar1=shift, scalar2=1.0 / (2 * PI),
                                            op0=TT.add, op1=TT.mult), [zib, t0b], [t0b])
        ew("dve", lambda e: e.tensor_copy(out=ni[:, :], in_=t0_[:, :]), [t0b], [nib])
        ew("dve", lambda e: e.tensor_copy(out=nf[:, :], in_=ni[:, :]), [nib], [nfb])
        ew("dve", lambda e: e.tensor_scalar(out=t1_[:, :], in0=zi[:, :], scalar1=shift, scalar2=None, op0=TT.add), [zib, t1b], [t1b])
        ew("dve", lambda e: e.scalar_tensor_tensor(out=t1_[:, :], in0=nf[:, :], scalar=-2 * PI, in1=t1_[:, :],
                                                   op0=TT.mult, op1=TT.add), [nfb, t1b], [t1b])
        ew("dve", lambda e: e.tensor_scalar(out=t0_[:, :], in0=t1_[:, :], scalar1=PI, scalar2=-2 * PI,
                                            op0=TT.is_gt, op1=TT.mult), [t1b, t0b], [t0b])
        ew("dve", lambda e: e.tensor_tensor(out=t1_[:, :], in0=t1_[:, :], in1=t0_[:, :], op=TT.add), [t1b, t0b], [t1b])
        ew("act", lambda e: e.activation(out=dst[:, :], in_=t1_[:, :], func=AF.Sin), [t1b], [dstb])

    sin_of(sn, snb, 0.0)
    sin_of(cs_, csb, 0.5 * PI)
    ew("dve", lambda e: e.tensor_tensor(out=ar[:, :], in0=mag[:, :], in1=cs_[:, :], op=TT.mult), [magb, csb], [arb])
    ew("dve", lambda e: e.tensor_tensor(out=ai[:, :], in0=mag[:, :], in1=sn[:, :], op=TT.mult), [magb, snb], [aib])
    ew("dve", lambda e: e.tensor_tensor(out=t0_[:, :], in0=are[:, :], in1=are[:, :], op=TT.mult), [areb, t0b], [t0b])
    ew("dve", lambda e: e.tensor_tensor(out=t1_[:, :], in0=aim[:, :], in1=aim[:, :], op=TT.mult), [aimb, t1b], [t1b])
    ew("dve", lambda e: e.tensor_tensor(out=t0_[:, :], in0=t0_[:, :], in1=t1_[:, :], op=TT.add), [t0b, t1b], [t0b])
    ew("dve", lambda e: e.reciprocal(out=t0_[:, :], in_=t0_[:, :]), [t0b], [t0b])
    ew("dve", lambda e: e.tensor_scalar(out=t1_[:, :], in0=ar[:, :], scalar1=-1.0, scalar2=None, op0=TT.add), [arb, t1b], [t1b])
    ew("dve", lambda e: e.tensor_tensor(out=cr[:, :], in0=t1_[:, :], in1=are[:, :], op=TT.mult), [t1b, areb], [crb])
    ew("dve", lambda e: e.tensor_tensor(out=zr[:, :], in0=ai[:, :], in1=aim[:, :], op=TT.mult), [aib, aimb, zrb], [zrb])
    ew("dve", lambda e: e.tensor_tensor(out=cr[:, :], in0=cr[:, :], in1=zr[:, :], op=TT.add), [crb, zrb], [crb])
    ew("dve", lambda e: e.tensor_tensor(out=cr[:, :], in0=cr[:, :], in1=t0_[:, :], op=TT.mult), [crb, t0b], [crb])
    ew("dve", lambda e: e.tensor_tensor(out=ci[:, :], in0=ai[:, :], in1=are[:, :], op=TT.mult), [aib, areb], [cib])
    ew("dve", lambda e: e.tensor_tensor(out=zr[:, :], in0=t1_[:, :], in1=aim[:, :], op=TT.mult), [t1b, aimb, zrb], [zrb])
    ew("dve", lambda e: e.tensor_tensor(out=ci[:, :], in0=ci[:, :], in1=zr[:, :], op=TT.subtract), [cib, zrb], [cib])
    ew("dve", lambda e: e.tensor_tensor(out=ci[:, :], in0=ci[:, :], in1=t0_[:, :], op=TT.mult), [cib, t0b], [cib])
    crB = cr[:, :].unsqueeze(2).to_broadcast([128, NPAIR, 16])
    ciB = ci[:, :].unsqueeze(2).to_broadcast([128, NPAIR, 16])
    ew("dve", lambda e: e.tensor_tensor(out=bbr[:, :, :], in0=bre[:, :, :], in1=crB, op=TT.mult), [breb, crb], [bbrb])
    ew("dve", lambda e: e.tensor_tensor(out=tmpb3[:, :, :], in0=bim[:, :, :], in1=ciB, op=TT.mult), [bimb, cib], [tmpb3b])
    ew("dve", lambda e: e.tensor_tensor(out=bbr[:, :, :], in0=bbr[:, :, :], in1=tmpb3[:, :, :], op=TT.subtract), [bbrb, tmpb3b], [bbrb])
    ew("dve", lambda e: e.tensor_tensor(out=bbi[:, :, :], in0=bim[:, :, :], in1=crB, op=TT.mult), [bimb, crb], [bbib])
    ew("dve", lambda e: e.tensor_tensor(out=tmpb3[:, :, :], in0=bre[:, :, :], in1=ciB, op=TT.mult), [breb, cib, bbrb], [tmpb3b])
    ew("dve", lambda e: e.tensor_tensor(out=bbi[:, :, :], in0=bbi[:, :, :], in1=tmpb3[:, :, :], op=TT.add), [bbib, tmpb3b], [bbib])
    ew("dve", lambda e: e.tensor_copy(out=pw[:, 0, 0, :], in_=ar[:, :]), [arb], [pwb[0]])
    ew("dve", lambda e: e.tensor_copy(out=pw[:, 0, 1, :], in_=ai[:, :]), [aib], [pwb[0]])
    for k in range(NSTEP):
        if k > 0:
            ew("dve", lambda e, k=k: e.tensor_tensor(out=t0_[:, :], in0=pw[:, k - 1, 0, :], in1=pw[:, k - 1, 0, :], op=TT.mult),
               [pwb[k - 1], t0b], [t0b])
            ew("dve", lambda e, k=k: e.tensor_tensor(out=t1_[:, :], in0=pw[:, k - 1, 1, :], in1=pw[:, k - 1, 1, :], op=TT.mult),
               [pwb[k - 1], t1b], [t1b])
            ew("dve", lambda e, k=k: e.tensor_tensor(out=pw[:, k, 0, :], in0=t0_[:, :], in1=t1_[:, :], op=TT.subtract),
               [t0b, t1b], [pwb[k]])
            ew("dve", lambda e, k=k: e.scalar_tensor_tensor(out=pw[:, k, 1, :], in0=pw[:, k - 1, 0, :], scalar=2.0,
                                                           in1=pw[:, k - 1, 1, :], op0=TT.mult, op1=TT.mult),
               [pwb[k - 1]], [pwb[k]])
        ew("dve", lambda e, k=k: e.tensor_scalar(out=pw[:, k, 2, :], in0=pw[:, k, 1, :], scalar1=-1.0, scalar2=None, op0=TT.mult),
           [pwb[k]], [pwb[k]])
    for pr in range(NPAIR):
        off = (pr % 4) * 32
        for ri, src in enumerate((bbr, bbi)):
            pd, pdb = pad[:, ri, :], padb[ri]
            s.op("pool", lambda e, pd=pd: e.memset(pd, 0.0), writes=[pdb])
            for gi in range(2):
                s.op("pool", lambda e, pd=pd, gi=gi, src=src, pr=pr, off=off: e.tensor_copy(
                    out=pd[gi * 64:(gi + 1) * 64, off + gi * 16:off + gi * 16 + 16], in_=src[gi * 64:(gi + 1) * 64, pr, :]),
                    reads=[bbrb, bbib], writes=[pdb])
            pt, pb = psrot.next()
            s.op("pe", lambda e, pt=pt, pd=pd: e.transpose(pt[:, 0:128], pd, ident[:, :]), reads=[pdb, idb], writes=[pb])
            s.op("act", lambda e, pt=pt, pr=pr, ri=ri: e.activation(out=lB[:, pr, ri, :], in_=pt[:, 0:128], func=AF.Copy),
                 reads=[pb], writes=[lBb[pr]])
        s.op("pool", lambda e, pr=pr: e.memset(lC[:, pr, :, :], 0.0), writes=[lCb[pr]])
        for gi in range(2):
            s.op("pool", lambda e, gi=gi, pr=pr, off=off: e.tensor_copy(
                out=lC[gi * 64:(gi + 1) * 64, pr, 0, off + gi * 16:off + gi * 16 + 16], in_=cre[gi * 64:(gi + 1) * 64, pr, :]),
                reads=[creb], writes=[lCb[pr]])
            s.op("pool", lambda e, gi=gi, pr=pr, off=off: e.tensor_scalar(
                out=lC[gi * 64:(gi + 1) * 64, pr, 1, off + gi * 16:off + gi * 16 + 16], in0=cim[gi * 64:(gi + 1) * 64, pr, :],
                scalar1=-1.0, scalar2=None, op0=TT.mult), reads=[cimb], writes=[lCb[pr]])

    NT = SEQ // 512
    for ct in range(2):
        for q4 in range(4):
            pr = ct * 4 + q4
            for tt in range(NT):
                ts_ = slice(tt * 512, (tt + 1) * 512)
                for ri, (dst, dstb) in enumerate(((R[0], Rb[0]), (I[0], Ib[0]))):
                    pt, pb = psrot.next()
                    mm(s, pt[:, :], lB[:, pr, ri, :], ubf[:, ct, ts_], True, True, reads=[lBb[pr], ubfb[ct]], writes=[pb])
                    s.op("act", lambda e, pt=pt, dst=dst, ts_=ts_: e.activation(out=dst[:, ts_], in_=pt[:, :], func=AF.Copy),
                         reads=[pb], writes=[dstb])
            cur = 0
            for k in range(NSTEP):
                d = 1 << k
                nx = 1 - cur
                R0, I0, R1, I1 = R[cur], I[cur], R[nx], I[nx]
                pre_, pim_, pnim_ = pw[:, k, 0, pr:pr + 1], pw[:, k, 1, pr:pr + 1], pw[:, k, 2, pr:pr + 1]
                s.op("act", lambda e, R0=R0, R1=R1, d=d: e.activation(out=R1[:, 0:d], in_=R0[:, 0:d], func=AF.Copy),
                     reads=[Rb[cur]], writes=[Rb[nx]])
                s.op("act", lambda e, I0=I0, I1=I1, d=d: e.activation(out=I1[:, 0:d], in_=I0[:, 0:d], func=AF.Copy),
                     reads=[Ib[cur]], writes=[Ib[nx]])
                s.op("dve", lambda e, R0=R0, R1=R1, d=d, sc=pre_: e.scalar_tensor_tensor(
                    out=R1[:, d:SEQ], in0=R0[:, 0:SEQ - d], scalar=sc, in1=R0[:, d:SEQ], op0=TT.mult, op1=TT.add),
                    reads=[Rb[cur], pwb[k]], writes=[Rb[nx]])
                s.op("dve", lambda e, I0=I0, R1=R1, d=d, sc=pnim_: e.scalar_tensor_tensor(
                    out=R1[:, d:SEQ], in0=I0[:, 0:SEQ - d], scalar=sc, in1=R1[:, d:SEQ], op0=TT.mult, op1=TT.add),
                    reads=[Ib[cur], Rb[nx], pwb[k]], writes=[Rb[nx]])
                s.op("dve", lambda e, I0=I0, I1=I1, d=d, sc=pre_: e.scalar_tensor_tensor(
                    out=I1[:, d:SEQ], in0=I0[:, 0:SEQ - d], scalar=sc, in1=I0[:, d:SEQ], op0=TT.mult, op1=TT.add),
                    reads=[Ib[cur], pwb[k]], writes=[Ib[nx]])
                s.op("dve", lambda e, R0=R0, I1=I1, d=d, sc=pim_: e.scalar_tensor_tensor(
                    out=I1[:, d:SEQ], in0=R0[:, 0:SEQ - d], scalar=sc, in1=I1[:, d:SEQ], op0=TT.mult, op1=TT.add),
                    reads=[Rb[cur], Ib[nx], pwb[k]], writes=[Ib[nx]])
                cur = nx
            for tt in range(NT):
                ts_ = slice(tt * 512, (tt + 1) * 512)
                pt, pb = psrot.next()
                mm(s, pt[:, :], lC[:, pr, 0, :], R[cur][:, ts_], True, False, reads=[lCb[pr], Rb[cur]], writes=[pb])
                mm(s, pt[:, :], lC[:, pr, 1, :], I[cur][:, ts_], False, True, reads=[lCb[pr], Ib[cur]], writes=[pb])
                if q4 == 0:
                    s.op("act", lambda e, pt=pt, ts_=ts_: e.activation(out=yacc[:, ts_], in_=pt[:, :], func=AF.Copy),
                         reads=[pb], writes=[yab])
                else:
                    s.op("dve", lambda e, pt=pt, ts_=ts_: e.tensor_tensor(out=yacc[:, ts_], in0=yacc[:, ts_], in1=pt[:, :], op=TT.add),
                         reads=[pb, yab], writes=[yab])
        s.op("dve", lambda e, ct=ct: e.scalar_tensor_tensor(out=yacc[:, :], in0=u[:, ct, :], scalar=dd[:, ct:ct + 1], in1=yacc[:, :],
                                                          op0=TT.mult, op1=TT.add), reads=[ub[ct], ddb, yab], writes=[yab])
        s.op("act", lambda e, ct=ct: e.activation(out=yo[:, ct, :], in_=yacc[:, :], func=AF.Gelu), reads=[yab], writes=[yob[ct]])
    ov = out_d.rearrange("(a p) t -> p a t", p=128)
    for ct in range(2):
        s.dma("sp", ov[:, ct, :], yo[:, ct, :], reads=[yob[ct]])
    return c.finish()


_NC_CACHE = {}


def _get(name, fn):
    if name not in _NC_CACHE:
        _NC_CACHE[name] = fn()
    return _NC_CACHE[name]


def _col(v, n):
    return np.ascontiguousarray(np.asarray(v, np.float32).reshape(n, 128).T)


def _halo(a, t0):
    out = np.zeros((NT2, a.shape[1]), a.dtype)
    lo = max(t0 - 2, 0)
    out[2 - (t0 - lo):] = a[lo:t0 + NTOK]
    return out


def _b1_inputs(proj, hd, P, l):
    xm = proj[:, hd * 256:(hd + 1) * 256]
    v = proj[:, 1024 + hd * 256:1024 + (hd + 1) * 256]
    o = proj[:, 2048 + hd * 256:2048 + (hd + 1) * 256]
    ii = proj[:, 3072 + hd]
    ff = proj[:, 3076 + hd]
    xmT = np.zeros((256, SEQ + 3), np.float32)
    xmT[:, 3:] = xm.T
    sl = slice(hd * 256, (hd + 1) * 256)
    return {"xmT": xmT, "v": np.ascontiguousarray(v), "oT": np.ascontiguousarray(o.T),
            "icol": np.ascontiguousarray(ii.reshape(NCH, 128).T), "fcol": np.ascontiguousarray(ff.reshape(NCH, 128).T),
            "cw": np.ascontiguousarray(P["m_conv_w"][l][:, sl].reshape(4, 2, 128).transpose(2, 1, 0)),
            "cb": _col(P["m_conv_b"][l][sl], 2),
            "wq": np.ascontiguousarray(P["w_q"][l][hd]), "wk": np.ascontiguousarray(P["w_k"][l][hd]),
            "ib": np.full((128, 1), P["m_i_bias"][l][hd], np.float32),
            "fb": np.full((128, 1), P["m_f_bias"][l][hd], np.float32),
            "hg": _col(P["m_head_g"][l][sl], 2), "sk": _col(P["m_skip"][l][sl], 2)}


def _b2_inputs(proj, hd, P, l):
    u = proj[:, 3080 + hd * 256:3080 + (hd + 1) * 256]
    g0 = hd * 16

    def lane(a):
        return np.ascontiguousarray(a.reshape(8, 2, 64).transpose(1, 2, 0).reshape(128, 8))

    def lane3(a):
        return np.ascontiguousarray(a.reshape(8, 2, 64, 16).transpose(1, 2, 0, 3).reshape(128, 8, 16))

    return {"uT": np.ascontiguousarray(u.T),
            "a_re": lane(P["s5_a_re"][l][g0:g0 + 16]), "a_im": lane(P["s5_a_im"][l][g0:g0 + 16]),
            "log_dt": lane(np.repeat(P["s5_log_dt"][l][g0:g0 + 16][:, None], 64, 1)),
            "b_re": lane3(P["s5_b_re"][l][g0:g0 + 16]), "b_im": lane3(P["s5_b_im"][l][g0:g0 + 16]),
            "c_re": lane3(P["s5_c_re"][l][g0:g0 + 16].transpose(0, 2, 1)),
            "c_im": lane3(P["s5_c_im"][l][g0:g0 + 16].transpose(0, 2, 1)),
            "dd": _col(P["s5_d"][l][hd * 256:(hd + 1) * 256], 2)}


def kernel(**inputs):
    P = {k: np.asarray(v) for k, v in inputs.items()}
    x = np.asarray(P["x"], np.float32)
    cores = list(range(NCORES))
    ncA = _get("A", build_A)
    ncB1 = _get("B1", build_B1)
    ncB2 = _get("B2", build_B2)
    for l in range(DEPTH):
        g = _col(P["norm_mix_g"][l], KC)
        w_in = np.ascontiguousarray(P["w_in"][l], dtype=np.float32)
        maps = []
        for cid in cores:
            b, sg = divmod(cid, 4)
            maps.append({"xT": np.ascontiguousarray(x[b, sg * NTOK:(sg + 1) * NTOK].T), "g": g, "w": w_in})
        res = run_bass_kernel_spmd(ncA, maps, core_ids=cores)
        proj = np.empty((BATCH, SEQ, IN_COLS), np.float32)
        for cid in cores:
            b, sg = divmod(cid, 4)
            proj[b, sg * NTOK:(sg + 1) * NTOK] = res.results[cid]["projT"].T
        res1 = run_bass_kernel_spmd(ncB1, [_b1_inputs(proj[cid // 4], cid % 4, P, l) for cid in cores], core_ids=cores)
        res2 = run_bass_kernel_spmd(ncB2, [_b2_inputs(proj[cid // 4], cid % 4, P, l) for cid in cores], core_ids=cores)
        mix = np.empty((BATCH, SEQ, D_MODEL), ml_dtypes.bfloat16)
        for cid in cores:
            b, hd = divmod(cid, 4)
            mix[b, :, hd * 256:(hd + 1) * 256] = res1.results[cid]["moT"].T
            mix[b, :, 1024 + hd * 256:1024 + (hd + 1) * 256] = res2.results[cid]["yT"].T
        final = (l == DEPTH - 1)
        ncC = _get("C1" if final else "C0", lambda: build_C(final))
        common = {"w_glu": np.ascontiguousarray(P["s5_w_glu"][l], dtype=np.float32), "b_glu": _col(P["s5_b_glu"][l], 8),
                  "w_out": np.ascontiguousarray(P["w_out"][l], dtype=np.float32), "g": _col(P["norm_ffn_g"][l], KC),
                  "w_gate": np.ascontiguousarray(P["w_gate"][l], dtype=np.float32),
                  "w_val": np.ascontiguousarray(P["w_val"][l], dtype=np.float32),
                  "cw": np.ascontiguousarray(np.asarray(P["f_conv_w"][l], np.float32).reshape(3, FC_TOT, 128).transpose(2, 1, 0)),
                  "cb": _col(P["f_conv_b"][l], FC_TOT),
                  "w_down": np.ascontiguousarray(P["w_down"][l], dtype=np.float32),
                  "gf": _col(P["norm_final_g"], KC)}
        maps = []
        for cid in cores:
            b, sg = divmod(cid, 4)
            m = dict(common)
            m["xT"] = np.ascontiguousarray(_halo(x[b], sg * NTOK).T)
            m["mixT"] = np.ascontiguousarray(_halo(mix[b], sg * NTOK).T)
            maps.append(m)
        res = run_bass_kernel_spmd(ncC, maps, core_ids=cores)
        xn = np.empty_like(x)
        for cid in cores:
            b, sg = divmod(cid, 4)
            xn[b, sg * NTOK:(sg + 1) * NTOK] = res.results[cid]["xoT"].T
        x = xn
    return x
```

```python
import contextlib
import numpy as np
import ml_dtypes
import concourse.bass as bass
import concourse.mybir as mybir
from concourse.bass_utils import run_bass_kernel_spmd

F32 = mybir.dt.float32
BF16 = mybir.dt.bfloat16
I32 = mybir.dt.int32
AF = mybir.ActivationFunctionType
ALU = mybir.AluOpType

D_MODEL = 2048
BATCH = 2
SEQ = 4096
DEPTH = 4
D_FF = 5504
IN_COLS = 4104
EPS = 1e-6
NCORES = 8
NTOK = 1024
HALO = 8
NTH = NTOK + HALO
KC = D_MODEL // 128
FC_TOT = D_FF // 128
NPASS = 4
NFP = (FC_TOT + NPASS - 1) // NPASS
LCH = 128
NCH = SEQ // LCH
DV = 256
DK = 128
NPAIR = 8
NSTEP = 12
PI = float(np.pi)
GROUPS = [[0, 1, 2, 3], [4, 5, 6, 7]]


class Buf:
    __slots__ = ("w", "r", "name")
    registry = []
    fence = None

    def __init__(self, name=""):
        self.w = Buf.fence
        self.r = []
        self.name = name
        Buf.registry.append(self)


def bufs(n, name=""):
    return [Buf(f"{name}{i}") for i in range(n)]


class BufSet:
    def __init__(self):
        self.all = []

    def new(self):
        b = Buf()
        self.all.append(b)
        return b


class Sched:
    ENG = ("pe", "act", "dve", "pool", "sp")
    NDMA_SEM = 8

    def __init__(self, nc, st):
        self.nc = nc
        self.ops = []
        self.done = 0
        self.esem = {e: st.enter_context(nc.semaphore(f"s_{e}")) for e in self.ENG}
        self.dsem = {e: [st.enter_context(nc.semaphore(f"d_{e}{k}")) for k in range(self.NDMA_SEM)]
                     for e in ("sp", "pool", "act")}
        self.csem = st.enter_context(nc.semaphore("s_cc"))
        self.ecount = {e: 0 for e in self.ENG}
        self.dcount = {e: 0 for e in self.dsem}
        self.ccount = 0
        self.tok = []
        self.seen = {e: {} for e in self.ENG}

    def op(self, eng, fn, reads=(), writes=(), kind=0, force=False):
        idx = len(self.ops)
        deps = set()
        for b in reads:
            if b.w is not None:
                deps.add(b.w)
        for b in writes:
            if b.w is not None:
                deps.add(b.w)
            deps.update(b.r)
        deps.discard(idx)
        for b in reads:
            b.r.append(idx)
        for b in writes:
            b.w = idx
            b.r = []
        self.ops.append((eng, fn, deps, kind, force))
        return idx

    def dma(self, eng, out, in_, reads=(), writes=(), **kw):
        return self.op(eng, lambda e: e.dma_start(out=out, in_=in_, **kw), reads, writes, kind=1)

    def fence(self, tiny):
        idx = self.op("dve", lambda e: e.memset(tiny[:, :], 0.0), writes=list(Buf.registry), force=True)
        Buf.fence = idx
        return idx

    def emit(self):
        nc = self.nc
        engs = {"pe": nc.tensor, "act": nc.scalar, "dve": nc.vector, "pool": nc.gpsimd, "sp": nc.sync}
        ops = self.ops
        n = len(ops)
        lo = self.done
        needed = {}
        for i in range(lo, n):
            e, fn, deps, kind, force = ops[i]
            if force:
                needed[i] = True
            for d in deps:
                if d < lo:
                    continue
                pe_, _, _, pk, _ = ops[d]
                if pk or pe_ != e or kind:
                    needed[d] = True
                elif e != "pe":
                    needed[d] = True
        tok = self.tok
        tok.extend([None] * (n - len(tok)))
        seen = self.seen
        for i in range(lo, n):
            e, fn, deps, kind, force = ops[i]
            E = engs[e]
            waits = {}
            for d in deps:
                t = tok[d]
                if t is None:
                    continue
                key = t[:-1]
                if t[-1] > waits.get(key, 0):
                    waits[key] = t[-1]
            if kind == 1:
                k = self.dcount[e] % self.NDMA_SEM
                rnd = self.dcount[e] // self.NDMA_SEM
                if rnd > 0:
                    key = ("d", e, k)
                    waits[key] = max(waits.get(key, 0), 16 * rnd)
            for key, val in waits.items():
                if seen[e].get(key, 0) >= val:
                    continue
                seen[e][key] = val
                if key[0] == "e":
                    sem = self.esem[key[1]]
                elif key[0] == "d":
                    sem = self.dsem[key[1]][key[2]]
                else:
                    sem = self.csem
                E.wait_ge(sem, val)
            ins = fn(E)
            if kind == 1:
                k = self.dcount[e] % self.NDMA_SEM
                rnd = self.dcount[e] // self.NDMA_SEM
                ins.then_inc(self.dsem[e][k], 16)
                tok[i] = ("d", e, k, 16 * (rnd + 1))
                self.dcount[e] += 1
            elif kind == 2:
                self.ccount += 1
                ins.then_inc(self.csem, 1)
                tok[i] = ("c", self.ccount)
            elif needed.get(i):
                self.ecount[e] += 1
                ins.then_inc(self.esem[e], 1)
                tok[i] = ("e", e, self.ecount[e])
        self.done = n

    def finalize(self):
        E = self.nc.sync
        for e in self.dsem:
            for k in range(self.NDMA_SEM):
                cnt = (self.dcount[e] - k + self.NDMA_SEM - 1) // self.NDMA_SEM
                if cnt > 0 and self.seen["sp"].get(("d", e, k), 0) < 16 * cnt:
                    E.wait_ge(self.dsem[e][k], 16 * cnt)


class Ctx:
    def __init__(self):
        Buf.registry = []
        Buf.fence = None
        self.nc = bass.Bass("TRN2", target_bir_lowering=False, num_devices=NCORES)
        self.st = contextlib.ExitStack()
        self.s = Sched(self.nc, self.st)
        self.scope = self.st
        self._n = 0

    def sb(self, shape, dt, name=None):
        self._n += 1
        return self.scope.enter_context(self.nc.sbuf_tensor(name or f"sb{self._n}", list(shape), dt))

    def ps(self, shape, dt=F32):
        self._n += 1
        return self.st.enter_context(self.nc.psum_tensor(f"ps{self._n}", list(shape), dt))

    def dram(self, name, shape, dt, kind="ExternalInput"):
        return self.nc.dram_tensor(name, list(shape), dt, kind=kind).ap()

    @contextlib.contextmanager
    def phase(self):
        old = self.scope
        with contextlib.ExitStack() as ps_:
            self.scope = ps_
            yield
            self.s.fence(self.tiny)
            self.s.emit()
        self.scope = old


class Rot:
    def __init__(self, items):
        self.items = items
        self.i = 0

    def next(self):
        it = self.items[self.i % len(self.items)]
        self.i += 1
        return it


def mm(s, out, lhsT, rhs, start, stop, reads, writes):
    s.op("pe", lambda e, o=out, l=lhsT, r=rhs, a=start, b=stop: e.matmul(o, l, r, start=a, stop=b),
         reads=reads, writes=writes)


def load_w(s, wrot, w_dram, kcs, f0, fw, k0=0):
    wt, wb = wrot.next()
    src = w_dram[k0 * 128:(k0 + kcs) * 128, f0:f0 + fw].rearrange("(kc p) f -> p kc f", p=128)
    s.dma("pool", wt[:, 0:kcs, 0:fw], src, writes=[wb])
    return wt, wb


def rmsnorm_T(c, K, xT, xb, c0, ntok, tiles, gcol, gb, hT, hb, sq, sqb, pstile, rstd, rstdb):
    s = c.s
    for kc in range(KC):
        s.op("act", lambda e, kc=kc: e.activation(out=sq[:, kc, 0:ntok], in_=xT[:, kc, c0:c0 + ntok], func=AF.Square),
             reads=[xb[kc]], writes=[sqb[kc]])
    for (t0, tn) in tiles:
        pt, pb = pstile(tn)
        for kc in range(KC):
            mm(s, pt[:, 0:tn], K.ones[:, :], sq[:, kc, t0:t0 + tn], kc == 0, kc == KC - 1,
               reads=[sqb[kc], K.b], writes=[pb])
        s.op("act", lambda e, pt=pt, t0=t0, tn=tn: e.activation(
            out=rstd[:, t0:t0 + tn], in_=pt[:, 0:tn], func=AF.Sqrt, bias=K.eps[:, 0:1]), reads=[pb, K.b], writes=[rstdb])
        s.op("dve", lambda e, t0=t0, tn=tn: e.reciprocal(out=rstd[:, t0:t0 + tn], in_=rstd[:, t0:t0 + tn]),
             reads=[rstdb], writes=[rstdb])
    for kc in range(KC):
        s.op("dve", lambda e, kc=kc: e.scalar_tensor_tensor(
            out=hT[:, kc, 0:ntok], in0=xT[:, kc, c0:c0 + ntok], scalar=gcol[:, kc:kc + 1], in1=rstd[:, 0:ntok],
            op0=ALU.mult, op1=ALU.mult), reads=[xb[kc], rstdb, gb], writes=[hb[kc]])


class Consts:
    pass


def phase_A(c, K, P, l, xT, xb, S1, S1b, S1g, S1gb):
    s = c.s
    with c.phase():
        hT = c.sb([128, KC, NTOK], BF16); hb = bufs(KC, "h")
        sq = c.sb([128, KC, NTOK], BF16); sqb = bufs(KC, "sq")
        rstd = c.sb([128, NTOK], F32); rstdb = Buf("rstd")
        gcol = c.sb([128, KC], F32); gb = Buf("g")
        psrot = Rot([(K.banks[i], Buf(f"psA{i}")) for i in range(8)])
        wrot = Rot([(c.sb([128, KC, 256], BF16), Buf(f"w{i}")) for i in range(3)])
        orot = Rot([(c.sb([128, NTOK], BF16), bufs(2, f"o{i}_")) for i in range(3)])
        vrot = Rot([(c.sb([128, 8, 256], BF16), bufs(8, f"v{i}_")) for i in range(2)])
        gt = c.sb([8, NTOK], F32); gtb = bufs(2, "gt")
        s.dma("sp", gcol[:, :], P["gmix"][l], writes=[gb])
        tiles = [(0, 512), (512, 512)]
        rmsnorm_T(c, K, xT, xb, HALO, NTOK, tiles, gcol, gb, hT, hb, sq, sqb, lambda tn: psrot.next(), rstd, rstdb)
        w = P["w_in"][l]
        jobs = []
        for k in range(8):
            jobs.append((k * 128, 128, ("S1", (k // 2) * 1024 + (k % 2) * 128)))
        for k in range(8):
            jobs.append((2048 + k * 128, 128, ("S1", (k // 2) * 1024 + 512 + (k % 2) * 128)))
        for k in range(8):
            jobs.append((3080 + k * 128, 128, ("S1", (k // 2) * 1024 + 768 + (k % 2) * 128)))
        jobs.append((3072, 8, ("G", 0)))
        for (f0, fw, (kind, r0)) in jobs:
            wt, wb = load_w(s, wrot, w, KC, f0, fw)
            if kind == "S1":
                ot, ob = orot.next()
            else:
                ot, ob = gt, gtb
            for ti, (t0, tn) in enumerate(tiles):
                pt, pb = psrot.next()
                for kc in range(KC):
                    mm(s, pt[0:fw, 0:tn], wt[:, kc, 0:fw], hT[:, kc, t0:t0 + tn], kc == 0, kc == KC - 1,
                       reads=[wb, hb[kc]], writes=[pb])
                if ti == 0:
                    s.op("act", lambda e, pt=pt, ot=ot, t0=t0, tn=tn, fw=fw: e.activation(
                        out=ot[0:fw, t0:t0 + tn], in_=pt[0:fw, 0:tn], func=AF.Copy), reads=[pb], writes=[ob[ti]])
                else:
                    s.op("dve", lambda e, pt=pt, ot=ot, t0=t0, tn=tn, fw=fw: e.tensor_copy(
                        out=ot[0:fw, t0:t0 + tn], in_=pt[0:fw, 0:tn]), reads=[pb], writes=[ob[ti]])
            if kind == "S1":
                s.dma("sp", S1[r0:r0 + fw, :], ot[0:fw, :], reads=ob, writes=[S1b[r0 // 512].new()])
            else:
                s.dma("sp", S1g[:, :], ot[0:fw, :], reads=ob, writes=[S1gb.new()])
        for vc in range(4):
            wt, wb = load_w(s, wrot, w, KC, 1024 + vc * 256, 256)
            vt, vtb = vrot.next()
            for tc in range(8):
                pt, pb = psrot.next()
                for kc in range(KC):
                    mm(s, pt[:, 0:256], hT[:, kc, tc * 128:(tc + 1) * 128], wt[:, kc, :], kc == 0, kc == KC - 1,
                       reads=[wb, hb[kc]], writes=[pb])
                if tc % 2 == 0:
                    s.op("act", lambda e, pt=pt, vt=vt, tc=tc: e.activation(out=vt[:, tc, :], in_=pt[:, 0:256], func=AF.Copy),
                         reads=[pb], writes=[vtb[tc]])
                else:
                    s.op("dve", lambda e, pt=pt, vt=vt, tc=tc: e.tensor_copy(out=vt[:, tc, :], in_=pt[:, 0:256]),
                         reads=[pb], writes=[vtb[tc]])
            dst = S1[vc * 1024 + 256:vc * 1024 + 512, :].rearrange("r (t4 f) -> (r t4) f", t4=4, f=256).rearrange("(tc p) f -> p tc f", p=128)
            s.dma("sp", dst, vt[:, :, :], reads=vtb, writes=[S1b[2 * vc].new()])


def allgather(c, src, srcb, dst, dstb):
    c.s.op("pool", lambda e: e.collective_compute("AllGather", ALU.bypass, ins=[src], outs=[dst],
                                                   replica_groups=GROUPS, unique_tensors="Yes"),
           reads=list(srcb.all), writes=[dstb], kind=2)


def phase_B1(c, K, P, l, H1, H1b, Hg, Hgb, S2, S2b):
    s = c.s
    with c.phase():
        xm = c.sb([128, 2, SEQ + 3], BF16); xmb = bufs(2, "xm")
        acc = c.sb([128, SEQ], F32); accb = Buf("acc")
        cbf = c.sb([128, 2, SEQ], BF16); cbfb = bufs(2, "cbf")
        csk = c.sb([128, 2, SEQ], BF16); cskb = bufs(2, "csk")
        vaug = c.sb([128, NCH, DV + 1], BF16); vb = Buf("v")
        qT = c.sb([128, SEQ], BF16); qb = Buf("q")
        kT = c.sb([128, SEQ], BF16); kb = Buf("k")
        kt = c.sb([128, NCH, DK], BF16); ktb = bufs(NCH, "kt")
        cw = c.sb([128, 2, 4], F32); cwb = Buf()
        cb = c.sb([128, 2], F32); cbb = Buf()
        wq = c.sb([128, 2, DK], BF16); wqb = Buf()
        wk = c.sb([128, 2, DK], BF16); wkb = Buf()
        ibc = c.sb([128, 1], F32); ibb = Buf()
        fbc = c.sb([128, 1], F32); fbb = Buf()
        hg = c.sb([128, 2], F32); hgb = Buf()
        sk = c.sb([128, 2], F32); skb = Buf()
        icol = c.sb([128, NCH], F32); icb = Buf()
        fcol = c.sb([128, NCH], F32); fcb = Buf()
        lf = c.sb([128, NCH], F32); lfb_ = Buf()
        bcol = c.sb([128, NCH], F32); bcb = Buf()
        gbc = c.sb([128, NCH], F32); gbcb = Buf()
        acol = c.sb([128, NCH], F32); acb = Buf()
        wkc = c.sb([128, NCH], F32); wkcb = Buf()
        egc = c.sb([128, NCH], F32); egb = Buf()
        Cf = c.sb([128, DV + 1], F32); Cfb = Buf("Cf")
        Cb = c.sb([128, DV + 1], BF16); Cbb = Buf("Cb")
        zt = c.sb([128, HALO], BF16); ztb = Buf()
        banks = K.banks
        bankb = bufs(8, "bank")
        tri, negm, ident, onesf = K.tri, K.negm, K.ident, K.onesf
        KB = K.b

        def dyn_dma(out, mk_in, reads, writes, **kw):
            def f(e):
                j = K.jv
                K.ndyn = getattr(K, "ndyn", 0) + 1
                try:
                    return e.dma_start(out=out, in_=mk_in(j), **kw)
                except Exception:
                    print("DYN FAIL at", K.ndyn, out.shape, kw)
                    raise
            s.op("sp", f, reads=reads, writes=writes, kind=1)

        for (t, name, b) in ((cb, "m_cb", cbb), (ibc, "m_ib", ibb), (fbc, "m_fb", fbb), (hg, "m_hg", hgb), (sk, "m_sk", skb)):
            s.dma("sp", t[:, :], P[name][l], writes=[b])
        s.dma("sp", cw[:, :, :], P["m_cw"][l], writes=[cwb])
        s.dma("pool", wq[:, :, :], P["m_wq"][l].rearrange("(a p) d -> p a d", p=128), writes=[wqb])
        s.dma("pool", wk[:, :, :], P["m_wk"][l].rearrange("(a p) d -> p a d", p=128), writes=[wkb])
        for ft in range(2):
            s.op("dve", lambda e, ft=ft: e.memset(xm[:, ft, 0:3], 0.0), writes=[xmb[ft]])
            for sg in range(4):
                s.dma("sp", xm[:, ft, 3 + sg * NTOK:3 + (sg + 1) * NTOK], H1[sg, ft * 128:(ft + 1) * 128, :],
                      reads=[H1b], writes=[xmb[ft]])
        for sg in range(4):
            src = H1[sg, 256:512, :].rearrange("r (t4 f) -> (r t4) f", t4=4, f=256).rearrange("(cc p) f -> p cc f", p=128)
            s.dma("sp", vaug[:, sg * 8:(sg + 1) * 8, 0:DV], src, reads=[H1b], writes=[vb])
        for (dst, dstb, ri) in ((icol, icb, 0), (fcol, fcb, 1)):
            for sg in range(4):
                s.dma("sp", dst[:, sg * 8:(sg + 1) * 8], Hg[sg, ri:ri + 1, :].rearrange("o (cc p) -> p (o cc)", p=128),
                      reads=[Hgb], writes=[dstb], allow_slow_non_contiguous=True)
        s.op("dve", lambda e: e.memset(vaug[:, :, DV:DV + 1], 1.0), writes=[vb])
        s.op("dve", lambda e: e.memset(Cf[:, :], 0.0), writes=[Cfb])
        s.op("dve", lambda e: e.memset(Cb[:, :], 0.0), writes=[Cbb])

        s.op("dve", lambda e: e.tensor_scalar(out=fbc[:, :], in0=fbc[:, :], scalar1=-1.0, scalar2=None, op0=ALU.mult),
             reads=[fbb], writes=[fbb])
        s.op("act", lambda e: e.activation(out=lf[:, :], in_=fcol[:, :], func=AF.Exp, scale=-1.0, bias=fbc[:, 0:1]),
             reads=[fcb, fbb], writes=[lfb_])
        s.op("act", lambda e: e.activation(out=lf[:, :], in_=lf[:, :], func=AF.Ln, bias=K.one1[:, 0:1]),
             reads=[lfb_, KB], writes=[lfb_])
        s.op("dve", lambda e: e.tensor_scalar(out=lf[:, :], in0=lf[:, :], scalar1=-1.0, scalar2=None, op0=ALU.mult),
             reads=[lfb_], writes=[lfb_])
        g0 = banks[1][:, 256:256 + NCH]
        g1 = banks[1][:, 320:320 + NCH]
        g0b, g1b = Buf(), Buf()
        mm(s, g0, tri[:, :], lf[:, :], True, True, reads=[KB, lfb_], writes=[g0b])
        mm(s, g1, onesf[:, :], lf[:, :], True, True, reads=[KB, lfb_], writes=[g1b])
        s.op("dve", lambda e: e.tensor_copy(out=bcol[:, :], in_=g0), reads=[g0b], writes=[bcb])
        s.op("dve", lambda e: e.tensor_copy(out=gbc[:, :], in_=g1), reads=[g1b], writes=[gbcb])
        s.op("dve", lambda e: e.scalar_tensor_tensor(out=acol[:, :], in0=icol[:, :], scalar=ibc[:, 0:1], in1=bcol[:, :],
                                                     op0=ALU.add, op1=ALU.subtract), reads=[icb, ibb, bcb], writes=[acb])
        s.op("dve", lambda e: e.tensor_tensor(out=wkc[:, :], in0=gbc[:, :], in1=acol[:, :], op=ALU.add),
             reads=[gbcb, acb], writes=[wkcb])
        s.op("act", lambda e: e.activation(out=wkc[:, :], in_=wkc[:, :], func=AF.Exp), reads=[wkcb], writes=[wkcb])
        s.op("act", lambda e: e.activation(out=egc[:, :], in_=gbc[:, :], func=AF.Exp), reads=[gbcb], writes=[egb])

        for ft in range(2):
            s.op("dve", lambda e, ft=ft: e.tensor_scalar(
                out=acc[:, :], in0=xm[:, ft, 3:SEQ + 3], scalar1=cw[:, ft, 3:4], scalar2=cb[:, ft:ft + 1],
                op0=ALU.mult, op1=ALU.add), reads=[xmb[ft], cwb, cbb], writes=[accb])
            for k in range(3):
                s.op("dve", lambda e, ft=ft, k=k: e.scalar_tensor_tensor(
                    out=acc[:, :], in0=xm[:, ft, k:SEQ + k], scalar=cw[:, ft, k:k + 1], in1=acc[:, :],
                    op0=ALU.mult, op1=ALU.add), reads=[xmb[ft], cwb, accb], writes=[accb])
            s.op("act", lambda e, ft=ft: e.activation(out=acc[:, :], in_=acc[:, :], func=AF.Silu), reads=[accb], writes=[accb])
            s.op("pool", lambda e, ft=ft: e.tensor_copy(out=cbf[:, ft, :], in_=acc[:, :]), reads=[accb], writes=[cbfb[ft]])
            s.op("pool", lambda e, ft=ft: e.tensor_scalar(out=csk[:, ft, :], in0=acc[:, :], scalar1=sk[:, ft:ft + 1],
                                                         scalar2=None, op0=ALU.mult), reads=[accb, skb], writes=[cskb[ft]])

        pre = Rot([(banks[i], bankb[i]) for i in (2, 3, 4, 5, 7)])
        for tt in range(SEQ // 512):
            for (wt, wtb, dst, dstb, scl) in ((wq, wqb, qT, qb, DK ** -0.5), (wk, wkb, kT, kb, 1.0)):
                pt, pb = pre.next()
                for ft in range(2):
                    mm(s, pt[:, :], wt[:, ft, :], cbf[:, ft, tt * 512:(tt + 1) * 512], ft == 0, ft == 1,
                       reads=[wtb, cbfb[ft]], writes=[pb])
                s.op("act", lambda e, pt=pt, dst=dst, tt=tt, scl=scl: e.activation(
                    out=dst[:, tt * 512:(tt + 1) * 512], in_=pt[:, :], func=AF.Copy, scale=scl), reads=[pb], writes=[dstb])
            pt, pb = pre.next()
            for cc in range(4):
                ch = tt * 4 + cc
                for ft in range(2):
                    mm(s, pt[:, cc * 128:(cc + 1) * 128], cbf[:, ft, ch * 128:(ch + 1) * 128], wk[:, ft, :],
                       ft == 0, ft == 1, reads=[wkb, cbfb[ft]], writes=[pb])
            for cc in range(4):
                ch = tt * 4 + cc
                s.op("dve", lambda e, pt=pt, cc=cc, ch=ch: e.tensor_scalar(
                    out=kt[:, ch, :], in0=pt[:, cc * 128:(cc + 1) * 128], scalar1=wkc[:, ch:ch + 1], scalar2=None,
                    op0=ALU.mult), reads=[pb, wkcb], writes=[ktb[ch]])

        pAB = [(banks[0][:, 256 * i:256 * i + 128], banks[0][:, 256 * i + 128:256 * i + 256], Buf(), Buf()) for i in range(2)]
        pS = [(banks[1][:, 128 * i:128 * i + 128], Buf()) for i in range(2)]
        pN = [(banks[2], bankb[2]), (banks[3], bankb[3])]
        pC = [(banks[4], bankb[4]), (banks[5], bankb[5])]
        pT = [[(banks[6][:, (2 * i + ft) * 128:(2 * i + ft + 1) * 128], Buf()) for ft in range(2)] for i in range(2)]
        lfrot = Rot([(c.sb([128, 128], F32), Buf()) for _ in range(2)])
        ebrot = Rot([(c.sb([128, 128], F32), Buf()) for _ in range(2)])
        wrot_ = Rot([(c.sb([128, 128], F32), Buf()) for _ in range(2)])
        qtrot = Rot([(c.sb([128, 128], BF16), Buf()) for _ in range(2)])
        swrot = Rot([(c.sb([128, 128], BF16), Buf()) for _ in range(2)])
        hnrot = Rot([(c.sb([128, DV], F32), Buf()) for _ in range(2)])
        jkrot = Rot([(c.sb([128, DV], BF16), Buf()) for _ in range(2)])
        smrot = Rot([(c.sb([128, 8], F32), Buf()) for _ in range(2)])
        orot = Rot([(c.sb([128, 2, 128], BF16), Buf()) for _ in range(3)])
        sgrot = Rot([(c.sb([128, 2, 128], F32), Buf()) for _ in range(2)])
        t2rot = Rot([(c.sb([128, 128], F32), Buf()) for _ in range(2)])
        morot = Rot([(c.sb([128, 2, 512], BF16), Buf()) for _ in range(2)])
        mo, mob = None, None
        for ch in range(NCH):
            cs = slice(ch * 128, (ch + 1) * 128)
            sg, tl = divmod(ch * 128, NTOK)
            pA, pB, pAb, pBb = pAB[ch % 2]
            lfm, lfmb = lfrot.next()
            s.op("dve", lambda e, lfm=lfm, ch=ch: e.tensor_scalar(out=lfm[:, :], in0=onesf[:, :], scalar1=lf[:, ch:ch + 1],
                                                                scalar2=None, op0=ALU.mult), reads=[KB, lfb_], writes=[lfmb])
            mm(s, pA, lfm[:, :], tri[:, :], True, True, reads=[lfmb, KB], writes=[pAb])
            mm(s, pB, lfm[:, :], tri[:, :], True, False, reads=[lfmb, KB], writes=[pBb])
            mm(s, pB, ident[:, :], negm[:, :], False, True, reads=[KB], writes=[pBb])
            eb, ebb = ebrot.next()
            s.op("act", lambda e, eb=eb, pA=pA: e.activation(out=eb[:, :], in_=pA, func=AF.Exp), reads=[pAb], writes=[ebb])
            W, Wb = wrot_.next()
            s.op("act", lambda e, W=W, pB=pB, ch=ch: e.activation(out=W[:, :], in_=pB, func=AF.Exp, bias=acol[:, ch:ch + 1]),
                 reads=[pBb, acb], writes=[Wb])
            qt, qtb = qtrot.next()
            s.op("dve", lambda e, qt=qt, eb=eb, cs=cs: e.tensor_tensor(out=qt[:, :], in0=qT[:, cs], in1=eb[:, :], op=ALU.mult),
                 reads=[qb, ebb], writes=[qtb])
            ps_, psb = pS[ch % 2]
            mm(s, ps_, kT[:, cs], qT[:, cs], True, True, reads=[kb, qb], writes=[psb])
            sw, swb = swrot.next()
            s.op("dve", lambda e, sw=sw, ps_=ps_, W=W: e.tensor_tensor(out=sw[:, :], in0=ps_, in1=W[:, :], op=ALU.mult),
                 reads=[psb, Wb], writes=[swb])
            pn, pnb = pN[ch % 2]
            mm(s, pn[:, 0:DV + 1], sw[:, :], vaug[:, ch, :], True, False, reads=[swb, vb], writes=[pnb])
            mm(s, pn[:, 0:DV + 1], qt[:, :], Cb[:, :], False, True, reads=[qtb, Cbb], writes=[pnb])
            pc, pcb = pC[ch % 2]
            mm(s, pc[:, 0:DV + 1], kt[:, ch, :], vaug[:, ch, :], True, True, reads=[ktb[ch], vb], writes=[pcb])
            s.op("dve", lambda e, pc=pc, ch=ch: e.scalar_tensor_tensor(
                out=Cf[:, :], in0=Cf[:, :], scalar=egc[:, ch:ch + 1], in1=pc[:, 0:DV + 1], op0=ALU.mult, op1=ALU.add),
                reads=[Cfb, egb, pcb], writes=[Cfb])
            s.op("act", lambda e: e.activation(out=Cb[:, :], in_=Cf[:, :], func=AF.Copy), reads=[Cfb], writes=[Cbb])
            sm, smb = smrot.next()
            s.op("act", lambda e, sm=sm, pn=pn: e.activation(out=sm[:, 0:1], in_=pn[:, DV:DV + 1], func=AF.Abs),
                 reads=[pnb], writes=[smb])
            s.op("dve", lambda e, sm=sm: e.tensor_scalar(out=sm[:, 0:1], in0=sm[:, 0:1], scalar1=1.0, scalar2=None,
                                                        op0=ALU.max), reads=[smb], writes=[smb])
            s.op("dve", lambda e, sm=sm: e.reciprocal(out=sm[:, 0:1], in_=sm[:, 0:1]), reads=[smb], writes=[smb])
            jk, jkb = jkrot.next()
            s.op("act", lambda e, jk=jk, pn=pn, sm=sm: e.activation(out=jk[:, :], in_=pn[:, 0:DV], func=AF.Square,
                                                                  accum_out=sm[:, 1:2]), reads=[pnb, smb], writes=[jkb, smb])
            s.op("dve", lambda e, sm=sm: e.tensor_tensor(out=sm[:, 2:3], in0=sm[:, 1:2], in1=sm[:, 0:1], op=ALU.mult),
                 reads=[smb], writes=[smb])
            s.op("dve", lambda e, sm=sm: e.tensor_tensor(out=sm[:, 2:3], in0=sm[:, 2:3], in1=sm[:, 0:1], op=ALU.mult),
                 reads=[smb], writes=[smb])
            s.op("act", lambda e, sm=sm: e.activation(out=sm[:, 3:4], in_=sm[:, 2:3], func=AF.Sqrt, scale=1.0 / DV,
                                                    bias=K.eps[:, 0:1]), reads=[smb, KB], writes=[smb])
            s.op("dve", lambda e, sm=sm: e.reciprocal(out=sm[:, 3:4], in_=sm[:, 3:4]), reads=[smb], writes=[smb])
            s.op("dve", lambda e, sm=sm: e.tensor_tensor(out=sm[:, 4:5], in0=sm[:, 3:4], in1=sm[:, 0:1], op=ALU.mult),
                 reads=[smb], writes=[smb])
            hn, hnb = hnrot.next()
            s.op("act", lambda e, hn=hn, pn=pn, sm=sm: e.activation(out=hn[:, :], in_=pn[:, 0:DV], func=AF.Copy,
                                                                  scale=sm[:, 4:5]), reads=[pnb, smb], writes=[hnb])
            ot, otb = orot.next()
            s.dma("sp", ot[:, :, :], H1[sg, 512:768, tl:tl + 128].rearrange("(a p) t -> p a t", p=128), reads=[H1b], writes=[otb])
            so, sob = sgrot.next()
            s.op("act", lambda e, ot=ot, so=so: e.activation(out=so[:, :, :], in_=ot[:, :, :], func=AF.Sigmoid),
                 reads=[otb], writes=[sob])
            if ch % 4 == 0:
                mo, mob = morot.next()
            for ft in range(2):
                ptt, pttb = pT[ch % 2][ft]
                s.op("pe", lambda e, ptt=ptt, hn=hn, ft=ft: e.transpose(ptt, hn[:, ft * 128:(ft + 1) * 128], ident[:, :]),
                     reads=[hnb, KB], writes=[pttb])
                t2, t2b = t2rot.next()
                s.op("dve", lambda e, t2=t2, ptt=ptt, ft=ft, cs=cs: e.scalar_tensor_tensor(
                    out=t2[:, :], in0=ptt, scalar=hg[:, ft:ft + 1], in1=csk[:, ft, cs], op0=ALU.mult, op1=ALU.add),
                    reads=[pttb, hgb, cskb[ft]], writes=[t2b])
                s.op("dve", lambda e, t2=t2, so=so, ft=ft, mo=mo, ch=ch: e.tensor_tensor(
                    out=mo[:, ft, (ch % 4) * 128:(ch % 4 + 1) * 128], in0=t2[:, :], in1=so[:, ft, :], op=ALU.mult),
                    reads=[t2b, sob], writes=[mob])
            if ch % 4 == 3:
                c0 = (ch - 3) * 128
                for ft in range(2):
                    s.dma("sp", S2[ft * 128:(ft + 1) * 128, c0:c0 + 512], mo[:, ft, :], reads=[mob], writes=[S2b[ft].new()])


def phase_B2(c, K, P, l, H1, H1b, S2, S2b):
    s = c.s
    TT = ALU
    with c.phase():
        u = c.sb([128, 2, SEQ], BF16); ub = bufs(2, "u")
        R = [c.sb([128, SEQ], F32) for _ in range(2)]; Rb = bufs(2, "R")
        I = [c.sb([128, SEQ], F32) for _ in range(2)]; Ib = bufs(2, "I")
        yacc = c.sb([128, SEQ], F32); yab = Buf("yacc")
        yo = c.sb([128, SEQ], BF16); yob = Buf("yo")
        Pm = lambda n: (c.sb([128, NPAIR], F32), Buf(n))
        are, areb = Pm("are"); aim, aimb = Pm("aim"); dt, dtb = Pm("dt")
        zr, zrb = Pm("zr"); zi, zib = Pm("zi"); mag, magb = Pm("mag")
        sn, snb = Pm("sn"); cs_, csb = Pm("cs"); t0_, t0b = Pm("t0"); t1_, t1b = Pm("t1")
        ar, arb = Pm("ar"); ai, aib = Pm("ai"); cr, crb = Pm("cr"); ci, cib = Pm("ci")
        nf, nfb = Pm("nf")
        ni = c.sb([128, NPAIR], I32); nib = Buf()
        pw = c.sb([128, NSTEP, 3, NPAIR], F32); pwb = bufs(NSTEP, "pw")
        bre = c.sb([128, NPAIR, 16], F32); breb = Buf()
        bim = c.sb([128, NPAIR, 16], F32); bimb = Buf()
        cre = c.sb([128, NPAIR, 16], F32); creb = Buf()
        cim = c.sb([128, NPAIR, 16], F32); cimb = Buf()
        bbr = c.sb([128, NPAIR, 16], F32); bbrb = Buf()
        bbi = c.sb([128, NPAIR, 16], F32); bbib = Buf()
        tmpb3 = c.sb([128, NPAIR, 16], F32); tmpb3b = Buf()
        dd = c.sb([128, 2], F32); ddb = Buf()
        pad = c.sb([128, 2, 128], F32); padb = bufs(2, "pad")
        lB = c.sb([128, NPAIR, 2, 128], BF16); lBb = bufs(NPAIR, "lB")
        lC = c.sb([128, NPAIR, 2, 128], F32); lCb = bufs(NPAIR, "lC")
        ident = K.ident
        KB = K.b
        psrot = Rot([(K.banks[i], Buf(f"psB{i}")) for i in range(8)])

        for (t, name, b) in ((are, "s_are", areb), (aim, "s_aim", aimb), (dt, "s_ldt", dtb), (dd, "s_dd", ddb)):
            s.dma("sp", t[:, :], P[name][l], writes=[b])
        for (t, name, b) in ((bre, "s_bre", breb), (bim, "s_bim", bimb), (cre, "s_cre", creb), (cim, "s_cim", cimb)):
            s.dma("sp", t[:, :, :], P[name][l], writes=[b])

        def dyn_dma(out, mk_in, reads, writes):
            def f(e):
                j = K.jv
                return e.dma_start(out=out, in_=mk_in(j))
            s.op("sp", f, reads=reads, writes=writes, kind=1)

        for ct in range(2):
            for sg in range(4):
                s.dma("sp", u[:, ct, sg * NTOK:(sg + 1) * NTOK], H1[sg, 768 + ct * 128:768 + (ct + 1) * 128, :],
                      reads=[H1b], writes=[ub[ct]])

        def ew(eng, fn, reads, writes):
            s.op(eng, fn, reads=reads, writes=writes)

        ew("act", lambda e: e.activation(out=dt[:, :], in_=dt[:, :], func=AF.Exp), [dtb], [dtb])
        ew("dve", lambda e: e.tensor_tensor(out=zr[:, :], in0=are[:, :], in1=dt[:, :], op=TT.mult), [areb, dtb], [zrb])
        ew("dve", lambda e: e.tensor_tensor(out=zi[:, :], in0=aim[:, :], in1=dt[:, :], op=TT.mult), [aimb, dtb], [zib])
        ew("act", lambda e: e.activation(out=mag[:, :], in_=zr[:, :], func=AF.Exp), [zrb], [magb])

        def sin_of(dst, dstb, shift):
            ew("dve", lambda e: e.tensor_scalar(out=t0_[:, :], in0=zi[:, :], scalar1=shift, scalar2=1.0 / (2 * PI),
                                                op0=TT.add, op1=TT.mult), [zib, t0b], [t0b])
            ew("dve", lambda e: e.tensor_copy(out=ni[:, :], in_=t0_[:, :]), [t0b], [nib])
            ew("dve", lambda e: e.tensor_copy(out=nf[:, :], in_=ni[:, :]), [nib], [nfb])
            ew("dve", lambda e: e.tensor_scalar(out=t1_[:, :], in0=zi[:, :], scalar1=shift, scalar2=None, op0=TT.add), [zib, t1b], [t1b])
            ew("dve", lambda e: e.scalar_tensor_tensor(out=t1_[:, :], in0=nf[:, :], scalar=-2 * PI, in1=t1_[:, :],
                                                       op0=TT.mult, op1=TT.add), [nfb, t1b], [t1b])
            ew("dve", lambda e: e.tensor_scalar(out=t0_[:, :], in0=t1_[:, :], scalar1=PI, scalar2=-2 * PI,
                                                op0=TT.is_gt, op1=TT.mult), [t1b, t0b], [t0b])
            ew("dve", lambda e: e.tensor_tensor(out=t1_[:, :], in0=t1_[:, :], in1=t0_[:, :], op=TT.add), [t1b, t0b], [t1b])
            ew("act", lambda e: e.activation(out=dst[:, :], in_=t1_[:, :], func=AF.Sin), [t1b], [dstb])

        sin_of(sn, snb, 0.0)
        sin_of(cs_, csb, 0.5 * PI)
        ew("dve", lambda e: e.tensor_tensor(out=ar[:, :], in0=mag[:, :], in1=cs_[:, :], op=TT.mult), [magb, csb], [arb])
        ew("dve", lambda e: e.tensor_tensor(out=ai[:, :], in0=mag[:, :], in1=sn[:, :], op=TT.mult), [magb, snb], [aib])
        ew("dve", lambda e: e.tensor_tensor(out=t0_[:, :], in0=are[:, :], in1=are[:, :], op=TT.mult), [areb, t0b], [t0b])
        ew("dve", lambda e: e.tensor_tensor(out=t1_[:, :], in0=aim[:, :], in1=aim[:, :], op=TT.mult), [aimb, t1b], [t1b])
        ew("dve", lambda e: e.tensor_tensor(out=t0_[:, :], in0=t0_[:, :], in1=t1_[:, :], op=TT.add), [t0b, t1b], [t0b])
        ew("dve", lambda e: e.reciprocal(out=t0_[:, :], in_=t0_[:, :]), [t0b], [t0b])
        ew("dve", lambda e: e.tensor_scalar(out=t1_[:, :], in0=ar[:, :], scalar1=-1.0, scalar2=None, op0=TT.add), [arb, t1b], [t1b])
        ew("dve", lambda e: e.tensor_tensor(out=cr[:, :], in0=t1_[:, :], in1=are[:, :], op=TT.mult), [t1b, areb], [crb])
        ew("dve", lambda e: e.tensor_tensor(out=zr[:, :], in0=ai[:, :], in1=aim[:, :], op=TT.mult), [aib, aimb, zrb], [zrb])
        ew("dve", lambda e: e.tensor_tensor(out=cr[:, :], in0=cr[:, :], in1=zr[:, :], op=TT.add), [crb, zrb], [crb])
        ew("dve", lambda e: e.tensor_tensor(out=cr[:, :], in0=cr[:, :], in1=t0_[:, :], op=TT.mult), [crb, t0b], [crb])
        ew("dve", lambda e: e.tensor_tensor(out=ci[:, :], in0=ai[:, :], in1=are[:, :], op=TT.mult), [aib, areb], [cib])
        ew("dve", lambda e: e.tensor_tensor(out=zr[:, :], in0=t1_[:, :], in1=aim[:, :], op=TT.mult), [t1b, aimb, zrb], [zrb])
        ew("dve", lambda e: e.tensor_tensor(out=ci[:, :], in0=ci[:, :], in1=zr[:, :], op=TT.subtract), [cib, zrb], [cib])
        ew("dve", lambda e: e.tensor_tensor(out=ci[:, :], in0=ci[:, :], in1=t0_[:, :], op=TT.mult), [cib, t0b], [cib])
        crB = cr[:, :].unsqueeze(2).to_broadcast([128, NPAIR, 16])
        ciB = ci[:, :].unsqueeze(2).to_broadcast([128, NPAIR, 16])
        ew("dve", lambda e: e.tensor_tensor(out=bbr[:, :, :], in0=bre[:, :, :], in1=crB, op=TT.mult), [breb, crb], [bbrb])
        ew("dve", lambda e: e.tensor_tensor(out=tmpb3[:, :, :], in0=bim[:, :, :], in1=ciB, op=TT.mult), [bimb, cib], [tmpb3b])
        ew("dve", lambda e: e.tensor_tensor(out=bbr[:, :, :], in0=bbr[:, :, :], in1=tmpb3[:, :, :], op=TT.subtract), [bbrb, tmpb3b], [bbrb])
        ew("dve", lambda e: e.tensor_tensor(out=bbi[:, :, :], in0=bim[:, :, :], in1=crB, op=TT.mult), [bimb, crb], [bbib])
        ew("dve", lambda e: e.tensor_tensor(out=tmpb3[:, :, :], in0=bre[:, :, :], in1=ciB, op=TT.mult), [breb, cib, bbrb], [tmpb3b])
        ew("dve", lambda e: e.tensor_tensor(out=bbi[:, :, :], in0=bbi[:, :, :], in1=tmpb3[:, :, :], op=TT.add), [bbib, tmpb3b], [bbib])
        ew("dve", lambda e: e.tensor_copy(out=pw[:, 0, 0, :], in_=ar[:, :]), [arb], [pwb[0]])
        ew("dve", lambda e: e.tensor_copy(out=pw[:, 0, 1, :], in_=ai[:, :]), [aib], [pwb[0]])
        for k in range(NSTEP):
            if k > 0:
                ew("dve", lambda e, k=k: e.tensor_tensor(out=t0_[:, :], in0=pw[:, k - 1, 0, :], in1=pw[:, k - 1, 0, :], op=TT.mult),
                   [pwb[k - 1], t0b], [t0b])
                ew("dve", lambda e, k=k: e.tensor_tensor(out=t1_[:, :], in0=pw[:, k - 1, 1, :], in1=pw[:, k - 1, 1, :], op=TT.mult),
                   [pwb[k - 1], t1b], [t1b])
                ew("dve", lambda e, k=k: e.tensor_tensor(out=pw[:, k, 0, :], in0=t0_[:, :], in1=t1_[:, :], op=TT.subtract),
                   [t0b, t1b], [pwb[k]])
                ew("dve", lambda e, k=k: e.scalar_tensor_tensor(out=pw[:, k, 1, :], in0=pw[:, k - 1, 0, :], scalar=2.0,
                                                               in1=pw[:, k - 1, 1, :], op0=TT.mult, op1=TT.mult),
                   [pwb[k - 1]], [pwb[k]])
            ew("dve", lambda e, k=k: e.tensor_scalar(out=pw[:, k, 2, :], in0=pw[:, k, 1, :], scalar1=-1.0, scalar2=None, op0=TT.mult),
               [pwb[k]], [pwb[k]])
        for pr in range(NPAIR):
            off = (pr % 4) * 32
            for ri, src in enumerate((bbr, bbi)):
                pd, pdb = pad[:, ri, :], padb[ri]
                s.op("pool", lambda e, pd=pd: e.memset(pd, 0.0), writes=[pdb])
                for gi in range(2):
                    s.op("pool", lambda e, pd=pd, gi=gi, src=src, pr=pr, off=off: e.tensor_copy(
                        out=pd[gi * 64:(gi + 1) * 64, off + gi * 16:off + gi * 16 + 16], in_=src[gi * 64:(gi + 1) * 64, pr, :]),
                        reads=[bbrb, bbib], writes=[pdb])
                pt, pb = psrot.next()
                s.op("pe", lambda e, pt=pt, pd=pd: e.transpose(pt[:, 0:128], pd, ident[:, :]), reads=[pdb, KB], writes=[pb])
                s.op("act", lambda e, pt=pt, pr=pr, ri=ri: e.activation(out=lB[:, pr, ri, :], in_=pt[:, 0:128], func=AF.Copy),
                     reads=[pb], writes=[lBb[pr]])
            s.op("pool", lambda e, pr=pr: e.memset(lC[:, pr, :, :], 0.0), writes=[lCb[pr]])
            for gi in range(2):
                s.op("pool", lambda e, gi=gi, pr=pr, off=off: e.tensor_copy(
                    out=lC[gi * 64:(gi + 1) * 64, pr, 0, off + gi * 16:off + gi * 16 + 16], in_=cre[gi * 64:(gi + 1) * 64, pr, :]),
                    reads=[creb], writes=[lCb[pr]])
                s.op("pool", lambda e, gi=gi, pr=pr, off=off: e.tensor_scalar(
                    out=lC[gi * 64:(gi + 1) * 64, pr, 1, off + gi * 16:off + gi * 16 + 16], in0=cim[gi * 64:(gi + 1) * 64, pr, :],
                    scalar1=-1.0, scalar2=None, op0=TT.mult), reads=[cimb], writes=[lCb[pr]])

        NT = SEQ // 512
        for ct in range(2):
            for q4 in range(4):
                pr = ct * 4 + q4
                for tt in range(NT):
                    ts_ = slice(tt * 512, (tt + 1) * 512)
                    for ri, (dst, dstb) in enumerate(((R[0], Rb[0]), (I[0], Ib[0]))):
                        pt, pb = psrot.next()
                        mm(s, pt[:, :], lB[:, pr, ri, :], u[:, ct, ts_], True, True, reads=[lBb[pr], ub[ct]], writes=[pb])
                        s.op("act", lambda e, pt=pt, dst=dst, ts_=ts_: e.activation(out=dst[:, ts_], in_=pt[:, :], func=AF.Copy),
                             reads=[pb], writes=[dstb])
                cur = 0
                for k in range(NSTEP):
                    d = 1 << k
                    nx = 1 - cur
                    R0, I0, R1, I1 = R[cur], I[cur], R[nx], I[nx]
                    pre_, pim_, pnim_ = pw[:, k, 0, pr:pr + 1], pw[:, k, 1, pr:pr + 1], pw[:, k, 2, pr:pr + 1]
                    s.op("act", lambda e, R0=R0, R1=R1, d=d: e.activation(out=R1[:, 0:d], in_=R0[:, 0:d], func=AF.Copy),
                         reads=[Rb[cur]], writes=[Rb[nx]])
                    s.op("act", lambda e, I0=I0, I1=I1, d=d: e.activation(out=I1[:, 0:d], in_=I0[:, 0:d], func=AF.Copy),
                         reads=[Ib[cur]], writes=[Ib[nx]])
                    s.op("dve", lambda e, R0=R0, R1=R1, d=d, sc=pre_: e.scalar_tensor_tensor(
                        out=R1[:, d:SEQ], in0=R0[:, 0:SEQ - d], scalar=sc, in1=R0[:, d:SEQ], op0=TT.mult, op1=TT.add),
                        reads=[Rb[cur], pwb[k]], writes=[Rb[nx]])
                    s.op("dve", lambda e, I0=I0, R1=R1, d=d, sc=pnim_: e.scalar_tensor_tensor(
                        out=R1[:, d:SEQ], in0=I0[:, 0:SEQ - d], scalar=sc, in1=R1[:, d:SEQ], op0=TT.mult, op1=TT.add),
                        reads=[Ib[cur], Rb[nx], pwb[k]], writes=[Rb[nx]])
                    s.op("dve", lambda e, I0=I0, I1=I1, d=d, sc=pre_: e.scalar_tensor_tensor(
                        out=I1[:, d:SEQ], in0=I0[:, 0:SEQ - d], scalar=sc, in1=I0[:, d:SEQ], op0=TT.mult, op1=TT.add),
                        reads=[Ib[cur], pwb[k]], writes=[Ib[nx]])
                    s.op("dve", lambda e, R0=R0, I1=I1, d=d, sc=pim_: e.scalar_tensor_tensor(
                        out=I1[:, d:SEQ], in0=R0[:, 0:SEQ - d], scalar=sc, in1=I1[:, d:SEQ], op0=TT.mult, op1=TT.add),
                        reads=[Rb[cur], Ib[nx], pwb[k]], writes=[Ib[nx]])
                    cur = nx
                for tt in range(NT):
                    ts_ = slice(tt * 512, (tt + 1) * 512)
                    pt, pb = psrot.next()
                    mm(s, pt[:, :], lC[:, pr, 0, :], R[cur][:, ts_], True, False, reads=[lCb[pr], Rb[cur]], writes=[pb])
                    mm(s, pt[:, :], lC[:, pr, 1, :], I[cur][:, ts_], False, True, reads=[lCb[pr], Ib[cur]], writes=[pb])
                    if q4 == 0:
                        s.op("act", lambda e, pt=pt, ts_=ts_: e.activation(out=yacc[:, ts_], in_=pt[:, :], func=AF.Copy),
                             reads=[pb], writes=[yab])
                    else:
                        s.op("dve", lambda e, pt=pt, ts_=ts_: e.tensor_tensor(out=yacc[:, ts_], in0=yacc[:, ts_], in1=pt[:, :], op=TT.add),
                             reads=[pb, yab], writes=[yab])
            s.op("dve", lambda e, ct=ct: e.scalar_tensor_tensor(out=yacc[:, :], in0=u[:, ct, :], scalar=dd[:, ct:ct + 1], in1=yacc[:, :],
                                                              op0=TT.mult, op1=TT.add), reads=[ub[ct], ddb, yab], writes=[yab])
            s.op("act", lambda e: e.activation(out=yo[:, :], in_=yacc[:, :], func=AF.Gelu), reads=[yab], writes=[yob])
            s.dma("sp", S2[256 + ct * 128:256 + (ct + 1) * 128, :], yo[:, :], reads=[yob], writes=[S2b[2 + ct].new()])


def phase_C(c, K, P, l, xT, xb, H2, H2b, out_d, final):
    s = c.s
    with c.phase():
        R1 = c.sb([128, KC, NTH], BF16); r1b = bufs(KC, "r1")
        R2 = c.sb([128, KC, NTH], BF16); r2b = bufs(KC, "r2")
        mT, mb = R1, r1b
        sT, sTb = R2, r2b
        hT, hb = R2, r2b
        sq, sqb = R1, r1b
        NA = NTH - 2
        act = R1[:, :, :].rearrange("p a b -> p (a b)")[:, 0:NFP * NA].rearrange("p (j t) -> p j t", t=NA)
        actb = bufs(NFP, "act")
        gcol = c.sb([128, KC], F32); gb = Buf("g")
        gfcol = c.sb([128, KC], F32); gfb = Buf("gf")
        bglu = c.sb([128, 8], F32); bglub = Buf("bglu")
        cw = c.sb([128, FC_TOT, 3], F32); cwb = Buf("cw")
        cb = c.sb([128, FC_TOT], F32); cbb = Buf("cb")
        rstd = c.sb([128, NTH], F32); rstdb = Buf("rstd")
        psrot = Rot([(K.banks[i], Buf(f"psC{i}")) for i in range(7)])
        psh = K.banks[7]
        hrot = Rot([(psh[:, 64 * i:64 * i + 64], Buf(f"psh{i}")) for i in range(8)])
        wrot = Rot([(c.sb([128, KC, 128], BF16), Buf(f"w{i}")) for i in range(4)])
        wdrot = Rot([(c.sb([128, NFP, 128], BF16), Buf(f"wd{i}")) for i in range(3)])
        gprot = Rot([(c.sb([128, NTH], F32), Buf(f"gp{i}")) for i in range(2)])
        t1rot = Rot([(c.sb([128, NA], F32), Buf(f"t1{i}")) for i in range(2)])
        t2rot = Rot([(c.sb([128, NA], F32), Buf(f"t2{i}")) for i in range(2)])

        def pstile(tn):
            return hrot.next() if tn <= 64 else psrot.next()

        s.dma("sp", gcol[:, :], P["gffn"][l], writes=[gb])
        s.dma("sp", gfcol[:, :], P["gf"], writes=[gfb])
        s.dma("sp", bglu[:, :], P["bglu"][l], writes=[bglub])
        s.dma("sp", cb[:, :], P["f_cb"][l], writes=[cbb])
        s.dma("sp", cw[:, :, :], P["f_cw"][l], writes=[cwb])
        def pos(m):
            return (m % 2) * 4 + m // 2 if m < 8 else (2 + m % 2) * 4 + (m - 8) // 2
        G2v = H2.rearrange("(q p) t -> p q t", p=128)
        s.op("act", lambda e: e.dma_start(out=mT[:, :, HALO:NTH], in_=G2v[:, :, bass.ds(K.jva * NTOK, NTOK)]),
             reads=[H2b], writes=list(mb), kind=1)
        s.op("act", lambda e: e.dma_start(out=mT[:, :, 0:HALO], in_=G2v[:, :, bass.ds(((K.jva + 3) % 4) * NTOK + (NTOK - HALO), HALO)]),
             reads=[H2b], writes=list(mb), kind=1)
        mq = c.sb([128, 1], F32); mqb = Buf("mq")
        s.dma("sp", mq[:, :], P["maskq"], writes=[mqb])
        s.op("dve", lambda e: e.tensor_scalar(out=mT[:, :, 0:HALO], in0=mT[:, :, 0:HALO], scalar1=mq[:, 0:1], scalar2=None,
                                              op0=ALU.mult), reads=list(mb) + [mqb], writes=list(mb))
        T3 = [(0, HALO), (HALO, 512), (HALO + 512, 512)]
        TF = [(2, HALO - 2), (HALO, 512), (HALO + 512, 512)]

        wglu = P["w_glu"][l]
        for fc in range(8):
            wt, wb = load_w(s, wrot, wglu, 8, fc * 128, 128)
            for (t0, tn) in T3:
                pt, pb = pstile(tn)
                for kc in range(8):
                    mm(s, pt[:, 0:tn], wt[:, kc, :], mT[:, pos(8 + kc), t0:t0 + tn], kc == 0, kc == 7,
                       reads=[wb, mb[pos(8 + kc)]], writes=[pb])
                t1, t1b = t1rot.next()
                s.op("act", lambda e, pt=pt, t1=t1, tn=tn, fc=fc: e.activation(
                    out=t1[:, 0:tn], in_=pt[:, 0:tn], func=AF.Sigmoid, bias=bglu[:, fc:fc + 1]),
                    reads=[pb, bglub], writes=[t1b])
                s.op("dve", lambda e, t1=t1, t0=t0, tn=tn, fc=fc: e.tensor_tensor(
                    out=sT[:, fc, t0:t0 + tn], in0=t1[:, 0:tn], in1=mT[:, pos(8 + fc), t0:t0 + tn], op=ALU.mult),
                    reads=[t1b, mb[pos(8 + fc)]], writes=[sTb[fc]])
        wout = P["w_out"][l]
        for dc in range(KC):
            wt, wb = load_w(s, wrot, wout, KC, dc * 128, 128)
            for (t0, tn) in T3:
                pt, pb = pstile(tn)
                for kc in range(KC):
                    rhs = mT[:, pos(kc), t0:t0 + tn] if kc < 8 else sT[:, kc - 8, t0:t0 + tn]
                    rb = mb[pos(kc)] if kc < 8 else sTb[kc - 8]
                    mm(s, pt[:, 0:tn], wt[:, kc, :], rhs, kc == 0, kc == KC - 1, reads=[wb, rb], writes=[pb])
                s.op("dve", lambda e, pt=pt, dc=dc, t0=t0, tn=tn: e.tensor_tensor(
                    out=xT[:, dc, t0:t0 + tn], in0=xT[:, dc, t0:t0 + tn], in1=pt[:, 0:tn], op=ALU.add),
                    reads=[pb, xb[dc]], writes=[xb[dc]])
        rmsnorm_T(c, K, xT, xb, 0, NTH, T3, gcol, gb, hT, hb, sq, sqb, pstile, rstd, rstdb)
        s.op("dve", lambda e: e.memset(K.tiny2[:, :], 0.0), writes=list(r1b) + list(actb))
        wg_d, wv_d, wd_d = P["w_gate"][l], P["w_val"][l], P["w_down"][l]
        fc_lists = [list(range(p * NFP, min(FC_TOT, (p + 1) * NFP))) for p in range(NPASS)]
        for fcs in fc_lists:
            for j, fc in enumerate(fcs):
                wg, wgb = load_w(s, wrot, wg_d, KC, fc * 128, 128)
                wv, wvb = load_w(s, wrot, wv_d, KC, fc * 128, 128)
                gp, gpb = gprot.next()
                for (t0, tn) in T3:
                    pt, pb = pstile(tn)
                    for kc in range(KC):
                        mm(s, pt[:, 0:tn], wg[:, kc, :], hT[:, kc, t0:t0 + tn], kc == 0, kc == KC - 1,
                           reads=[wgb, hb[kc]], writes=[pb])
                    s.op("act", lambda e, pt=pt, gp=gp, t0=t0, tn=tn: e.activation(
                        out=gp[:, t0:t0 + tn], in_=pt[:, 0:tn], func=AF.Copy), reads=[pb], writes=[gpb])
                t1, t1b = t1rot.next()
                s.op("dve", lambda e, gp=gp, t1=t1, fc=fc: e.tensor_scalar(
                    out=t1[:, :], in0=gp[:, 2:NTH], scalar1=cw[:, fc, 2:3], scalar2=cb[:, fc:fc + 1],
                    op0=ALU.mult, op1=ALU.add), reads=[gpb, cwb, cbb], writes=[t1b])
                s.op("dve", lambda e, gp=gp, t1=t1, fc=fc: e.scalar_tensor_tensor(
                    out=t1[:, :], in0=gp[:, 1:NTH - 1], scalar=cw[:, fc, 1:2], in1=t1[:, :],
                    op0=ALU.mult, op1=ALU.add), reads=[gpb, cwb, t1b], writes=[t1b])
                s.op("dve", lambda e, gp=gp, t1=t1, fc=fc: e.scalar_tensor_tensor(
                    out=t1[:, :], in0=gp[:, 0:NTH - 2], scalar=cw[:, fc, 0:1], in1=t1[:, :],
                    op0=ALU.mult, op1=ALU.add), reads=[gpb, cwb, t1b], writes=[t1b])
                t2, t2b = t2rot.next()
                s.op("act", lambda e, t1=t1, t2=t2: e.activation(out=t2[:, :], in_=t1[:, :], func=AF.Gelu),
                     reads=[t1b], writes=[t2b])
                for (t0, tn) in TF:
                    pt, pb = pstile(tn)
                    for kc in range(KC):
                        mm(s, pt[:, 0:tn], wv[:, kc, :], hT[:, kc, t0:t0 + tn], kc == 0, kc == KC - 1,
                           reads=[wvb, hb[kc]], writes=[pb])
                    s.op("dve", lambda e, pt=pt, t2=t2, j=j, t0=t0, tn=tn: e.tensor_tensor(
                        out=act[:, j, t0 - 2:t0 - 2 + tn], in0=t2[:, t0 - 2:t0 - 2 + tn], in1=pt[:, 0:tn],
                        op=ALU.mult), reads=[pb, t2b], writes=[actb[j]])
            nfc = len(fcs)
            for dc in range(KC):
                wdt, wdb = wdrot.next()
                src = wd_d[fcs[0] * 128:(fcs[0] + nfc) * 128, dc * 128:(dc + 1) * 128].rearrange("(j p) f -> p j f", p=128)
                s.dma("pool", wdt[:, 0:nfc, :], src, writes=[wdb])
                for (t0, tn) in TF:
                    pt, pb = pstile(tn)
                    for j in range(nfc):
                        mm(s, pt[:, 0:tn], wdt[:, j, :], act[:, j, t0 - 2:t0 - 2 + tn], j == 0, j == nfc - 1,
                           reads=[wdb, actb[j]], writes=[pb])
                    s.op("dve", lambda e, pt=pt, dc=dc, t0=t0, tn=tn: e.tensor_tensor(
                        out=xT[:, dc, t0:t0 + tn], in0=xT[:, dc, t0:t0 + tn], in1=pt[:, 0:tn], op=ALU.add),
                        reads=[pb, xb[dc]], writes=[xb[dc]])
        if final:
            s.op("dve", lambda e: e.memset(K.tiny2[:, :], 0.0), writes=list(actb) + list(r1b))
            T2 = [(HALO, 512), (HALO + 512, 512)]
            for kc in range(KC):
                s.op("act", lambda e, kc=kc: e.activation(out=sq[:, kc, HALO:NTH], in_=xT[:, kc, HALO:NTH], func=AF.Square),
                     reads=[xb[kc]], writes=[sqb[kc]])
            for (t0, tn) in T2:
                pt, pb = psrot.next()
                for kc in range(KC):
                    mm(s, pt[:, 0:tn], K.ones[:, :], sq[:, kc, t0:t0 + tn], kc == 0, kc == KC - 1,
                       reads=[sqb[kc], K.b], writes=[pb])
                s.op("act", lambda e, pt=pt, t0=t0, tn=tn: e.activation(
                    out=rstd[:, t0:t0 + tn], in_=pt[:, 0:tn], func=AF.Sqrt, bias=K.eps[:, 0:1]),
                    reads=[pb, K.b], writes=[rstdb])
                s.op("dve", lambda e, t0=t0, tn=tn: e.reciprocal(out=rstd[:, t0:t0 + tn], in_=rstd[:, t0:t0 + tn]),
                     reads=[rstdb], writes=[rstdb])
            for kc in range(KC):
                s.op("dve", lambda e, kc=kc: e.scalar_tensor_tensor(
                    out=xT[:, kc, HALO:NTH], in0=xT[:, kc, HALO:NTH], scalar=gfcol[:, kc:kc + 1], in1=rstd[:, HALO:NTH],
                    op0=ALU.mult, op1=ALU.mult), reads=[xb[kc], rstdb, gfb], writes=[xb[kc]])
            ov = out_d.rearrange("(kc p) t -> p kc t", p=128)
            for kc in range(KC):
                s.dma("sp", ov[:, kc, :], xT[:, kc, HALO:NTH], reads=[xb[kc]])


PARAM_SPECS = [
    ("xT0", [D_MODEL, NTH]), ("maskq", [128, 1]), ("gmix", [DEPTH, 128, KC]), ("gffn", [DEPTH, 128, KC]), ("gf", [128, KC]),
    ("bglu", [DEPTH, 128, 8]), ("f_cw", [DEPTH, 128, FC_TOT, 3]), ("f_cb", [DEPTH, 128, FC_TOT]),
    ("w_in", [DEPTH, D_MODEL, IN_COLS]), ("w_glu", [DEPTH, 1024, 1024]), ("w_out", [DEPTH, D_MODEL, D_MODEL]),
    ("w_gate", [DEPTH, D_MODEL, D_FF]), ("w_val", [DEPTH, D_MODEL, D_FF]), ("w_down", [DEPTH, D_FF, D_MODEL]),
    ("m_cw", [DEPTH, 128, 2, 4]), ("m_cb", [DEPTH, 128, 2]), ("m_wq", [DEPTH, DV, DK]), ("m_wk", [DEPTH, DV, DK]),
    ("m_ib", [DEPTH, 128, 1]), ("m_fb", [DEPTH, 128, 1]), ("m_hg", [DEPTH, 128, 2]), ("m_sk", [DEPTH, 128, 2]),
    ("s_are", [DEPTH, 128, NPAIR]), ("s_aim", [DEPTH, 128, NPAIR]), ("s_ldt", [DEPTH, 128, NPAIR]),
    ("s_bre", [DEPTH, 128, NPAIR, 16]), ("s_bim", [DEPTH, 128, NPAIR, 16]),
    ("s_cre", [DEPTH, 128, NPAIR, 16]), ("s_cim", [DEPTH, 128, NPAIR, 16]), ("s_dd", [DEPTH, 128, 2]),
]


def build_fused(depth=DEPTH, stop_after=None):
    c = Ctx()
    s = c.s
    P = {name: c.dram(name, shape, F32) for (name, shape) in PARAM_SPECS}
    out_d = c.dram("outT", [D_MODEL, NTOK], F32, "ExternalOutput")
    K = Consts()
    K.jv = c.nc.sync.partition_id() % 4
    K.jva = c.nc.scalar.partition_id() % 4
    K.b = Buf("consts")
    K.banks = [c.ps([128, 512]) for _ in range(8)]
    c.tiny = c.sb([128, 1], F32)
    K.tiny2 = c.sb([128, 1], F32)
    K.ones = c.sb([128, 128], BF16)
    K.eps = c.sb([128, 1], F32)
    K.one1 = c.sb([128, 1], F32)
    K.onesf = c.sb([128, 128], F32)
    K.dmat = c.sb([128, 128], F32)
    K.tri = c.sb([128, 128], F32)
    K.negm = c.sb([128, 128], F32)
    K.ident = c.sb([128, 128], F32)
    xT = c.sb([128, KC, NTH], F32)
    xb = bufs(KC, "x")
    dmb = Buf()
    s.op("pool", lambda e: e.iota(K.dmat[:, :], [[1, 128]], base=0, channel_multiplier=-1,
                                  allow_small_or_imprecise_dtypes=True), writes=[dmb])
    s.op("dve", lambda e: e.tensor_single_scalar(out=K.tri[:, :], in_=K.dmat[:, :], scalar=0.0, op=ALU.is_ge),
         reads=[dmb], writes=[K.b])
    s.op("dve", lambda e: e.tensor_single_scalar(out=K.ident[:, :], in_=K.dmat[:, :], scalar=0.0, op=ALU.is_equal),
         reads=[dmb], writes=[K.b])
    s.op("dve", lambda e: e.tensor_scalar(out=K.negm[:, :], in0=K.tri[:, :], scalar1=-1.0, scalar2=30000.0,
                                          op0=ALU.add, op1=ALU.mult), reads=[K.b], writes=[K.b])
    s.op("dve", lambda e: e.memset(K.onesf[:, :], 1.0), writes=[K.b])
    s.op("dve", lambda e: e.memset(K.one1[:, :], 1.0), writes=[K.b])
    s.op("dve", lambda e: e.memset(K.eps[:, :], EPS), writes=[K.b])
    s.op("dve", lambda e: e.memset(K.ones[:, :], 1.0 / D_MODEL), writes=[K.b])
    xv = P["xT0"].rearrange("(kc p) t -> p kc t", p=128)
    for kc in range(KC):
        s.dma("sp", xT[:, kc, :], xv[:, kc, :], writes=[xb[kc]])
    for l in range(depth):
        S1 = c.dram(f"S1_{l}", [4096, NTOK], BF16, "Internal"); S1b = [BufSet() for _ in range(8)]
        G1 = c.dram(f"G1_{l}", [8 * 2048, NTOK], BF16, "Internal"); G1b = bufs(8, "G1")
        S1g = c.dram(f"S1g_{l}", [8, NTOK], F32, "Internal"); S1gb = BufSet()
        G1g = c.dram(f"G1g_{l}", [32, NTOK], F32, "Internal"); G1gb = Buf()
        S2 = c.dram(f"S2_{l}", [512, SEQ], BF16, "Internal"); S2b = [BufSet() for _ in range(4)]
        G2 = c.dram(f"G2_{l}", [2048, SEQ], BF16, "Internal"); G2b = Buf()
        H1 = c.dram(f"H1_{l}", [4, 1024, NTOK], BF16, "Internal"); H1b = Buf()
        Hg = c.dram(f"Hg_{l}", [4, 2, NTOK], F32, "Internal"); Hgb = Buf()
        phase_A(c, K, P, l, xT, xb, S1, S1b, S1g, S1gb)
        for k in range(8):
            allgather(c, S1[k * 512:(k + 1) * 512, :], S1b[k], G1[k * 2048:(k + 1) * 2048, :], G1b[k])
        allgather(c, S1g[:, :], S1gb, G1g[:, :], G1gb)
        with c.phase():
            stg = c.sb([128, 32, NTOK], BF16); stgb = Buf("stg")
            stg2 = c.sb([8, NTOK], F32); stg2b = Buf("stg2")
            s.op("sp", lambda e, G1=G1: e.dma_start(
                out=stg[:, :, :], in_=G1[bass.ds(K.jv * 4096, 4096), :].rearrange("(q p) t -> p q t", p=128)),
                reads=list(G1b), writes=[stgb], kind=1)
            for kk in range(2):
                for r in range(4):
                    s.dma("sp", H1[r, kk * 512:(kk + 1) * 512, :].rearrange("(i8 p) t -> p i8 t", p=128),
                          stg[:, kk * 16 + r * 4:kk * 16 + r * 4 + 4, :], reads=[stgb], writes=[H1b])
            G1gv = G1g.rearrange("(sga h) t -> sga h t", h=4)
            s.op("act", lambda e, G1gv=G1gv: e.dma_start(
                out=stg2[:, :], in_=G1gv[:, bass.ds(K.jva, 1), :].rearrange("sga o t -> sga (o t)")),
                reads=[G1gb], writes=[stg2b], kind=1)
            s.dma("sp", Hg.rearrange("sg a t -> (sg a) t"), stg2[:, :], reads=[stg2b], writes=[Hgb])
        phase_B1(c, K, P, l, H1, H1b, Hg, Hgb, S2, S2b)
        phase_B2(c, K, P, l, H1, H1b, S2, S2b)
        G2bs = bufs(4, "G2")
        for k in range(4):
            allgather(c, S2[k * 128:(k + 1) * 128, :], S2b[k], G2[k * 512:(k + 1) * 512, :], G2bs[k])
        s.op("dve", lambda e: e.memset(K.tiny2[:, :], 0.0), reads=G2bs, writes=[G2b])
        phase_C(c, K, P, l, xT, xb, G2, G2b, out_d, final=(l == depth - 1))
    s.emit()
    s.finalize()
    c.st.close()
    return c.nc


def _col(v, n):
    return np.ascontiguousarray(np.asarray(v, np.float32).reshape(n, 128).T)


def _lane(a):
    return np.ascontiguousarray(np.asarray(a, np.float32).reshape(8, 2, 64).transpose(1, 2, 0).reshape(128, 8))


def _lane3(a):
    return np.ascontiguousarray(np.asarray(a, np.float32).reshape(8, 2, 64, 16).transpose(1, 2, 0, 3).reshape(128, 8, 16))


def make_in_maps(P):
    f = lambda a: np.ascontiguousarray(np.asarray(a, np.float32))
    L = DEPTH
    common = {
        "gmix": np.stack([_col(P["norm_mix_g"][l], KC) for l in range(L)]),
        "gffn": np.stack([_col(P["norm_ffn_g"][l], KC) for l in range(L)]),
        "gf": _col(P["norm_final_g"], KC),
        "bglu": np.stack([_col(P["s5_b_glu"][l], 8) for l in range(L)]),
        "f_cw": np.stack([np.ascontiguousarray(f(P["f_conv_w"][l]).reshape(3, FC_TOT, 128).transpose(2, 1, 0)) for l in range(L)]),
        "f_cb": np.stack([_col(P["f_conv_b"][l], FC_TOT) for l in range(L)]),
        "w_in": f(P["w_in"]), "w_glu": f(P["s5_w_glu"]), "w_out": f(P["w_out"]),
        "w_gate": f(P["w_gate"]), "w_val": f(P["w_val"]), "w_down": f(P["w_down"]),
    }
    x = f(P["x"])
    maps = []
    for cid in range(NCORES):
        b, q = divmod(cid, 4)
        m = dict(common)
        xh = np.zeros((NTH, D_MODEL), np.float32)
        lo = max(q * NTOK - HALO, 0)
        xh[HALO - (q * NTOK - lo):] = x[b, lo:(q + 1) * NTOK]
        m["xT0"] = np.ascontiguousarray(xh.T)
        m["maskq"] = np.full((128, 1), 0.0 if q == 0 else 1.0, np.float32)
        sl = slice(q * 256, (q + 1) * 256)
        g0 = q * 16
        m["m_cw"] = np.stack([np.ascontiguousarray(f(P["m_conv_w"][l])[:, sl].reshape(4, 2, 128).transpose(2, 1, 0)) for l in range(L)])
        m["m_cb"] = np.stack([_col(P["m_conv_b"][l][sl], 2) for l in range(L)])
        m["m_wq"] = np.stack([f(P["w_q"][l][q]) for l in range(L)])
        m["m_wk"] = np.stack([f(P["w_k"][l][q]) for l in range(L)])
        m["m_ib"] = np.stack([np.full((128, 1), P["m_i_bias"][l][q], np.float32) for l in range(L)])
        m["m_fb"] = np.stack([np.full((128, 1), P["m_f_bias"][l][q], np.float32) for l in range(L)])
        m["m_hg"] = np.stack([_col(P["m_head_g"][l][sl], 2) for l in range(L)])
        m["m_sk"] = np.stack([_col(P["m_skip"][l][sl], 2) for l in range(L)])
        m["s_are"] = np.stack([_lane(P["s5_a_re"][l][g0:g0 + 16]) for l in range(L)])
        m["s_aim"] = np.stack([_lane(P["s5_a_im"][l][g0:g0 + 16]) for l in range(L)])
        m["s_ldt"] = np.stack([_lane(np.repeat(np.asarray(P["s5_log_dt"][l][g0:g0 + 16])[:, None], 64, 1)) for l in range(L)])
        m["s_bre"] = np.stack([_lane3(P["s5_b_re"][l][g0:g0 + 16]) for l in range(L)])
        m["s_bim"] = np.stack([_lane3(P["s5_b_im"][l][g0:g0 + 16]) for l in range(L)])
        m["s_cre"] = np.stack([_lane3(np.asarray(P["s5_c_re"][l][g0:g0 + 16]).transpose(0, 2, 1)) for l in range(L)])
        m["s_cim"] = np.stack([_lane3(np.asarray(P["s5_c_im"][l][g0:g0 + 16]).transpose(0, 2, 1)) for l in range(L)])
        m["s_dd"] = np.stack([_col(P["s5_d"][l][sl], 2) for l in range(L)])
        maps.append(m)
    return maps


_NC = {}


def kernel(**inputs):
    P = {k: np.asarray(v) for k, v in inputs.items()}
    if "nc" not in _NC:
        _NC["nc"] = build_fused()
    maps = make_in_maps(P)
    res = run_bass_kernel_spmd(_NC["nc"], maps, core_ids=list(range(NCORES)))
    out = np.empty((BATCH, SEQ, D_MODEL), np.float32)
    for cid in range(NCORES):
        b, q = divmod(cid, 4)
        out[b, q * NTOK:(q + 1) * NTOK] = res.results[cid]["outT"].T
    return out
```

```python
import contextlib
import numpy as np
import ml_dtypes
import concourse.bass as bass
import concourse.mybir as mybir
from concourse.bass_utils import run_bass_kernel_spmd

F32 = mybir.dt.float32
BF16 = mybir.dt.bfloat16
I32 = mybir.dt.int32
AF = mybir.ActivationFunctionType
ALU = mybir.AluOpType

D_MODEL = 2048
BATCH = 2
SEQ = 4096
DEPTH = 4
D_FF = 5504
IN_COLS = 4104
EPS = 1e-6
NCORES = 8
NTOK = 1024
HALO = 8
NTH = NTOK + HALO
KC = D_MODEL // 128
FC_TOT = D_FF // 128
NPASS = 4
NFP = (FC_TOT + NPASS - 1) // NPASS
LCH = 128
NCH = SEQ // LCH
DV = 256
DK = 128
NPAIR = 8
NSTEP = 12
PI = float(np.pi)
GROUPS = [[0, 1, 2, 3], [4, 5, 6, 7]]


class Buf:
    __slots__ = ("w", "r", "name")
    registry = []
    fence = None

    def __init__(self, name=""):
        self.w = Buf.fence
        self.r = []
        self.name = name
        Buf.registry.append(self)


def bufs(n, name=""):
    return [Buf(f"{name}{i}") for i in range(n)]


class BufSet:
    def __init__(self):
        self.all = []

    def new(self):
        b = Buf()
        self.all.append(b)
        return b


class Sched:
    ENG = ("pe", "act", "dve", "pool", "sp")
    NDMA_SEM = 8
    DEDUPE = False
    NDMA_Q = {"sp": 8, "pool": 6, "act": 4}

    def __init__(self, nc, st):
        self.nc = nc
        self.ops = []
        self.done = 0
        self.esem = {e: st.enter_context(nc.semaphore(f"s_{e}")) for e in self.ENG}
        self.dsem = {e: [st.enter_context(nc.semaphore(f"d_{e}{k}")) for k in range(self.NDMA_Q[e])]
                     for e in ("sp", "pool", "act")}
        self.csem = st.enter_context(nc.semaphore("s_cc"))
        self.ecount = {e: 0 for e in self.ENG}
        self.dcount = {e: 0 for e in self.dsem}
        self.ccount = 0
        self.tok = []
        self.seen = {e: {} for e in self.ENG}

    def op(self, eng, fn, reads=(), writes=(), kind=0, force=False):
        idx = len(self.ops)
        deps = set()
        for b in reads:
            if b.w is not None:
                deps.add(b.w)
        last = {}
        for b in writes:
            if b.w is not None:
                deps.add(b.w)
            for r in b.r:
                re_, _, _, rk, _ = self.ops[r]
                if rk:
                    deps.add(r)
                elif not Sched.DEDUPE:
                    deps.add(r)
                elif r > last.get(re_, -1):
                    last[re_] = r
        deps.update(last.values())
        deps.discard(idx)
        for b in reads:
            b.r.append(idx)
        for b in writes:
            b.w = idx
            b.r = []
        self.ops.append((eng, fn, deps, kind, force))
        return idx

    def dma(self, eng, out, in_, reads=(), writes=(), **kw):
        return self.op(eng, lambda e: e.dma_start(out=out, in_=in_, **kw), reads, writes, kind=1)

    def fence(self, tiny):
        idx = self.op("dve", lambda e: e.memset(tiny[:, :], 0.0), writes=list(Buf.registry), force=True)
        Buf.fence = idx
        return idx

    def emit(self):
        nc = self.nc
        engs = {"pe": nc.tensor, "act": nc.scalar, "dve": nc.vector, "pool": nc.gpsimd, "sp": nc.sync}
        ops = self.ops
        n = len(ops)
        lo = self.done
        needed = {}
        for i in range(lo, n):
            e, fn, deps, kind, force = ops[i]
            if force:
                needed[i] = True
            for d in deps:
                if d < lo:
                    continue
                pe_, _, _, pk, _ = ops[d]
                if pk or pe_ != e or kind:
                    needed[d] = True
                elif e != "pe":
                    needed[d] = True
        tok = self.tok
        tok.extend([None] * (n - len(tok)))
        seen = self.seen
        for i in range(lo, n):
            e, fn, deps, kind, force = ops[i]
            E = engs[e]
            waits = {}
            for d in deps:
                t = tok[d]
                if t is None:
                    continue
                key = t[:-1]
                if t[-1] > waits.get(key, 0):
                    waits[key] = t[-1]
            if kind == 1:
                k = self.dcount[e] % self.NDMA_Q[e]
                rnd = self.dcount[e] // self.NDMA_Q[e]
                if rnd > 0:
                    key = ("d", e, k)
                    waits[key] = max(waits.get(key, 0), 16 * rnd)
            for key, val in waits.items():
                if seen[e].get(key, 0) >= val:
                    continue
                seen[e][key] = val
                if key[0] == "e":
                    sem = self.esem[key[1]]
                elif key[0] == "d":
                    sem = self.dsem[key[1]][key[2]]
                else:
                    sem = self.csem
                E.wait_ge(sem, val)
            ins = fn(E)
            if kind == 1:
                k = self.dcount[e] % self.NDMA_Q[e]
                rnd = self.dcount[e] // self.NDMA_Q[e]
                ins.then_inc(self.dsem[e][k], 16)
                tok[i] = ("d", e, k, 16 * (rnd + 1))
                self.dcount[e] += 1
            elif kind == 2:
                self.ccount += 1
                ins.then_inc(self.csem, 1)
                tok[i] = ("c", self.ccount)
            elif needed.get(i):
                self.ecount[e] += 1
                ins.then_inc(self.esem[e], 1)
                tok[i] = ("e", e, self.ecount[e])
        self.done = n

    def finalize(self):
        E = self.nc.sync
        for e in self.dsem:
            for k in range(self.NDMA_Q[e]):
                cnt = (self.dcount[e] - k + self.NDMA_Q[e] - 1) // self.NDMA_Q[e]
                if cnt > 0 and self.seen["sp"].get(("d", e, k), 0) < 16 * cnt:
                    E.wait_ge(self.dsem[e][k], 16 * cnt)


class Ctx:
    def __init__(self):
        Buf.registry = []
        Buf.fence = None
        self.nc = bass.Bass("TRN2", target_bir_lowering=False, num_devices=NCORES)
        self.st = contextlib.ExitStack()
        self.s = Sched(self.nc, self.st)
        self.scope = self.st
        self._n = 0

    def sb(self, shape, dt, name=None):
        self._n += 1
        return self.scope.enter_context(self.nc.sbuf_tensor(name or f"sb{self._n}", list(shape), dt))

    def ps(self, shape, dt=F32):
        self._n += 1
        return self.st.enter_context(self.nc.psum_tensor(f"ps{self._n}", list(shape), dt))

    def dram(self, name, shape, dt, kind="ExternalInput"):
        return self.nc.dram_tensor(name, list(shape), dt, kind=kind).ap()

    @contextlib.contextmanager
    def phase(self):
        old = self.scope
        with contextlib.ExitStack() as ps_:
            self.scope = ps_
            yield
            self.s.fence(self.tiny)
            self.s.emit()
        self.scope = old


class Rot:
    def __init__(self, items):
        self.items = items
        self.i = 0

    def next(self):
        it = self.items[self.i % len(self.items)]
        self.i += 1
        return it


def mm(s, out, lhsT, rhs, start, stop, reads, writes):
    s.op("pe", lambda e, o=out, l=lhsT, r=rhs, a=start, b=stop: e.matmul(o, l, r, start=a, stop=b),
         reads=reads, writes=writes)


def load_w(s, wrot, w_dram, kcs, f0, fw, k0=0):
    wt, wb = wrot.next()
    src = w_dram[k0 * 128:(k0 + kcs) * 128, f0:f0 + fw].rearrange("(kc p) f -> p kc f", p=128)
    s.dma("pool", wt[:, 0:kcs, 0:fw], src, writes=[wb])
    return wt, wb


def rmsnorm_T(c, K, xT, xb, c0, ntok, tiles, gcol, gb, hT, hb, sq, sqb, pstile, rstd, rstdb):
    s = c.s
    for kc in range(KC):
        s.op("act", lambda e, kc=kc: e.activation(out=sq[:, kc, 0:ntok], in_=xT[:, kc, c0:c0 + ntok], func=AF.Square),
             reads=[xb[kc]], writes=[sqb[kc]])
    for (t0, tn) in tiles:
        pt, pb = pstile(tn)
        for kc in range(KC):
            mm(s, pt[:, 0:tn], K.ones[:, :], sq[:, kc, t0:t0 + tn], kc == 0, kc == KC - 1,
               reads=[sqb[kc], K.b], writes=[pb])
        s.op("act", lambda e, pt=pt, t0=t0, tn=tn: e.activation(
            out=rstd[:, t0:t0 + tn], in_=pt[:, 0:tn], func=AF.Sqrt, bias=K.eps[:, 0:1]), reads=[pb, K.b], writes=[rstdb])
        s.op("dve", lambda e, t0=t0, tn=tn: e.reciprocal(out=rstd[:, t0:t0 + tn], in_=rstd[:, t0:t0 + tn]),
             reads=[rstdb], writes=[rstdb])
    for kc in range(KC):
        s.op("dve", lambda e, kc=kc: e.scalar_tensor_tensor(
            out=hT[:, kc, 0:ntok], in0=xT[:, kc, c0:c0 + ntok], scalar=gcol[:, kc:kc + 1], in1=rstd[:, 0:ntok],
            op0=ALU.mult, op1=ALU.mult), reads=[xb[kc], rstdb, gb], writes=[hb[kc]])


class Consts:
    pass


DEDUPE_PHASES = "C"


@contextlib.contextmanager
def dedupe_scope(name):
    old = Sched.DEDUPE
    Sched.DEDUPE = name in DEDUPE_PHASES
    try:
        yield
    finally:
        Sched.DEDUPE = old


def phase_A(c, K, P, l, xT, xb, S1, S1b, S1g, S1gb):
    s = c.s
    with c.phase(), dedupe_scope("A"):
        hT = c.sb([128, KC, NTOK], BF16); hb = bufs(KC, "h")
        sq = c.sb([128, KC, NTOK], BF16); sqb = bufs(KC, "sq")
        rstd = c.sb([128, NTOK], F32); rstdb = Buf("rstd")
        gcol = c.sb([128, KC], F32); gb = Buf("g")
        psrot = Rot([(K.banks[i], Buf(f"psA{i}")) for i in range(8)])
        wrot = Rot([(c.sb([128, KC, 256], BF16), Buf(f"w{i}")) for i in range(3)])
        orot = Rot([(c.sb([128, NTOK], BF16), bufs(2, f"o{i}_")) for i in range(3)])
        vrot = Rot([(c.sb([128, 8, 256], BF16), bufs(8, f"v{i}_")) for i in range(2)])
        gt = c.sb([8, NTOK], F32); gtb = bufs(2, "gt")
        s.dma("sp", gcol[:, :], P["gmix"][l], writes=[gb])
        tiles = [(0, 512), (512, 512)]
        rmsnorm_T(c, K, xT, xb, HALO, NTOK, tiles, gcol, gb, hT, hb, sq, sqb, lambda tn: psrot.next(), rstd, rstdb)
        w = P["w_in"][l]
        jobs = []
        for k in range(8):
            jobs.append((k * 128, 128, ("S1", (k // 2) * 1024 + (k % 2) * 128)))
        for k in range(8):
            jobs.append((2048 + k * 128, 128, ("S1", (k // 2) * 1024 + 512 + (k % 2) * 128)))
        for k in range(8):
            jobs.append((3080 + k * 128, 128, ("S1", (k // 2) * 1024 + 768 + (k % 2) * 128)))
        jobs.append((3072, 8, ("G", 0)))
        for (f0, fw, (kind, r0)) in jobs:
            wt, wb = load_w(s, wrot, w, KC, f0, fw)
            if kind == "S1":
                ot, ob = orot.next()
            else:
                ot, ob = gt, gtb
            for ti, (t0, tn) in enumerate(tiles):
                pt, pb = psrot.next()
                for kc in range(KC):
                    mm(s, pt[0:fw, 0:tn], wt[:, kc, 0:fw], hT[:, kc, t0:t0 + tn], kc == 0, kc == KC - 1,
                       reads=[wb, hb[kc]], writes=[pb])
                if ti == 0:
                    s.op("act", lambda e, pt=pt, ot=ot, t0=t0, tn=tn, fw=fw: e.activation(
                        out=ot[0:fw, t0:t0 + tn], in_=pt[0:fw, 0:tn], func=AF.Copy), reads=[pb], writes=[ob[ti]])
                else:
                    s.op("dve", lambda e, pt=pt, ot=ot, t0=t0, tn=tn, fw=fw: e.tensor_copy(
                        out=ot[0:fw, t0:t0 + tn], in_=pt[0:fw, 0:tn]), reads=[pb], writes=[ob[ti]])
            if kind == "S1":
                s.dma("sp", S1[r0:r0 + fw, :], ot[0:fw, :], reads=ob, writes=[S1b[r0 // 512].new()])
            else:
                s.dma("sp", S1g[:, :], ot[0:fw, :], reads=ob, writes=[S1gb.new()])
        for vc in range(4):
            wt, wb = load_w(s, wrot, w, KC, 1024 + vc * 256, 256)
            vt, vtb = vrot.next()
            for tc in range(8):
                pt, pb = psrot.next()
                for kc in range(KC):
                    mm(s, pt[:, 0:256], hT[:, kc, tc * 128:(tc + 1) * 128], wt[:, kc, :], kc == 0, kc == KC - 1,
                       reads=[wb, hb[kc]], writes=[pb])
                if tc % 2 == 0:
                    s.op("act", lambda e, pt=pt, vt=vt, tc=tc: e.activation(out=vt[:, tc, :], in_=pt[:, 0:256], func=AF.Copy),
                         reads=[pb], writes=[vtb[tc]])
                else:
                    s.op("dve", lambda e, pt=pt, vt=vt, tc=tc: e.tensor_copy(out=vt[:, tc, :], in_=pt[:, 0:256]),
                         reads=[pb], writes=[vtb[tc]])
            dst = S1[vc * 1024 + 256:vc * 1024 + 512, :].rearrange("r (t4 f) -> (r t4) f", t4=4, f=256).rearrange("(tc p) f -> p tc f", p=128)
            s.dma("sp", dst, vt[:, :, :], reads=vtb, writes=[S1b[2 * vc].new()])


def allgather(c, src, srcb, dst, dstb):
    c.s.op("pool", lambda e: e.collective_compute("AllGather", ALU.bypass, ins=[src], outs=[dst],
                                                   replica_groups=GROUPS, unique_tensors="Yes"),
           reads=list(srcb.all), writes=[dstb], kind=2)


def phase_B1(c, K, P, l, H1, H1b, Hg, Hgb, S2, S2b):
    s = c.s
    with c.phase():
        xm = c.sb([128, 2, SEQ + 3], BF16); xmb = bufs(2, "xm")
        acc = c.sb([128, SEQ], F32); accb = Buf("acc")
        cbf = c.sb([128, 2, SEQ], BF16); cbfb = bufs(2, "cbf")
        csk = c.sb([128, 2, SEQ], BF16); cskb = bufs(2, "csk")
        vaug = c.sb([128, NCH, DV + 1], BF16); vb = Buf("v")
        qT = c.sb([128, SEQ], BF16); qb = Buf("q")
        kT = c.sb([128, SEQ], BF16); kb = Buf("k")
        kt = c.sb([128, NCH, DK], BF16); ktb = bufs(NCH, "kt")
        cw = c.sb([128, 2, 4], F32); cwb = Buf()
        cb = c.sb([128, 2], F32); cbb = Buf()
        wq = c.sb([128, 2, DK], BF16); wqb = Buf()
        wk = c.sb([128, 2, DK], BF16); wkb = Buf()
        ibc = c.sb([128, 1], F32); ibb = Buf()
        fbc = c.sb([128, 1], F32); fbb = Buf()
        hg = c.sb([128, 2], F32); hgb = Buf()
        sk = c.sb([128, 2], F32); skb = Buf()
        icol = c.sb([128, NCH], F32); icb = Buf()
        fcol = c.sb([128, NCH], F32); fcb = Buf()
        lf = c.sb([128, NCH], F32); lfb_ = Buf()
        bcol = c.sb([128, NCH], F32); bcb = Buf()
        gbc = c.sb([128, NCH], F32); gbcb = Buf()
        acol = c.sb([128, NCH], F32); acb = Buf()
        wkc = c.sb([128, NCH], F32); wkcb = Buf()
        egc = c.sb([128, NCH], F32); egb = Buf()
        Cf = c.sb([128, DV + 1], F32); Cfb = Buf("Cf")
        Cb = c.sb([128, DV + 1], BF16); Cbb = Buf("Cb")
        zt = c.sb([128, HALO], BF16); ztb = Buf()
        banks = K.banks
        bankb = bufs(8, "bank")
        tri, negm, ident, onesf = K.tri, K.negm, K.ident, K.onesf
        KB = K.b

        def dyn_dma(out, mk_in, reads, writes, **kw):
            def f(e):
                j = K.jv
                K.ndyn = getattr(K, "ndyn", 0) + 1
                try:
                    return e.dma_start(out=out, in_=mk_in(j), **kw)
                except Exception:
                    print("DYN FAIL at", K.ndyn, out.shape, kw)
                    raise
            s.op("sp", f, reads=reads, writes=writes, kind=1)

        for (t, name, b) in ((cb, "m_cb", cbb), (ibc, "m_ib", ibb), (fbc, "m_fb", fbb), (hg, "m_hg", hgb), (sk, "m_sk", skb)):
            s.dma("sp", t[:, :], P[name][l], writes=[b])
        s.dma("sp", cw[:, :, :], P["m_cw"][l], writes=[cwb])
        s.dma("pool", wq[:, :, :], P["m_wq"][l].rearrange("(a p) d -> p a d", p=128), writes=[wqb])
        s.dma("pool", wk[:, :, :], P["m_wk"][l].rearrange("(a p) d -> p a d", p=128), writes=[wkb])
        for ft in range(2):
            s.op("dve", lambda e, ft=ft: e.memset(xm[:, ft, 0:3], 0.0), writes=[xmb[ft]])
            for sg in range(4):
                s.dma("sp", xm[:, ft, 3 + sg * NTOK:3 + (sg + 1) * NTOK], H1[sg, ft * 128:(ft + 1) * 128, :],
                      reads=[H1b], writes=[xmb[ft]])
        for sg in range(4):
            src = H1[sg, 256:512, :].rearrange("r (t4 f) -> (r t4) f", t4=4, f=256).rearrange("(cc p) f -> p cc f", p=128)
            s.dma("sp", vaug[:, sg * 8:(sg + 1) * 8, 0:DV], src, reads=[H1b], writes=[vb])
        for (dst, dstb, ri) in ((icol, icb, 0), (fcol, fcb, 1)):
            for sg in range(4):
                s.dma("sp", dst[:, sg * 8:(sg + 1) * 8], Hg[sg, ri:ri + 1, :].rearrange("o (cc p) -> p (o cc)", p=128),
                      reads=[Hgb], writes=[dstb], allow_slow_non_contiguous=True)
        s.op("dve", lambda e: e.memset(vaug[:, :, DV:DV + 1], 1.0), writes=[vb])
        s.op("dve", lambda e: e.memset(Cf[:, :], 0.0), writes=[Cfb])
        s.op("dve", lambda e: e.memset(Cb[:, :], 0.0), writes=[Cbb])

        s.op("dve", lambda e: e.tensor_scalar(out=fbc[:, :], in0=fbc[:, :], scalar1=-1.0, scalar2=None, op0=ALU.mult),
             reads=[fbb], writes=[fbb])
        s.op("act", lambda e: e.activation(out=lf[:, :], in_=fcol[:, :], func=AF.Exp, scale=-1.0, bias=fbc[:, 0:1]),
             reads=[fcb, fbb], writes=[lfb_])
        s.op("act", lambda e: e.activation(out=lf[:, :], in_=lf[:, :], func=AF.Ln, bias=K.one1[:, 0:1]),
             reads=[lfb_, KB], writes=[lfb_])
        s.op("dve", lambda e: e.tensor_scalar(out=lf[:, :], in0=lf[:, :], scalar1=-1.0, scalar2=None, op0=ALU.mult),
             reads=[lfb_], writes=[lfb_])
        g0 = banks[1][:, 256:256 + NCH]
        g1 = banks[1][:, 320:320 + NCH]
        g0b, g1b = Buf(), Buf()
        mm(s, g0, tri[:, :], lf[:, :], True, True, reads=[KB, lfb_], writes=[g0b])
        mm(s, g1, onesf[:, :], lf[:, :], True, True, reads=[KB, lfb_], writes=[g1b])
        s.op("dve", lambda e: e.tensor_copy(out=bcol[:, :], in_=g0), reads=[g0b], writes=[bcb])
        s.op("dve", lambda e: e.tensor_copy(out=gbc[:, :], in_=g1), reads=[g1b], writes=[gbcb])
        s.op("dve", lambda e: e.scalar_tensor_tensor(out=acol[:, :], in0=icol[:, :], scalar=ibc[:, 0:1], in1=bcol[:, :],
                                                     op0=ALU.add, op1=ALU.subtract), reads=[icb, ibb, bcb], writes=[acb])
        s.op("dve", lambda e: e.tensor_tensor(out=wkc[:, :], in0=gbc[:, :], in1=acol[:, :], op=ALU.add),
             reads=[gbcb, acb], writes=[wkcb])
        s.op("act", lambda e: e.activation(out=wkc[:, :], in_=wkc[:, :], func=AF.Exp), reads=[wkcb], writes=[wkcb])
        s.op("act", lambda e: e.activation(out=egc[:, :], in_=gbc[:, :], func=AF.Exp), reads=[gbcb], writes=[egb])

        for ft in range(2):
            s.op("dve", lambda e, ft=ft: e.tensor_scalar(
                out=acc[:, :], in0=xm[:, ft, 3:SEQ + 3], scalar1=cw[:, ft, 3:4], scalar2=cb[:, ft:ft + 1],
                op0=ALU.mult, op1=ALU.add), reads=[xmb[ft], cwb, cbb], writes=[accb])
            for k in range(3):
                s.op("dve", lambda e, ft=ft, k=k: e.scalar_tensor_tensor(
                    out=acc[:, :], in0=xm[:, ft, k:SEQ + k], scalar=cw[:, ft, k:k + 1], in1=acc[:, :],
                    op0=ALU.mult, op1=ALU.add), reads=[xmb[ft], cwb, accb], writes=[accb])
            s.op("act", lambda e, ft=ft: e.activation(out=acc[:, :], in_=acc[:, :], func=AF.Silu), reads=[accb], writes=[accb])
            s.op("pool", lambda e, ft=ft: e.tensor_copy(out=cbf[:, ft, :], in_=acc[:, :]), reads=[accb], writes=[cbfb[ft]])
            s.op("pool", lambda e, ft=ft: e.tensor_scalar(out=csk[:, ft, :], in0=acc[:, :], scalar1=sk[:, ft:ft + 1],
                                                         scalar2=None, op0=ALU.mult), reads=[accb, skb], writes=[cskb[ft]])

        pre = Rot([(banks[i], bankb[i]) for i in (2, 3, 4, 5, 7)])
        for tt in range(SEQ // 512):
            for (wt, wtb, dst, dstb, scl) in ((wq, wqb, qT, qb, DK ** -0.5), (wk, wkb, kT, kb, 1.0)):
                pt, pb = pre.next()
                for ft in range(2):
                    mm(s, pt[:, :], wt[:, ft, :], cbf[:, ft, tt * 512:(tt + 1) * 512], ft == 0, ft == 1,
                       reads=[wtb, cbfb[ft]], writes=[pb])
                s.op("act", lambda e, pt=pt, dst=dst, tt=tt, scl=scl: e.activation(
                    out=dst[:, tt * 512:(tt + 1) * 512], in_=pt[:, :], func=AF.Copy, scale=scl), reads=[pb], writes=[dstb])
            pt, pb = pre.next()
            for cc in range(4):
                ch = tt * 4 + cc
                for ft in range(2):
                    mm(s, pt[:, cc * 128:(cc + 1) * 128], cbf[:, ft, ch * 128:(ch + 1) * 128], wk[:, ft, :],
                       ft == 0, ft == 1, reads=[wkb, cbfb[ft]], writes=[pb])
            for cc in range(4):
                ch = tt * 4 + cc
                s.op("dve", lambda e, pt=pt, cc=cc, ch=ch: e.tensor_scalar(
                    out=kt[:, ch, :], in0=pt[:, cc * 128:(cc + 1) * 128], scalar1=wkc[:, ch:ch + 1], scalar2=None,
                    op0=ALU.mult), reads=[pb, wkcb], writes=[ktb[ch]])

        pAB = [(banks[0][:, 256 * i:256 * i + 128], banks[0][:, 256 * i + 128:256 * i + 256], Buf(), Buf()) for i in range(2)]
        pS = [(banks[1][:, 128 * i:128 * i + 128], Buf()) for i in range(2)]
        pN = [(banks[2], bankb[2]), (banks[3], bankb[3])]
        pC = [(banks[4], bankb[4]), (banks[5], bankb[5])]
        pT = [[(banks[6][:, (2 * i + ft) * 128:(2 * i + ft + 1) * 128], Buf()) for ft in range(2)] for i in range(2)]
        lfrot = Rot([(c.sb([128, 128], F32), Buf()) for _ in range(2)])
        ebrot = Rot([(c.sb([128, 128], F32), Buf()) for _ in range(2)])
        wrot_ = Rot([(c.sb([128, 128], F32), Buf()) for _ in range(2)])
        qtrot = Rot([(c.sb([128, 128], BF16), Buf()) for _ in range(2)])
        swrot = Rot([(c.sb([128, 128], BF16), Buf()) for _ in range(2)])
        hnrot = Rot([(c.sb([128, DV], F32), Buf()) for _ in range(2)])
        jkrot = Rot([(c.sb([128, DV], BF16), Buf()) for _ in range(2)])
        smrot = Rot([(c.sb([128, 8], F32), Buf()) for _ in range(2)])
        orot = Rot([(c.sb([128, 2, 128], BF16), Buf()) for _ in range(3)])
        sgrot = Rot([(c.sb([128, 2, 128], F32), Buf()) for _ in range(2)])
        t2rot = Rot([(c.sb([128, 128], F32), Buf()) for _ in range(2)])
        morot = Rot([(c.sb([128, 2, 512], BF16), Buf()) for _ in range(2)])
        mo, mob = None, None
        for ch in range(NCH):
            cs = slice(ch * 128, (ch + 1) * 128)
            sg, tl = divmod(ch * 128, NTOK)
            pA, pB, pAb, pBb = pAB[ch % 2]
            lfm, lfmb = lfrot.next()
            s.op("dve", lambda e, lfm=lfm, ch=ch: e.tensor_scalar(out=lfm[:, :], in0=onesf[:, :], scalar1=lf[:, ch:ch + 1],
                                                                scalar2=None, op0=ALU.mult), reads=[KB, lfb_], writes=[lfmb])
            mm(s, pA, lfm[:, :], tri[:, :], True, True, reads=[lfmb, KB], writes=[pAb])
            mm(s, pB, lfm[:, :], tri[:, :], True, False, reads=[lfmb, KB], writes=[pBb])
            mm(s, pB, ident[:, :], negm[:, :], False, True, reads=[KB], writes=[pBb])
            eb, ebb = ebrot.next()
            s.op("act", lambda e, eb=eb, pA=pA: e.activation(out=eb[:, :], in_=pA, func=AF.Exp), reads=[pAb], writes=[ebb])
            W, Wb = wrot_.next()
            s.op("act", lambda e, W=W, pB=pB, ch=ch: e.activation(out=W[:, :], in_=pB, func=AF.Exp, bias=acol[:, ch:ch + 1]),
                 reads=[pBb, acb], writes=[Wb])
            qt, qtb = qtrot.next()
            s.op("dve", lambda e, qt=qt, eb=eb, cs=cs: e.tensor_tensor(out=qt[:, :], in0=qT[:, cs], in1=eb[:, :], op=ALU.mult),
                 reads=[qb, ebb], writes=[qtb])
            ps_, psb = pS[ch % 2]
            mm(s, ps_, kT[:, cs], qT[:, cs], True, True, reads=[kb, qb], writes=[psb])
            sw, swb = swrot.next()
            s.op("dve", lambda e, sw=sw, ps_=ps_, W=W: e.tensor_tensor(out=sw[:, :], in0=ps_, in1=W[:, :], op=ALU.mult),
                 reads=[psb, Wb], writes=[swb])
            pn, pnb = pN[ch % 2]
            mm(s, pn[:, 0:DV + 1], sw[:, :], vaug[:, ch, :], True, False, reads=[swb, vb], writes=[pnb])
            mm(s, pn[:, 0:DV + 1], qt[:, :], Cb[:, :], False, True, reads=[qtb, Cbb], writes=[pnb])
            pc, pcb = pC[ch % 2]
            mm(s, pc[:, 0:DV + 1], kt[:, ch, :], vaug[:, ch, :], True, True, reads=[ktb[ch], vb], writes=[pcb])
            s.op("dve", lambda e, pc=pc, ch=ch: e.scalar_tensor_tensor(
                out=Cf[:, :], in0=Cf[:, :], scalar=egc[:, ch:ch + 1], in1=pc[:, 0:DV + 1], op0=ALU.mult, op1=ALU.add),
                reads=[Cfb, egb, pcb], writes=[Cfb])
            s.op("act", lambda e: e.activation(out=Cb[:, :], in_=Cf[:, :], func=AF.Copy), reads=[Cfb], writes=[Cbb])
            sm, smb = smrot.next()
            s.op("act", lambda e, sm=sm, pn=pn: e.activation(out=sm[:, 0:1], in_=pn[:, DV:DV + 1], func=AF.Abs),
                 reads=[pnb], writes=[smb])
            s.op("dve", lambda e, sm=sm: e.tensor_scalar(out=sm[:, 0:1], in0=sm[:, 0:1], scalar1=1.0, scalar2=None,
                                                        op0=ALU.max), reads=[smb], writes=[smb])
            s.op("dve", lambda e, sm=sm: e.reciprocal(out=sm[:, 0:1], in_=sm[:, 0:1]), reads=[smb], writes=[smb])
            jk, jkb = jkrot.next()
            s.op("act", lambda e, jk=jk, pn=pn, sm=sm: e.activation(out=jk[:, :], in_=pn[:, 0:DV], func=AF.Square,
                                                                  accum_out=sm[:, 1:2]), reads=[pnb, smb], writes=[jkb, smb])
            s.op("dve", lambda e, sm=sm: e.tensor_tensor(out=sm[:, 2:3], in0=sm[:, 1:2], in1=sm[:, 0:1], op=ALU.mult),
                 reads=[smb], writes=[smb])
            s.op("dve", lambda e, sm=sm: e.tensor_tensor(out=sm[:, 2:3], in0=sm[:, 2:3], in1=sm[:, 0:1], op=ALU.mult),
                 reads=[smb], writes=[smb])
            s.op("act", lambda e, sm=sm: e.activation(out=sm[:, 3:4], in_=sm[:, 2:3], func=AF.Sqrt, scale=1.0 / DV,
                                                    bias=K.eps[:, 0:1]), reads=[smb, KB], writes=[smb])
            s.op("dve", lambda e, sm=sm: e.reciprocal(out=sm[:, 3:4], in_=sm[:, 3:4]), reads=[smb], writes=[smb])
            s.op("dve", lambda e, sm=sm: e.tensor_tensor(out=sm[:, 4:5], in0=sm[:, 3:4], in1=sm[:, 0:1], op=ALU.mult),
                 reads=[smb], writes=[smb])
            hn, hnb = hnrot.next()
            s.op("act", lambda e, hn=hn, pn=pn, sm=sm: e.activation(out=hn[:, :], in_=pn[:, 0:DV], func=AF.Copy,
                                                                  scale=sm[:, 4:5]), reads=[pnb, smb], writes=[hnb])
            ot, otb = orot.next()
            s.dma("sp", ot[:, :, :], H1[sg, 512:768, tl:tl + 128].rearrange("(a p) t -> p a t", p=128), reads=[H1b], writes=[otb])
            so, sob = sgrot.next()
            s.op("act", lambda e, ot=ot, so=so: e.activation(out=so[:, :, :], in_=ot[:, :, :], func=AF.Sigmoid),
                 reads=[otb], writes=[sob])
            if ch % 4 == 0:
                mo, mob = morot.next()
            for ft in range(2):
                ptt, pttb = pT[ch % 2][ft]
                s.op("pe", lambda e, ptt=ptt, hn=hn, ft=ft: e.transpose(ptt, hn[:, ft * 128:(ft + 1) * 128], ident[:, :]),
                     reads=[hnb, KB], writes=[pttb])
                t2, t2b = t2rot.next()
                s.op("dve", lambda e, t2=t2, ptt=ptt, ft=ft, cs=cs: e.scalar_tensor_tensor(
                    out=t2[:, :], in0=ptt, scalar=hg[:, ft:ft + 1], in1=csk[:, ft, cs], op0=ALU.mult, op1=ALU.add),
                    reads=[pttb, hgb, cskb[ft]], writes=[t2b])
                s.op("dve", lambda e, t2=t2, so=so, ft=ft, mo=mo, ch=ch: e.tensor_tensor(
                    out=mo[:, ft, (ch % 4) * 128:(ch % 4 + 1) * 128], in0=t2[:, :], in1=so[:, ft, :], op=ALU.mult),
                    reads=[t2b, sob], writes=[mob])
            if ch % 4 == 3:
                c0 = (ch - 3) * 128
                for ft in range(2):
                    s.dma("sp", S2[ft * 128:(ft + 1) * 128, c0:c0 + 512], mo[:, ft, :], reads=[mob], writes=[S2b[ft].new()])


def phase_B2(c, K, P, l, H1, H1b, S2, S2b):
    s = c.s
    TT = ALU
    with c.phase():
        u = c.sb([128, 2, SEQ], BF16); ub = bufs(2, "u")
        R = [c.sb([128, SEQ], F32) for _ in range(2)]; Rb = bufs(2, "R")
        I = [c.sb([128, SEQ], F32) for _ in range(2)]; Ib = bufs(2, "I")
        yacc = c.sb([128, SEQ], F32); yab = Buf("yacc")
        yo = c.sb([128, SEQ], BF16); yob = Buf("yo")
        Pm = lambda n: (c.sb([128, NPAIR], F32), Buf(n))
        are, areb = Pm("are"); aim, aimb = Pm("aim"); dt, dtb = Pm("dt")
        zr, zrb = Pm("zr"); zi, zib = Pm("zi"); mag, magb = Pm("mag")
        sn, snb = Pm("sn"); cs_, csb = Pm("cs"); t0_, t0b = Pm("t0"); t1_, t1b = Pm("t1")
        ar, arb = Pm("ar"); ai, aib = Pm("ai"); cr, crb = Pm("cr"); ci, cib = Pm("ci")
        nf, nfb = Pm("nf")
        ni = c.sb([128, NPAIR], I32); nib = Buf()
        pw = c.sb([128, NSTEP, 3, NPAIR], F32); pwb = bufs(NSTEP, "pw")
        bre = c.sb([128, NPAIR, 16], F32); breb = Buf()
        bim = c.sb([128, NPAIR, 16], F32); bimb = Buf()
        cre = c.sb([128, NPAIR, 16], F32); creb = Buf()
        cim = c.sb([128, NPAIR, 16], F32); cimb = Buf()
        bbr = c.sb([128, NPAIR, 16], F32); bbrb = Buf()
        bbi = c.sb([128, NPAIR, 16], F32); bbib = Buf()
        tmpb3 = c.sb([128, NPAIR, 16], F32); tmpb3b = Buf()
        dd = c.sb([128, 2], F32); ddb = Buf()
        pad = c.sb([128, 2, 128], F32); padb = bufs(2, "pad")
        lB = c.sb([128, NPAIR, 2, 128], BF16); lBb = bufs(NPAIR, "lB")
        lC = c.sb([128, NPAIR, 2, 128], F32); lCb = bufs(NPAIR, "lC")
        ident = K.ident
        KB = K.b
        psrot = Rot([(K.banks[i], Buf(f"psB{i}")) for i in range(8)])

        for (t, name, b) in ((are, "s_are", areb), (aim, "s_aim", aimb), (dt, "s_ldt", dtb), (dd, "s_dd", ddb)):
            s.dma("sp", t[:, :], P[name][l], writes=[b])
        for (t, name, b) in ((bre, "s_bre", breb), (bim, "s_bim", bimb), (cre, "s_cre", creb), (cim, "s_cim", cimb)):
            s.dma("sp", t[:, :, :], P[name][l], writes=[b])

        def dyn_dma(out, mk_in, reads, writes):
            def f(e):
                j = K.jv
                return e.dma_start(out=out, in_=mk_in(j))
            s.op("sp", f, reads=reads, writes=writes, kind=1)

        for ct in range(2):
            for sg in range(4):
                s.dma("sp", u[:, ct, sg * NTOK:(sg + 1) * NTOK], H1[sg, 768 + ct * 128:768 + (ct + 1) * 128, :],
                      reads=[H1b], writes=[ub[ct]])

        def ew(eng, fn, reads, writes):
            s.op(eng, fn, reads=reads, writes=writes)

        ew("act", lambda e: e.activation(out=dt[:, :], in_=dt[:, :], func=AF.Exp), [dtb], [dtb])
        ew("dve", lambda e: e.tensor_tensor(out=zr[:, :], in0=are[:, :], in1=dt[:, :], op=TT.mult), [areb, dtb], [zrb])
        ew("dve", lambda e: e.tensor_tensor(out=zi[:, :], in0=aim[:, :], in1=dt[:, :], op=TT.mult), [aimb, dtb], [zib])
        ew("act", lambda e: e.activation(out=mag[:, :], in_=zr[:, :], func=AF.Exp), [zrb], [magb])

        def sin_of(dst, dstb, shift):
            ew("dve", lambda e: e.tensor_scalar(out=t0_[:, :], in0=zi[:, :], scalar1=shift, scalar2=1.0 / (2 * PI),
                                                op0=TT.add, op1=TT.mult), [zib, t0b], [t0b])
            ew("dve", lambda e: e.tensor_copy(out=ni[:, :], in_=t0_[:, :]), [t0b], [nib])
            ew("dve", lambda e: e.tensor_copy(out=nf[:, :], in_=ni[:, :]), [nib], [nfb])
            ew("dve", lambda e: e.tensor_scalar(out=t1_[:, :], in0=zi[:, :], scalar1=shift, scalar2=None, op0=TT.add), [zib, t1b], [t1b])
            ew("dve", lambda e: e.scalar_tensor_tensor(out=t1_[:, :], in0=nf[:, :], scalar=-2 * PI, in1=t1_[:, :],
                                                       op0=TT.mult, op1=TT.add), [nfb, t1b], [t1b])
            ew("dve", lambda e: e.tensor_scalar(out=t0_[:, :], in0=t1_[:, :], scalar1=PI, scalar2=-2 * PI,
                                                op0=TT.is_gt, op1=TT.mult), [t1b, t0b], [t0b])
            ew("dve", lambda e: e.tensor_tensor(out=t1_[:, :], in0=t1_[:, :], in1=t0_[:, :], op=TT.add), [t1b, t0b], [t1b])
            ew("act", lambda e: e.activation(out=dst[:, :], in_=t1_[:, :], func=AF.Sin), [t1b], [dstb])

        sin_of(sn, snb, 0.0)
        sin_of(cs_, csb, 0.5 * PI)
        ew("dve", lambda e: e.tensor_tensor(out=ar[:, :], in0=mag[:, :], in1=cs_[:, :], op=TT.mult), [magb, csb], [arb])
        ew("dve", lambda e: e.tensor_tensor(out=ai[:, :], in0=mag[:, :], in1=sn[:, :], op=TT.mult), [magb, snb], [aib])
        ew("dve", lambda e: e.tensor_tensor(out=t0_[:, :], in0=are[:, :], in1=are[:, :], op=TT.mult), [areb, t0b], [t0b])
        ew("dve", lambda e: e.tensor_tensor(out=t1_[:, :], in0=aim[:, :], in1=aim[:, :], op=TT.mult), [aimb, t1b], [t1b])
        ew("dve", lambda e: e.tensor_tensor(out=t0_[:, :], in0=t0_[:, :], in1=t1_[:, :], op=TT.add), [t0b, t1b], [t0b])
        ew("dve", lambda e: e.reciprocal(out=t0_[:, :], in_=t0_[:, :]), [t0b], [t0b])
        ew("dve", lambda e: e.tensor_scalar(out=t1_[:, :], in0=ar[:, :], scalar1=-1.0, scalar2=None, op0=TT.add), [arb, t1b], [t1b])
        ew("dve", lambda e: e.tensor_tensor(out=cr[:, :], in0=t1_[:, :], in1=are[:, :], op=TT.mult), [t1b, areb], [crb])
        ew("dve", lambda e: e.tensor_tensor(out=zr[:, :], in0=ai[:, :], in1=aim[:, :], op=TT.mult), [aib, aimb, zrb], [zrb])
        ew("dve", lambda e: e.tensor_tensor(out=cr[:, :], in0=cr[:, :], in1=zr[:, :], op=TT.add), [crb, zrb], [crb])
        ew("dve", lambda e: e.tensor_tensor(out=cr[:, :], in0=cr[:, :], in1=t0_[:, :], op=TT.mult), [crb, t0b], [crb])
        ew("dve", lambda e: e.tensor_tensor(out=ci[:, :], in0=ai[:, :], in1=are[:, :], op=TT.mult), [aib, areb], [cib])
        ew("dve", lambda e: e.tensor_tensor(out=zr[:, :], in0=t1_[:, :], in1=aim[:, :], op=TT.mult), [t1b, aimb, zrb], [zrb])
        ew("dve", lambda e: e.tensor_tensor(out=ci[:, :], in0=ci[:, :], in1=zr[:, :], op=TT.subtract), [cib, zrb], [cib])
        ew("dve", lambda e: e.tensor_tensor(out=ci[:, :], in0=ci[:, :], in1=t0_[:, :], op=TT.mult), [cib, t0b], [cib])
        crB = cr[:, :].unsqueeze(2).to_broadcast([128, NPAIR, 16])
        ciB = ci[:, :].unsqueeze(2).to_broadcast([128, NPAIR, 16])
        ew("dve", lambda e: e.tensor_tensor(out=bbr[:, :, :], in0=bre[:, :, :], in1=crB, op=TT.mult), [breb, crb], [bbrb])
        ew("dve", lambda e: e.tensor_tensor(out=tmpb3[:, :, :], in0=bim[:, :, :], in1=ciB, op=TT.mult), [bimb, cib], [tmpb3b])
        ew("dve", lambda e: e.tensor_tensor(out=bbr[:, :, :], in0=bbr[:, :, :], in1=tmpb3[:, :, :], op=TT.subtract), [bbrb, tmpb3b], [bbrb])
        ew("dve", lambda e: e.tensor_tensor(out=bbi[:, :, :], in0=bim[:, :, :], in1=crB, op=TT.mult), [bimb, crb], [bbib])
        ew("dve", lambda e: e.tensor_tensor(out=tmpb3[:, :, :], in0=bre[:, :, :], in1=ciB, op=TT.mult), [breb, cib, bbrb], [tmpb3b])
        ew("dve", lambda e: e.tensor_tensor(out=bbi[:, :, :], in0=bbi[:, :, :], in1=tmpb3[:, :, :], op=TT.add), [bbib, tmpb3b], [bbib])
        ew("dve", lambda e: e.tensor_copy(out=pw[:, 0, 0, :], in_=ar[:, :]), [arb], [pwb[0]])
        ew("dve", lambda e: e.tensor_copy(out=pw[:, 0, 1, :], in_=ai[:, :]), [aib], [pwb[0]])
        for k in range(NSTEP):
            if k > 0:
                ew("dve", lambda e, k=k: e.tensor_tensor(out=t0_[:, :], in0=pw[:, k - 1, 0, :], in1=pw[:, k - 1, 0, :], op=TT.mult),
                   [pwb[k - 1], t0b], [t0b])
                ew("dve", lambda e, k=k: e.tensor_tensor(out=t1_[:, :], in0=pw[:, k - 1, 1, :], in1=pw[:, k - 1, 1, :], op=TT.mult),
                   [pwb[k - 1], t1b], [t1b])
                ew("dve", lambda e, k=k: e.tensor_tensor(out=pw[:, k, 0, :], in0=t0_[:, :], in1=t1_[:, :], op=TT.subtract),
                   [t0b, t1b], [pwb[k]])
                ew("dve", lambda e, k=k: e.scalar_tensor_tensor(out=pw[:, k, 1, :], in0=pw[:, k - 1, 0, :], scalar=2.0,
                                                               in1=pw[:, k - 1, 1, :], op0=TT.mult, op1=TT.mult),
                   [pwb[k - 1]], [pwb[k]])
            ew("dve", lambda e, k=k: e.tensor_scalar(out=pw[:, k, 2, :], in0=pw[:, k, 1, :], scalar1=-1.0, scalar2=None, op0=TT.mult),
               [pwb[k]], [pwb[k]])
        for pr in range(NPAIR):
            off = (pr % 4) * 32
            for ri, src in enumerate((bbr, bbi)):
                pd, pdb = pad[:, ri, :], padb[ri]
                s.op("pool", lambda e, pd=pd: e.memset(pd, 0.0), writes=[pdb])
                for gi in range(2):
                    s.op("pool", lambda e, pd=pd, gi=gi, src=src, pr=pr, off=off: e.tensor_copy(
                        out=pd[gi * 64:(gi + 1) * 64, off + gi * 16:off + gi * 16 + 16], in_=src[gi * 64:(gi + 1) * 64, pr, :]),
                        reads=[bbrb, bbib], writes=[pdb])
                pt, pb = psrot.next()
                s.op("pe", lambda e, pt=pt, pd=pd: e.transpose(pt[:, 0:128], pd, ident[:, :]), reads=[pdb, KB], writes=[pb])
                s.op("act", lambda e, pt=pt, pr=pr, ri=ri: e.activation(out=lB[:, pr, ri, :], in_=pt[:, 0:128], func=AF.Copy),
                     reads=[pb], writes=[lBb[pr]])
            s.op("pool", lambda e, pr=pr: e.memset(lC[:, pr, :, :], 0.0), writes=[lCb[pr]])
            for gi in range(2):
                s.op("pool", lambda e, gi=gi, pr=pr, off=off: e.tensor_copy(
                    out=lC[gi * 64:(gi + 1) * 64, pr, 0, off + gi * 16:off + gi * 16 + 16], in_=cre[gi * 64:(gi + 1) * 64, pr, :]),
                    reads=[creb], writes=[lCb[pr]])
                s.op("pool", lambda e, gi=gi, pr=pr, off=off: e.tensor_scalar(
                    out=lC[gi * 64:(gi + 1) * 64, pr, 1, off + gi * 16:off + gi * 16 + 16], in0=cim[gi * 64:(gi + 1) * 64, pr, :],
                    scalar1=-1.0, scalar2=None, op0=TT.mult), reads=[cimb], writes=[lCb[pr]])

        NT = SEQ // 512
        for ct in range(2):
            for q4 in range(4):
                pr = ct * 4 + q4
                for tt in range(NT):
                    ts_ = slice(tt * 512, (tt + 1) * 512)
                    for ri, (dst, dstb) in enumerate(((R[0], Rb[0]), (I[0], Ib[0]))):
                        pt, pb = psrot.next()
                        mm(s, pt[:, :], lB[:, pr, ri, :], u[:, ct, ts_], True, True, reads=[lBb[pr], ub[ct]], writes=[pb])
                        s.op("act", lambda e, pt=pt, dst=dst, ts_=ts_: e.activation(out=dst[:, ts_], in_=pt[:, :], func=AF.Copy),
                             reads=[pb], writes=[dstb])
                cur = 0
                for k in range(NSTEP):
                    d = 1 << k
                    nx = 1 - cur
                    R0, I0, R1, I1 = R[cur], I[cur], R[nx], I[nx]
                    pre_, pim_, pnim_ = pw[:, k, 0, pr:pr + 1], pw[:, k, 1, pr:pr + 1], pw[:, k, 2, pr:pr + 1]
                    s.op("act", lambda e, R0=R0, R1=R1, d=d: e.activation(out=R1[:, 0:d], in_=R0[:, 0:d], func=AF.Copy),
                         reads=[Rb[cur]], writes=[Rb[nx]])
                    s.op("act", lambda e, I0=I0, I1=I1, d=d: e.activation(out=I1[:, 0:d], in_=I0[:, 0:d], func=AF.Copy),
                         reads=[Ib[cur]], writes=[Ib[nx]])
                    s.op("dve", lambda e, R0=R0, R1=R1, d=d, sc=pre_: e.scalar_tensor_tensor(
                        out=R1[:, d:SEQ], in0=R0[:, 0:SEQ - d], scalar=sc, in1=R0[:, d:SEQ], op0=TT.mult, op1=TT.add),
                        reads=[Rb[cur], pwb[k]], writes=[Rb[nx]])
                    s.op("dve", lambda e, I0=I0, R1=R1, d=d, sc=pnim_: e.scalar_tensor_tensor(
                        out=R1[:, d:SEQ], in0=I0[:, 0:SEQ - d], scalar=sc, in1=R1[:, d:SEQ], op0=TT.mult, op1=TT.add),
                        reads=[Ib[cur], Rb[nx], pwb[k]], writes=[Rb[nx]])
                    s.op("dve", lambda e, I0=I0, I1=I1, d=d, sc=pre_: e.scalar_tensor_tensor(
                        out=I1[:, d:SEQ], in0=I0[:, 0:SEQ - d], scalar=sc, in1=I0[:, d:SEQ], op0=TT.mult, op1=TT.add),
                        reads=[Ib[cur], pwb[k]], writes=[Ib[nx]])
                    s.op("dve", lambda e, R0=R0, I1=I1, d=d, sc=pim_: e.scalar_tensor_tensor(
                        out=I1[:, d:SEQ], in0=R0[:, 0:SEQ - d], scalar=sc, in1=I1[:, d:SEQ], op0=TT.mult, op1=TT.add),
                        reads=[Rb[cur], Ib[nx], pwb[k]], writes=[Ib[nx]])
                    cur = nx
                for tt in range(NT):
                    ts_ = slice(tt * 512, (tt + 1) * 512)
                    pt, pb = psrot.next()
                    mm(s, pt[:, :], lC[:, pr, 0, :], R[cur][:, ts_], True, False, reads=[lCb[pr], Rb[cur]], writes=[pb])
                    mm(s, pt[:, :], lC[:, pr, 1, :], I[cur][:, ts_], False, True, reads=[lCb[pr], Ib[cur]], writes=[pb])
                    if q4 == 0:
                        s.op("act", lambda e, pt=pt, ts_=ts_: e.activation(out=yacc[:, ts_], in_=pt[:, :], func=AF.Copy),
                             reads=[pb], writes=[yab])
                    else:
                        s.op("dve", lambda e, pt=pt, ts_=ts_: e.tensor_tensor(out=yacc[:, ts_], in0=yacc[:, ts_], in1=pt[:, :], op=TT.add),
                             reads=[pb, yab], writes=[yab])
            s.op("dve", lambda e, ct=ct: e.scalar_tensor_tensor(out=yacc[:, :], in0=u[:, ct, :], scalar=dd[:, ct:ct + 1], in1=yacc[:, :],
                                                              op0=TT.mult, op1=TT.add), reads=[ub[ct], ddb, yab], writes=[yab])
            s.op("act", lambda e: e.activation(out=yo[:, :], in_=yacc[:, :], func=AF.Gelu), reads=[yab], writes=[yob])
            s.dma("sp", S2[256 + ct * 128:256 + (ct + 1) * 128, :], yo[:, :], reads=[yob], writes=[S2b[2 + ct].new()])


def phase_C(c, K, P, l, xT, xb, H2, H2b, out_d, final):
    s = c.s
    with c.phase(), dedupe_scope("C"):
        R1 = c.sb([128, KC, NTH], BF16); r1b = bufs(KC, "r1")
        R2 = c.sb([128, KC, NTH], BF16); r2b = bufs(KC, "r2")
        mT, mb = R1, r1b
        sT, sTb = R2, r2b
        hT, hb = R2, r2b
        sq, sqb = R1, r1b
        NA = NTH - 2
        act = R1[:, :, :].rearrange("p a b -> p (a b)")[:, 0:NFP * NA].rearrange("p (j t) -> p j t", t=NA)
        actb = bufs(NFP, "act")
        gcol = c.sb([128, KC], F32); gb = Buf("g")
        gfcol = c.sb([128, KC], F32); gfb = Buf("gf")
        bglu = c.sb([128, 8], F32); bglub = Buf("bglu")
        cw = c.sb([128, FC_TOT, 3], F32); cwb = Buf("cw")
        cb = c.sb([128, FC_TOT], F32); cbb = Buf("cb")
        rstd = c.sb([128, NTH], F32); rstdb = Buf("rstd")
        psrot = Rot([(K.banks[i], Buf(f"psC{i}")) for i in range(7)])
        psh = K.banks[7]
        hrot = Rot([(psh[:, 64 * i:64 * i + 64], Buf(f"psh{i}")) for i in range(8)])
        wrot = Rot([(c.sb([128, KC, 128], BF16), Buf(f"w{i}")) for i in range(4)])
        wdrot = Rot([(c.sb([128, NFP, 128], BF16), Buf(f"wd{i}")) for i in range(3)])
        gprot = Rot([(c.sb([128, NTH], F32), Buf(f"gp{i}")) for i in range(2)])
        t1rot = Rot([(c.sb([128, NA], F32), Buf(f"t1{i}")) for i in range(2)])
        t2rot = Rot([(c.sb([128, NA], F32), Buf(f"t2{i}")) for i in range(2)])

        def pstile(tn):
            return hrot.next() if tn <= 64 else psrot.next()

        s.dma("sp", gcol[:, :], P["gffn"][l], writes=[gb])
        s.dma("sp", gfcol[:, :], P["gf"], writes=[gfb])
        s.dma("sp", bglu[:, :], P["bglu"][l], writes=[bglub])
        s.dma("sp", cb[:, :], P["f_cb"][l], writes=[cbb])
        s.dma("sp", cw[:, :, :], P["f_cw"][l], writes=[cwb])
        def pos(m):
            return (m % 2) * 4 + m // 2 if m < 8 else (2 + m % 2) * 4 + (m - 8) // 2
        G2v = H2.rearrange("(q p) t -> p q t", p=128)
        s.op("act", lambda e: e.dma_start(out=mT[:, :, HALO:NTH], in_=G2v[:, :, bass.ds(K.jva * NTOK, NTOK)]),
             reads=[H2b], writes=list(mb), kind=1)
        s.op("act", lambda e: e.dma_start(out=mT[:, :, 0:HALO], in_=G2v[:, :, bass.ds(((K.jva + 3) % 4) * NTOK + (NTOK - HALO), HALO)]),
             reads=[H2b], writes=list(mb), kind=1)
        mq = c.sb([128, 1], F32); mqb = Buf("mq")
        s.dma("sp", mq[:, :], P["maskq"], writes=[mqb])
        s.op("dve", lambda e: e.tensor_scalar(out=mT[:, :, 0:HALO], in0=mT[:, :, 0:HALO], scalar1=mq[:, 0:1], scalar2=None,
                                              op0=ALU.mult), reads=list(mb) + [mqb], writes=list(mb))
        T3 = [(0, HALO), (HALO, 512), (HALO + 512, 512)]
        TF = [(2, HALO - 2), (HALO, 512), (HALO + 512, 512)]

        wglu = P["w_glu"][l]
        for fc in range(8):
            wt, wb = load_w(s, wrot, wglu, 8, fc * 128, 128)
            for (t0, tn) in T3:
                pt, pb = pstile(tn)
                for kc in range(8):
                    mm(s, pt[:, 0:tn], wt[:, kc, :], mT[:, pos(8 + kc), t0:t0 + tn], kc == 0, kc == 7,
                       reads=[wb, mb[pos(8 + kc)]], writes=[pb])
                t1, t1b = t1rot.next()
                s.op("act", lambda e, pt=pt, t1=t1, tn=tn, fc=fc: e.activation(
                    out=t1[:, 0:tn], in_=pt[:, 0:tn], func=AF.Sigmoid, bias=bglu[:, fc:fc + 1]),
                    reads=[pb, bglub], writes=[t1b])
                s.op("dve", lambda e, t1=t1, t0=t0, tn=tn, fc=fc: e.tensor_tensor(
                    out=sT[:, fc, t0:t0 + tn], in0=t1[:, 0:tn], in1=mT[:, pos(8 + fc), t0:t0 + tn], op=ALU.mult),
                    reads=[t1b, mb[pos(8 + fc)]], writes=[sTb[fc]])
        wout = P["w_out"][l]
        for dc in range(KC):
            wt, wb = load_w(s, wrot, wout, KC, dc * 128, 128)
            for (t0, tn) in T3:
                pt, pb = pstile(tn)
                for kc in range(KC):
                    rhs = mT[:, pos(kc), t0:t0 + tn] if kc < 8 else sT[:, kc - 8, t0:t0 + tn]
                    rb = mb[pos(kc)] if kc < 8 else sTb[kc - 8]
                    mm(s, pt[:, 0:tn], wt[:, kc, :], rhs, kc == 0, kc == KC - 1, reads=[wb, rb], writes=[pb])
                s.op("dve", lambda e, pt=pt, dc=dc, t0=t0, tn=tn: e.tensor_tensor(
                    out=xT[:, dc, t0:t0 + tn], in0=xT[:, dc, t0:t0 + tn], in1=pt[:, 0:tn], op=ALU.add),
                    reads=[pb, xb[dc]], writes=[xb[dc]])
        rmsnorm_T(c, K, xT, xb, 0, NTH, T3, gcol, gb, hT, hb, sq, sqb, pstile, rstd, rstdb)
        s.op("dve", lambda e: e.memset(K.tiny2[:, :], 0.0), writes=list(r1b) + list(actb))
        wg_d, wv_d, wd_d = P["w_gate"][l], P["w_val"][l], P["w_down"][l]
        fc_lists = [list(range(p * NFP, min(FC_TOT, (p + 1) * NFP))) for p in range(NPASS)]
        for fcs in fc_lists:
            for j, fc in enumerate(fcs):
                wg, wgb = load_w(s, wrot, wg_d, KC, fc * 128, 128)
                wv, wvb = load_w(s, wrot, wv_d, KC, fc * 128, 128)
                gp, gpb = gprot.next()
                for (t0, tn) in T3:
                    pt, pb = pstile(tn)
                    for kc in range(KC):
                        mm(s, pt[:, 0:tn], wg[:, kc, :], hT[:, kc, t0:t0 + tn], kc == 0, kc == KC - 1,
                           reads=[wgb, hb[kc]], writes=[pb])
                    s.op("act", lambda e, pt=pt, gp=gp, t0=t0, tn=tn: e.activation(
                        out=gp[:, t0:t0 + tn], in_=pt[:, 0:tn], func=AF.Copy), reads=[pb], writes=[gpb])
                t1, t1b = t1rot.next()
                s.op("dve", lambda e, gp=gp, t1=t1, fc=fc: e.tensor_scalar(
                    out=t1[:, :], in0=gp[:, 2:NTH], scalar1=cw[:, fc, 2:3], scalar2=cb[:, fc:fc + 1],
                    op0=ALU.mult, op1=ALU.add), reads=[gpb, cwb, cbb], writes=[t1b])
                s.op("dve", lambda e, gp=gp, t1=t1, fc=fc: e.scalar_tensor_tensor(
                    out=t1[:, :], in0=gp[:, 1:NTH - 1], scalar=cw[:, fc, 1:2], in1=t1[:, :],
                    op0=ALU.mult, op1=ALU.add), reads=[gpb, cwb, t1b], writes=[t1b])
                s.op("dve", lambda e, gp=gp, t1=t1, fc=fc: e.scalar_tensor_tensor(
                    out=t1[:, :], in0=gp[:, 0:NTH - 2], scalar=cw[:, fc, 0:1], in1=t1[:, :],
                    op0=ALU.mult, op1=ALU.add), reads=[gpb, cwb, t1b], writes=[t1b])
                t2, t2b = t2rot.next()
                s.op("act", lambda e, t1=t1, t2=t2: e.activation(out=t2[:, :], in_=t1[:, :], func=AF.Gelu),
                     reads=[t1b], writes=[t2b])
                for (t0, tn) in TF:
                    pt, pb = pstile(tn)
                    for kc in range(KC):
                        mm(s, pt[:, 0:tn], wv[:, kc, :], hT[:, kc, t0:t0 + tn], kc == 0, kc == KC - 1,
                           reads=[wvb, hb[kc]], writes=[pb])
                    s.op("dve", lambda e, pt=pt, t2=t2, j=j, t0=t0, tn=tn: e.tensor_tensor(
                        out=act[:, j, t0 - 2:t0 - 2 + tn], in0=t2[:, t0 - 2:t0 - 2 + tn], in1=pt[:, 0:tn],
                        op=ALU.mult), reads=[pb, t2b], writes=[actb[j]])
            nfc = len(fcs)
            for dc in range(KC):
                wdt, wdb = wdrot.next()
                src = wd_d[fcs[0] * 128:(fcs[0] + nfc) * 128, dc * 128:(dc + 1) * 128].rearrange("(j p) f -> p j f", p=128)
                s.dma("pool", wdt[:, 0:nfc, :], src, writes=[wdb])
                for (t0, tn) in TF:
                    pt, pb = pstile(tn)
                    for j in range(nfc):
                        mm(s, pt[:, 0:tn], wdt[:, j, :], act[:, j, t0 - 2:t0 - 2 + tn], j == 0, j == nfc - 1,
                           reads=[wdb, actb[j]], writes=[pb])
                    s.op("dve", lambda e, pt=pt, dc=dc, t0=t0, tn=tn: e.tensor_tensor(
                        out=xT[:, dc, t0:t0 + tn], in0=xT[:, dc, t0:t0 + tn], in1=pt[:, 0:tn], op=ALU.add),
                        reads=[pb, xb[dc]], writes=[xb[dc]])
        if final:
            s.op("dve", lambda e: e.memset(K.tiny2[:, :], 0.0), writes=list(actb) + list(r1b))
            T2 = [(HALO, 512), (HALO + 512, 512)]
            for kc in range(KC):
                s.op("act", lambda e, kc=kc: e.activation(out=sq[:, kc, HALO:NTH], in_=xT[:, kc, HALO:NTH], func=AF.Square),
                     reads=[xb[kc]], writes=[sqb[kc]])
            for (t0, tn) in T2:
                pt, pb = psrot.next()
                for kc in range(KC):
                    mm(s, pt[:, 0:tn], K.ones[:, :], sq[:, kc, t0:t0 + tn], kc == 0, kc == KC - 1,
                       reads=[sqb[kc], K.b], writes=[pb])
                s.op("act", lambda e, pt=pt, t0=t0, tn=tn: e.activation(
                    out=rstd[:, t0:t0 + tn], in_=pt[:, 0:tn], func=AF.Sqrt, bias=K.eps[:, 0:1]),
                    reads=[pb, K.b], writes=[rstdb])
                s.op("dve", lambda e, t0=t0, tn=tn: e.reciprocal(out=rstd[:, t0:t0 + tn], in_=rstd[:, t0:t0 + tn]),
                     reads=[rstdb], writes=[rstdb])
            for kc in range(KC):
                s.op("dve", lambda e, kc=kc: e.scalar_tensor_tensor(
                    out=xT[:, kc, HALO:NTH], in0=xT[:, kc, HALO:NTH], scalar=gfcol[:, kc:kc + 1], in1=rstd[:, HALO:NTH],
                    op0=ALU.mult, op1=ALU.mult), reads=[xb[kc], rstdb, gfb], writes=[xb[kc]])
            ov = out_d.rearrange("(kc p) t -> p kc t", p=128)
            for kc in range(KC):
                s.dma("sp", ov[:, kc, :], xT[:, kc, HALO:NTH], reads=[xb[kc]])


PARAM_SPECS = [
    ("xT0", [D_MODEL, NTH]), ("maskq", [128, 1]), ("gmix", [DEPTH, 128, KC]), ("gffn", [DEPTH, 128, KC]), ("gf", [128, KC]),
    ("bglu", [DEPTH, 128, 8]), ("f_cw", [DEPTH, 128, FC_TOT, 3]), ("f_cb", [DEPTH, 128, FC_TOT]),
    ("w_in", [DEPTH, D_MODEL, IN_COLS]), ("w_glu", [DEPTH, 1024, 1024]), ("w_out", [DEPTH, D_MODEL, D_MODEL]),
    ("w_gate", [DEPTH, D_MODEL, D_FF]), ("w_val", [DEPTH, D_MODEL, D_FF]), ("w_down", [DEPTH, D_FF, D_MODEL]),
    ("m_cw", [DEPTH, 128, 2, 4]), ("m_cb", [DEPTH, 128, 2]), ("m_wq", [DEPTH, DV, DK]), ("m_wk", [DEPTH, DV, DK]),
    ("m_ib", [DEPTH, 128, 1]), ("m_fb", [DEPTH, 128, 1]), ("m_hg", [DEPTH, 128, 2]), ("m_sk", [DEPTH, 128, 2]),
    ("s_are", [DEPTH, 128, NPAIR]), ("s_aim", [DEPTH, 128, NPAIR]), ("s_ldt", [DEPTH, 128, NPAIR]),
    ("s_bre", [DEPTH, 128, NPAIR, 16]), ("s_bim", [DEPTH, 128, NPAIR, 16]),
    ("s_cre", [DEPTH, 128, NPAIR, 16]), ("s_cim", [DEPTH, 128, NPAIR, 16]), ("s_dd", [DEPTH, 128, 2]),
]


def build_fused(depth=DEPTH, stop_after=None):
    c = Ctx()
    s = c.s
    P = {name: c.dram(name, ([depth] + shape[1:]) if (len(shape) > 2 and shape[0] == DEPTH) else shape, F32) for (name, shape) in PARAM_SPECS}
    out_d = c.dram("outT", [D_MODEL, NTOK], F32, "ExternalOutput")
    K = Consts()
    K.jv = c.nc.sync.partition_id() % 4
    K.jva = c.nc.scalar.partition_id() % 4
    K.b = Buf("consts")
    K.banks = [c.ps([128, 512]) for _ in range(8)]
    c.tiny = c.sb([128, 1], F32)
    K.tiny2 = c.sb([128, 1], F32)
    K.ones = c.sb([128, 128], BF16)
    K.eps = c.sb([128, 1], F32)
    K.one1 = c.sb([128, 1], F32)
    K.onesf = c.sb([128, 128], F32)
    K.dmat = c.sb([128, 128], F32)
    K.tri = c.sb([128, 128], F32)
    K.negm = c.sb([128, 128], F32)
    K.ident = c.sb([128, 128], F32)
    xT = c.sb([128, KC, NTH], F32)
    xb = bufs(KC, "x")
    dmb = Buf()
    s.op("pool", lambda e: e.iota(K.dmat[:, :], [[1, 128]], base=0, channel_multiplier=-1,
                                  allow_small_or_imprecise_dtypes=True), writes=[dmb])
    s.op("dve", lambda e: e.tensor_single_scalar(out=K.tri[:, :], in_=K.dmat[:, :], scalar=0.0, op=ALU.is_ge),
         reads=[dmb], writes=[K.b])
    s.op("dve", lambda e: e.tensor_single_scalar(out=K.ident[:, :], in_=K.dmat[:, :], scalar=0.0, op=ALU.is_equal),
         reads=[dmb], writes=[K.b])
    s.op("dve", lambda e: e.tensor_scalar(out=K.negm[:, :], in0=K.tri[:, :], scalar1=-1.0, scalar2=30000.0,
                                          op0=ALU.add, op1=ALU.mult), reads=[K.b], writes=[K.b])
    s.op("dve", lambda e: e.memset(K.onesf[:, :], 1.0), writes=[K.b])
    s.op("dve", lambda e: e.memset(K.one1[:, :], 1.0), writes=[K.b])
    s.op("dve", lambda e: e.memset(K.eps[:, :], EPS), writes=[K.b])
    s.op("dve", lambda e: e.memset(K.ones[:, :], 1.0 / D_MODEL), writes=[K.b])
    xv = P["xT0"].rearrange("(kc p) t -> p kc t", p=128)
    for kc in range(KC):
        s.dma("sp", xT[:, kc, :], xv[:, kc, :], writes=[xb[kc]])
    for l in range(depth):
        S1 = c.dram(f"S1_{l}", [4096, NTOK], BF16, "Internal"); S1b = [BufSet() for _ in range(8)]
        G1 = c.dram(f"G1_{l}", [8 * 2048, NTOK], BF16, "Internal"); G1b = bufs(8, "G1")
        S1g = c.dram(f"S1g_{l}", [8, NTOK], F32, "Internal"); S1gb = BufSet()
        G1g = c.dram(f"G1g_{l}", [32, NTOK], F32, "Internal"); G1gb = Buf()
        S2 = c.dram(f"S2_{l}", [512, SEQ], BF16, "Internal"); S2b = [BufSet() for _ in range(4)]
        G2 = c.dram(f"G2_{l}", [2048, SEQ], BF16, "Internal"); G2b = Buf()
        H1 = c.dram(f"H1_{l}", [4, 1024, NTOK], BF16, "Internal"); H1b = Buf()
        Hg = c.dram(f"Hg_{l}", [4, 2, NTOK], F32, "Internal"); Hgb = Buf()
        phase_A(c, K, P, l, xT, xb, S1, S1b, S1g, S1gb)
        for k in range(8):
            allgather(c, S1[k * 512:(k + 1) * 512, :], S1b[k], G1[k * 2048:(k + 1) * 2048, :], G1b[k])
        allgather(c, S1g[:, :], S1gb, G1g[:, :], G1gb)
        with c.phase():
            stg = c.sb([128, 32, NTOK], BF16); stgb = Buf("stg")
            stg2 = c.sb([8, NTOK], F32); stg2b = Buf("stg2")
            s.op("sp", lambda e, G1=G1: e.dma_start(
                out=stg[:, :, :], in_=G1[bass.ds(K.jv * 4096, 4096), :].rearrange("(q p) t -> p q t", p=128)),
                reads=list(G1b), writes=[stgb], kind=1)
            for kk in range(2):
                for r in range(4):
                    s.dma("sp", H1[r, kk * 512:(kk + 1) * 512, :].rearrange("(i8 p) t -> p i8 t", p=128),
                          stg[:, kk * 16 + r * 4:kk * 16 + r * 4 + 4, :], reads=[stgb], writes=[H1b])
            G1gv = G1g.rearrange("(sga h) t -> sga h t", h=4)
            s.op("act", lambda e, G1gv=G1gv: e.dma_start(
                out=stg2[:, :], in_=G1gv[:, bass.ds(K.jva, 1), :].rearrange("sga o t -> sga (o t)")),
                reads=[G1gb], writes=[stg2b], kind=1)
            s.dma("sp", Hg.rearrange("sg a t -> (sg a) t"), stg2[:, :], reads=[stg2b], writes=[Hgb])
        phase_B1(c, K, P, l, H1, H1b, Hg, Hgb, S2, S2b)
        phase_B2(c, K, P, l, H1, H1b, S2, S2b)
        G2bs = bufs(4, "G2")
        for k in range(4):
            allgather(c, S2[k * 128:(k + 1) * 128, :], S2b[k], G2[k * 512:(k + 1) * 512, :], G2bs[k])
        s.op("dve", lambda e: e.memset(K.tiny2[:, :], 0.0), reads=G2bs, writes=[G2b])
        phase_C(c, K, P, l, xT, xb, G2, G2b, out_d, final=(l == depth - 1))
    s.emit()
    s.finalize()
    c.st.close()
    return c.nc


def _col(v, n):
    return np.ascontiguousarray(np.asarray(v, np.float32).reshape(n, 128).T)


def _lane(a):
    return np.ascontiguousarray(np.asarray(a, np.float32).reshape(8, 2, 64).transpose(1, 2, 0).reshape(128, 8))


def _lane3(a):
    return np.ascontiguousarray(np.asarray(a, np.float32).reshape(8, 2, 64, 16).transpose(1, 2, 0, 3).reshape(128, 8, 16))


def make_in_maps(P, L=DEPTH):
    f = lambda a: np.ascontiguousarray(np.asarray(a, np.float32))
    common = {
        "gmix": np.stack([_col(P["norm_mix_g"][l], KC) for l in range(L)]),
        "gffn": np.stack([_col(P["norm_ffn_g"][l], KC) for l in range(L)]),
        "gf": _col(P["norm_final_g"], KC),
        "bglu": np.stack([_col(P["s5_b_glu"][l], 8) for l in range(L)]),
        "f_cw": np.stack([np.ascontiguousarray(f(P["f_conv_w"][l]).reshape(3, FC_TOT, 128).transpose(2, 1, 0)) for l in range(L)]),
        "f_cb": np.stack([_col(P["f_conv_b"][l], FC_TOT) for l in range(L)]),
        "w_in": f(P["w_in"][:L]), "w_glu": f(P["s5_w_glu"][:L]), "w_out": f(P["w_out"][:L]),
        "w_gate": f(P["w_gate"][:L]), "w_val": f(P["w_val"][:L]), "w_down": f(P["w_down"][:L]),
    }
    x = f(P["x"])
    maps = []
    for cid in range(NCORES):
        b, q = divmod(cid, 4)
        m = dict(common)
        xh = np.zeros((NTH, D_MODEL), np.float32)
        lo = max(q * NTOK - HALO, 0)
        xh[HALO - (q * NTOK - lo):] = x[b, lo:(q + 1) * NTOK]
        m["xT0"] = np.ascontiguousarray(xh.T)
        m["maskq"] = np.full((128, 1), 0.0 if q == 0 else 1.0, np.float32)
        sl = slice(q * 256, (q + 1) * 256)
        g0 = q * 16
        m["m_cw"] = np.stack([np.ascontiguousarray(f(P["m_conv_w"][l])[:, sl].reshape(4, 2, 128).transpose(2, 1, 0)) for l in range(L)])
        m["m_cb"] = np.stack([_col(P["m_conv_b"][l][sl], 2) for l in range(L)])
        m["m_wq"] = np.stack([f(P["w_q"][l][q]) for l in range(L)])
        m["m_wk"] = np.stack([f(P["w_k"][l][q]) for l in range(L)])
        m["m_ib"] = np.stack([np.full((128, 1), P["m_i_bias"][l][q], np.float32) for l in range(L)])
        m["m_fb"] = np.stack([np.full((128, 1), P["m_f_bias"][l][q], np.float32) for l in range(L)])
        m["m_hg"] = np.stack([_col(P["m_head_g"][l][sl], 2) for l in range(L)])
        m["m_sk"] = np.stack([_col(P["m_skip"][l][sl], 2) for l in range(L)])
        m["s_are"] = np.stack([_lane(P["s5_a_re"][l][g0:g0 + 16]) for l in range(L)])
        m["s_aim"] = np.stack([_lane(P["s5_a_im"][l][g0:g0 + 16]) for l in range(L)])
        m["s_ldt"] = np.stack([_lane(np.repeat(np.asarray(P["s5_log_dt"][l][g0:g0 + 16])[:, None], 64, 1)) for l in range(L)])
        m["s_bre"] = np.stack([_lane3(P["s5_b_re"][l][g0:g0 + 16]) for l in range(L)])
        m["s_bim"] = np.stack([_lane3(P["s5_b_im"][l][g0:g0 + 16]) for l in range(L)])
        m["s_cre"] = np.stack([_lane3(np.asarray(P["s5_c_re"][l][g0:g0 + 16]).transpose(0, 2, 1)) for l in range(L)])
        m["s_cim"] = np.stack([_lane3(np.asarray(P["s5_c_im"][l][g0:g0 + 16]).transpose(0, 2, 1)) for l in range(L)])
        m["s_dd"] = np.stack([_col(P["s5_d"][l][sl], 2) for l in range(L)])
        maps.append(m)
    return maps


_NC = {}


def kernel(**inputs):
    P = {k: np.asarray(v) for k, v in inputs.items()}
    if "nc" not in _NC:
        _NC["nc"] = build_fused()
    maps = make_in_maps(P)
    res = run_bass_kernel_spmd(_NC["nc"], maps, core_ids=list(range(NCORES)))
    out = np.empty((BATCH, SEQ, D_MODEL), np.float32)
    for cid in range(NCORES):
        b, q = divmod(cid, 4)
        out[b, q * NTOK:(q + 1) * NTOK] = res.results[cid]["outT"].T
    return out
```
